# Optimizing a Trainium2 kernel written in Bass

```python
import jax, jax.numpy as jnp
from jax import lax
import numpy as np

D_MODEL = 2048
BATCH = 2
SEQ = 16384
DEPTH = 2
DEC_BATCH = 8
DEC_SEQ = 32
PAST_LEN = 1024

CHUNK = 64
N_MIXERS = 2
N_A = (DEPTH + 1) // 2
N_B = DEPTH // 2
D_MIXER = D_MODEL
CONV_W = 3
GMLP_CHUNK = 128
GMLP_GROUPS = 8
GMLP_GROUP_DIM = D_MIXER // GMLP_GROUPS
N_MEM = 256
XA_HEADS = 4
XA_HEAD_DIM = D_MODEL // 8
D_XQ = XA_HEADS * XA_HEAD_DIM
D_MIX = D_MIXER + D_XQ
D_FF = 5632
EPS = 1e-6

kernel_name = "hybrid_shortconv_gmlp_memxattn_convffn_step"


def rmsnorm(x, g):
    x32 = x.astype(jnp.float32)
    y = x32 * lax.rsqrt(jnp.mean(x32 * x32, axis=-1, keepdims=True) + EPS)
    return (y * g.astype(jnp.float32)).astype(x.dtype)


def layernorm(x, g, b):
    x32 = x.astype(jnp.float32)
    xc = x32 - jnp.mean(x32, axis=-1, keepdims=True)
    y = xc * lax.rsqrt(jnp.mean(xc * xc, axis=-1, keepdims=True) + EPS)
    return (y * g.astype(jnp.float32) + b.astype(jnp.float32)).astype(x.dtype)


def causal_dwconv(z, w, prev):
    T = z.shape[1]
    zp = jnp.concatenate([prev.astype(z.dtype), z], axis=1)
    y = w[0] * zp[:, :T]
    for k in range(1, CONV_W):
        y = y + w[k] * zp[:, k:k + T]
    return y, zp[:, -(CONV_W - 1):]


def short_conv_mixer(proj, conv_w, prev):
    b_gate, c_gate, h = jnp.split(proj, 3, axis=-1)
    y, buf = causal_dwconv(c_gate * h, conv_w, prev)
    return b_gate * y, buf


def chunk_mlp_mixer(proj, norm_g, norm_b, ws, bias):
    z = jax.nn.gelu(proj, approximate=False)
    u, v = jnp.split(z, 2, axis=-1)
    v = layernorm(v, norm_g, norm_b)
    B, T, _ = v.shape
    n_chunks = -(-T // GMLP_CHUNK)
    pad = n_chunks * GMLP_CHUNK - T
    vp = jnp.pad(v, ((0, 0), (0, pad), (0, 0))).reshape(
        B, n_chunks, GMLP_CHUNK, GMLP_GROUPS, GMLP_GROUP_DIM)
    mask = jnp.tril(jnp.ones((GMLP_CHUNK, GMLP_CHUNK), dtype=bool))
    wm = jnp.where(mask, ws, jnp.zeros((), ws.dtype)).astype(v.dtype)
    mixed = jnp.einsum('gts,bcsgd->bctgd', wm, vp) + jnp.swapaxes(bias, 0, 1)[:, :, None].astype(v.dtype)
    mixed = mixed.reshape(B, n_chunks * GMLP_CHUNK, D_MIXER)[:, :T]
    return u * mixed, v


def memory_kv(mem, g, wk, wv):
    B = mem.shape[0]
    mn = rmsnorm(mem, g)
    k = (mn @ wk).reshape(B, N_MEM, XA_HEADS, XA_HEAD_DIM)
    v = (mn @ wv).reshape(B, N_MEM, XA_HEADS, XA_HEAD_DIM)
    return k, v


def memory_attention(q, k, v):
    B, T, _ = q.shape
    q = q.reshape(B, T, XA_HEADS, XA_HEAD_DIM)
    s = jnp.einsum('bthd,bmhd->bhtm', q, k).astype(jnp.float32) * (XA_HEAD_DIM ** -0.5)
    p = jax.nn.softmax(s, axis=-1).astype(v.dtype)
    return jnp.einsum('bhtm,bmhd->bthd', p, v).reshape(B, T, D_XQ)


def trunk(x, mem_k, mem_v, conv_a_prev, ffn_prev, norm_mix_g, w_in_a, conv_a_w, w_in_b,
          gmlp_norm_g, gmlp_norm_b, gmlp_ws, gmlp_bias, w_out, norm_ffn_g, w_up,
          ffn_conv_w, ffn_conv_b, w_down, norm_final_g):
    conv_a_new, ffn_new, gmlp_v = [], [], []
    for i in range(DEPTH):
        kind, j = i % N_MIXERS, i // N_MIXERS
        xn = rmsnorm(x, norm_mix_g[i])
        if kind == 0:
            proj = xn @ w_in_a[j]
            y_mix, buf = short_conv_mixer(proj[..., :3 * D_MIXER], conv_a_w[j], conv_a_prev[j])
            conv_a_new.append(buf)
        else:
            proj = xn @ w_in_b[j]
            y_mix, v_rows = chunk_mlp_mixer(proj[..., :2 * D_MIXER], gmlp_norm_g[j],
                                            gmlp_norm_b[j], gmlp_ws[j], gmlp_bias[j])
            gmlp_v.append(v_rows)
        y_mem = memory_attention(proj[..., -D_XQ:], mem_k[i], mem_v[i])
        x = x + jnp.concatenate([y_mix, y_mem], axis=-1) @ w_out[i]
        xn = rmsnorm(x, norm_ffn_g[i])
        z, buf = causal_dwconv(xn @ w_up[i], ffn_conv_w[i], ffn_prev[i])
        a, g = jnp.split(z + ffn_conv_b[i], 2, axis=-1)
        x = x + (jax.nn.silu(g) * a) @ w_down[i]
        ffn_new.append(buf)
    return rmsnorm(x, norm_final_g), jnp.stack(conv_a_new), jnp.stack(ffn_new), jnp.stack(gmlp_v)


def setup_inputs(seed: int = 0) -> dict:
    key = jax.random.key(seed)
    ks = iter(jax.random.split(key, 32))
    f32 = jnp.float32

    def nrm(shape, scale=1.0):
        return jax.random.normal(next(ks), shape, f32) * scale

    def gain(shape):
        return 1.0 + 0.01 * jax.random.normal(next(ks), shape, f32)

    D = D_MODEL
    return {
        'x_prompt': nrm((BATCH, SEQ, D)),
        'x_sample': nrm((DEC_BATCH, DEC_SEQ, D)),
        'mem_prompt': nrm((BATCH, N_MEM, D)),
        'cache_conv_a': nrm((N_A, DEC_BATCH, CONV_W - 1, D_MIXER)),
        'cache_ffn_conv': nrm((DEPTH, DEC_BATCH, CONV_W - 1, 2 * D_FF)),
        'cache_mem_k': nrm((DEPTH, DEC_BATCH, N_MEM, XA_HEADS, XA_HEAD_DIM)),
        'cache_mem_v': nrm((DEPTH, DEC_BATCH, N_MEM, XA_HEADS, XA_HEAD_DIM)),
        'norm_mix_g': gain((DEPTH, D)),
        'norm_mem_g': gain((DEPTH, D)),
        'w_mem_k': nrm((DEPTH, D, D_XQ), D ** -0.5),
        'w_mem_v': nrm((DEPTH, D, D_XQ), D ** -0.5),
        'w_in_a': nrm((N_A, D, 3 * D_MIXER + D_XQ), D ** -0.5),
        'conv_a_w': nrm((N_A, CONV_W, D_MIXER), CONV_W ** -0.5),
        'w_in_b': nrm((N_B, D, 2 * D_MIXER + D_XQ), D ** -0.5),
        'gmlp_norm_g': gain((N_B, D_MIXER)),
        'gmlp_norm_b': nrm((N_B, D_MIXER), 0.01),
        'gmlp_ws': nrm((N_B, GMLP_GROUPS, GMLP_CHUNK, GMLP_CHUNK), GMLP_CHUNK ** -0.5),
        'gmlp_bias': gain((N_B, GMLP_GROUPS, GMLP_CHUNK)),
        'w_out': nrm((DEPTH, D_MIX, D), D_MIX ** -0.5),
        'norm_ffn_g': gain((DEPTH, D)),
        'w_up': nrm((DEPTH, D, 2 * D_FF), D ** -0.5),
        'ffn_conv_w': nrm((DEPTH, CONV_W, 2 * D_FF), CONV_W ** -0.5),
        'ffn_conv_b': nrm((DEPTH, 2 * D_FF), 0.01),
        'w_down': nrm((DEPTH, D_FF, D), D_FF ** -0.5),
        'norm_final_g': gain((D,)),
    }


def reference(x_prompt, x_sample, mem_prompt, cache_conv_a, cache_ffn_conv, cache_mem_k,
              cache_mem_v, norm_mix_g, norm_mem_g, w_mem_k, w_mem_v, w_in_a, conv_a_w,
              w_in_b, gmlp_norm_g, gmlp_norm_b, gmlp_ws, gmlp_bias, w_out, norm_ffn_g,
              w_up, ffn_conv_w, ffn_conv_b, w_down, norm_final_g):
    weights = (norm_mix_g, w_in_a, conv_a_w, w_in_b, gmlp_norm_g, gmlp_norm_b, gmlp_ws,
               gmlp_bias, w_out, norm_ffn_g, w_up, ffn_conv_w, ffn_conv_b, w_down, norm_final_g)

    B = x_prompt.shape[0]
    kv = [memory_kv(mem_prompt, norm_mem_g[i], w_mem_k[i], w_mem_v[i]) for i in range(DEPTH)]
    mem_k_prompt = jnp.stack([k for k, _ in kv])
    mem_v_prompt = jnp.stack([v for _, v in kv])
    conv_a_zero = jnp.zeros((N_A, B, CONV_W - 1, D_MIXER), x_prompt.dtype)
    ffn_zero = jnp.zeros((DEPTH, B, CONV_W - 1, 2 * D_FF), x_prompt.dtype)
    y_prompt, conv_a_prompt, ffn_conv_prompt, _ = trunk(
        x_prompt, mem_k_prompt, mem_v_prompt, conv_a_zero, ffn_zero, *weights)

    y_sample, conv_a_sample, ffn_conv_sample, gmlp_v_sample = trunk(
        x_sample, cache_mem_k, cache_mem_v, cache_conv_a, cache_ffn_conv, *weights)

    return (y_prompt, y_sample, conv_a_prompt, ffn_conv_prompt, mem_k_prompt, mem_v_prompt,
            conv_a_sample, ffn_conv_sample, gmlp_v_sample)
```

```python
import numpy as np
import concourse.bass as bass
import concourse.mybir as mybir
from concourse.bass_utils import run_bass_kernel_spmd
from contextlib import ExitStack

F32 = mybir.dt.float32
BF16 = mybir.dt.bfloat16
AF = mybir.ActivationFunctionType
ALU = mybir.AluOpType

P = 128
D = 2048
KD = 16
DXQ = 1024
NMEM = 256
DFF = 5632
NH = 44
NUP = 88
TMAX = 512
WARM = 256
TS = 32
EPS = 1e-6
NB_RING = 5


class Tracker:
    ENGS = ("pe", "act", "dve", "pool", "sp")

    def __init__(self, nc):
        self.nc = nc
        self.q = {e: [] for e in self.ENGS}
        self.cnt = {e: 0 for e in self.ENGS}
        self.waited = {e: {} for e in self.ENGS}
        self.last_w = {}
        self.reads = {}
        self.sems = {}
        self.sem_keys = [("eng", e) for e in self.ENGS]
        self.dma_rings = {}

    def _need(self, eng, toks):
        need = {}
        for t in toks:
            if t is None:
                continue
            k, v = t
            if eng == "pe" and k == ("eng", "pe"):
                continue
            if need.get(k, 0) < v:
                need[k] = v
        w = self.waited[eng]
        for k, v in need.items():
            if w.get(k, 0) < v:
                w[k] = v
                self.q[eng].append(("wait", k, v))

    def _deps(self, reads, writes):
        toks = []
        for r in reads:
            toks.append(self.last_w.get(r))
        for w in writes:
            toks.append(self.last_w.get(w))
            toks.extend(self.reads.get(w, ()))
        return toks

    def _commit(self, tok, reads, writes):
        for r in reads:
            self.reads.setdefault(r, []).append(tok)
        for w in writes:
            self.last_w[w] = tok
            self.reads[w] = []

    def op(self, eng, fn, reads=(), writes=()):
        self._need(eng, self._deps(reads, writes))
        self.cnt[eng] += 1
        tok = (("eng", eng), self.cnt[eng])
        self.q[eng].append(("op", fn, ("eng", eng), 1))
        self._commit(tok, reads, writes)
        return tok

    def dma_ring(self, name, n):
        keys = [("dma", name, i) for i in range(n)]
        self.sem_keys.extend(keys)
        self.dma_rings[name] = {"keys": keys, "n": 0}

    def dma(self, eng, ring, fn, reads=(), writes=()):
        R = self.dma_rings[ring]
        i = R["n"]
        R["n"] += 1
        k = R["keys"][i % len(R["keys"])]
        prev = i // len(R["keys"])
        toks = self._deps(reads, writes)
        if prev > 0:
            toks.append((k, 16 * prev))
        self._need(eng, toks)
        tok = (k, 16 * (prev + 1))
        self.q[eng].append(("op", fn, k, 16))
        self._commit(tok, reads, writes)
        return tok

    def final_tokens(self):
        toks = list(self.last_w.values())
        for l in self.reads.values():
            toks.extend(l)
        return toks

    def emit(self):
        nc = self.nc
        with ExitStack() as es:
            for k in self.sem_keys:
                self.sems[k] = es.enter_context(nc.semaphore("s_" + "_".join(str(x) for x in k)))
            block = es.enter_context(nc.Block())
            engmap = {"pe": block.tensor, "act": block.scalar, "dve": block.vector,
                      "pool": block.gpsimd, "sp": block.sync}
            for e in self.ENGS:
                items = self.q[e]

                def body(engine, items=items):
                    for it in items:
                        if it[0] == "wait":
                            engine.wait_ge(self.sems[it[1]], it[2])
                        else:
                            it[1](engine).then_inc(self.sems[it[2]], it[3])
                engmap[e](body)


def build_program(n_tiles):
    nc = bass.Bass("TRN2", target_bir_lowering=False)
    tk = Tracker(nc)
    tk.dma_ring("wr", NB_RING)
    tk.dma_ring("cast", 8)
    tk.dma_ring("io", 8)
    tk.dma_ring("io2", 4)
    tk.dma_ring("wb", 6)
    NP = n_tiles * TMAX

    def din(name, shape):
        return nc.dram_tensor(name, list(shape), F32, kind="ExternalInput").ap()

    def dout(name, shape):
        return nc.dram_tensor(name, list(shape), F32, kind="ExternalOutput").ap()

    xp = din("xp", [WARM + NP, D]); xs = din("xs", [TS, D]); memp = din("memp", [NMEM, D])
    flag_d = din("flag", [P, 1])
    cca = din("cca", [2, D]); cfc = din("cfc", [2, 2, 2 * DFF])
    cmk = din("cmk", [2, NMEM, DXQ]); cmv = din("cmv", [2, NMEM, DXQ])
    norm_mix_g = din("norm_mix_g", [2, D]); norm_mem_g = din("norm_mem_g", [2, D])
    w_mem_k = din("w_mem_k", [2, D, DXQ]); w_mem_v = din("w_mem_v", [2, D, DXQ])
    w_in_a = din("w_in_a", [1, D, 3 * D + DXQ]); conv_a_w = din("conv_a_w", [1, 3, D])
    w_in_b = din("w_in_b", [1, D, 2 * D + DXQ])
    gmlp_norm_g = din("gmlp_norm_g", [1, D]); gmlp_norm_b = din("gmlp_norm_b", [1, D])
    gmlp_ws = din("gmlp_ws", [1, 8, P, P]); gmlp_bias = din("gmlp_bias", [1, 8, P])
    w_out = din("w_out", [2, D + DXQ, D]); norm_ffn_g = din("norm_ffn_g", [2, D])
    w_up = din("w_up", [2, D, 2 * DFF]); ffn_conv_w = din("ffn_conv_w", [2, 3, 2 * DFF])
    ffn_conv_b = din("ffn_conv_b", [2, 2 * DFF]); w_down = din("w_down", [2, DFF, D])
    norm_final_g = din("norm_final_g", [D])

    y_p = dout("y_p", [NP, D]); y_s = dout("y_s", [TS, D])
    ca_p = dout("ca_p", [2, D]); fc_p = dout("fc_p", [2, 2, 2 * DFF])
    mk_p = dout("mk_p", [2, NMEM, DXQ]); mv_p = dout("mv_p", [2, NMEM, DXQ])
    ca_s = dout("ca_s", [2, D]); fc_s = dout("fc_s", [2, 2, 2 * DFF])
    gv_s = dout("gv_s", [TS, D])

    slot_specs = []

    def add_slot(kind, W, k0, nk, c0):
        slot_specs.append((kind, W, k0, nk, c0))
        return len(slot_specs) - 1

    S = {}
    for l in range(2):
        S[("kT", l)] = [add_slot("std", w_mem_k[l], 0, 16, dc * P) for dc in range(8)]
        for nm, W in (("ktok", w_mem_k[l]), ("vtok", w_mem_v[l])):
            S[(nm, l)] = [[add_slot("wide", W, 4 * s, 4, cg * 512) for s in range(4)] for cg in range(2)]
    S["inA"] = [[add_slot("std", w_in_a[0], 0, 16, D + j * P), add_slot("std", w_in_a[0], 0, 16, 2 * D + j * P),
                 add_slot("std", w_in_a[0], 0, 16, j * P)] for j in range(16)]
    S["qA"] = [add_slot("std", w_in_a[0], 0, 16, 3 * D + qc * P) for qc in range(8)]
    S["uB"] = [add_slot("std", w_in_b[0], 0, 16, j * P) for j in range(16)]
    S["vB"] = [[add_slot("wide", w_in_b[0], 4 * s, 4, D + fg * 512) for s in range(4)] for fg in range(4)]
    S["qB"] = [add_slot("std", w_in_b[0], 0, 16, 2 * D + qc * P) for qc in range(8)]
    for l in range(2):
        S[("o1", l)] = [add_slot("std", w_out[l], 0, 16, oc * P) for oc in range(16)]
        S[("o2", l)] = [add_slot("std", w_out[l], 16, 8, oc * P) for oc in range(16)]
        S[("up", l)] = [[add_slot("std", w_up[l], 0, 16, j * P), add_slot("std", w_up[l], 0, 16, DFF + j * P)]
                        for j in range(NH)]
        S[("dn", l)] = [[add_slot("std", w_down[l], 11 * q, 11, oc * P) for oc in range(16)] for q in range(4)]
    NSLOT = len(slot_specs)
    wsc = nc.dram_tensor("wsc", [NSLOT, P, 2048], BF16).ap()

    def sb(name, shape, dt=F32):
        return nc.alloc_sbuf_tensor("sb_" + name, list(shape), dt)

    xT = sb("xT", [P, KD, TMAX])
    xn = sb("xn", [P, KD, TMAX], BF16)
    R1 = sb("R1", [P, 32 * TMAX], BF16)
    mixin = R1[:].rearrange("p (k t) -> p k t", t=TMAX)
    hid = R1[:, 0:22 * TMAX].rearrange("p (b k t) -> p b k t", b=2, k=11)
    memT = R1[:, 0:16 * 2 * NMEM].bitcast(F32).rearrange("p (k t) -> p k t", t=NMEM)
    KTp = sb("KTp", [P, 2, 8, NMEM], BF16); Vp = sb("Vp", [P, 2, 2, DXQ], BF16)
    KTs = sb("KTs", [P, 8, NMEM], BF16); Vs = sb("Vs", [P, 2, DXQ], BF16)
    Eb = sb("Eb", [P, 2, 2, TMAX], BF16)
    XS = sb("XS", [P, 2, D])
    vn = sb("vn", [P, 4, D], BF16)
    gb = sb("gb", [P, 2, D])
    wring = sb("wring", [P, NB_RING, 2048], BF16)
    CT = sb("CT", [P, 2, 3, 520])
    rstd = sb("rstd", [P, TMAX]); rden = sb("rden", [P, TMAX])
    histA = sb("histA", [P, 2, 16, 2]); histF = sb("histF", [P, 2, 2, NUP, 2])
    ident = sb("ident", [P, P]); ones_f = sb("ones_f", [P, P]); ones_b = sb("ones_b", [P, P], BF16)
    wmT = sb("wmT", [P, 8, P], BF16); biasbc = sb("biasbc", [P, 8, P])
    g_mix = sb("g_mix", [P, 2, KD]); g_ffn = sb("g_ffn", [P, 2, KD]); g_mem = sb("g_mem", [P, 2, KD])
    g_fin = sb("g_fin", [P, KD])
    caw = sb("caw", [P, 3, KD]); fcw = sb("fcw", [P, 2, 3, NUP]); fcb = sb("fcb", [P, 2, NUP])
    flag = sb("flag", [P, 1]); epsb = sb("epsb", [P, 1])
    stt = sb("stt", [P, 2, 4, 6]); mv = sb("mv", [P, 2, 2]); lnr = sb("lnr", [P, 2, 2])
    banks = [nc.alloc_psum_tensor(f"bank{i}", [P, TMAX], F32) for i in range(8)]
    bank_ctr = [0]

    def next_bank():
        b = bank_ctr[0] % 7
        bank_ctr[0] += 1
        return b, banks[b], ("bank", b)

    ring_ctr = [0]
    cast_done = set()
    cur_pass = [-1]
    n_pass = n_tiles + 1
    wb_pass = {}
    for l in range(2):
        for sid in S[("kT", l)] + [x for nm in ("ktok", "vtok") for row in S[(nm, l)] for x in row]:
            wb_pass[sid] = None
        ffn_sids = [x for pair in S[("up", l)] for x in pair] + [x for row in S[("dn", l)] for x in row]
        for i, sid in enumerate(ffn_sids):
            wb_pass[sid] = i % 3

    def load_slot(sid):
        b = ring_ctr[0] % NB_RING
        ring_ctr[0] += 1
        kind, W, k0, nk, c0 = slot_specs[sid]
        cw = 128 if kind == "std" else 512
        ncol = nk * cw
        if sid not in cast_done:
            src = W[k0 * P:(k0 + nk) * P, c0:c0 + cw].rearrange("(k p) c -> p k c", p=P)
            dst = wring[:, b, 0:ncol].rearrange("p (k c) -> p k c", c=cw)
            tk.dma("pool", "cast", lambda e, dst=dst, src=src: e.dma_start(out=dst, in_=src), writes=[("wr", b)])
            wp = wb_pass.get(sid, 0)
            if wp is not None and cur_pass[0] >= wp and cur_pass[0] < n_pass - 1:
                cast_done.add(sid)
                tk.dma("sp", "wb", lambda e, b=b, sid=sid, ncol=ncol: e.dma_start(out=wsc[sid][:, 0:ncol], in_=wring[:, b, 0:ncol]),
                       reads=[("wr", b)], writes=[("wsc", sid)])
        else:
            tk.dma("sp", "wr", lambda e, b=b, sid=sid, ncol=ncol: e.dma_start(out=wring[:, b, 0:ncol], in_=wsc[sid][:, 0:ncol]),
                   reads=[("wsc", sid)], writes=[("wr", b)])
        return b

    def emit_cast(sid):
        kind, W, k0, nk, c0 = slot_specs[sid]
        cw = 128 if kind == "std" else 512
        src = W[k0 * P:(k0 + nk) * P, c0:c0 + cw].rearrange("(k p) c -> p k c", p=P)
        dst = wsc[sid][:, 0:nk * cw].rearrange("p (k c) -> p k c", c=cw)
        tk.dma("pool", "cast", lambda e: e.dma_start(out=dst, in_=src), writes=[("wsc", sid)])

    def io_dma(out_ap, in_ap, reads=(), writes=(), eng="pool"):
        tk.dma(eng, "io" if eng == "pool" else "io2", lambda e: e.dma_start(out=out_ap, in_=in_ap), reads=reads, writes=writes)

    def std_group(sid, nk, rhs_fn, rhs_res, N):
        b = load_slot(sid)
        bi, bk, bres = next_bank()
        w3 = wring[:, b, :].rearrange("p (k c) -> p k c", c=P)

        def fn(e):
            for k in range(nk):
                ins = e.matmul(bk[:, 0:N], lhsT=w3[:, k, :], rhs=rhs_fn(k), start=(k == 0), stop=(k == nk - 1))
            return ins
        tk.op("pe", fn, reads=[("wr", b)] + list(rhs_res), writes=[bres])
        return bk, bres

    def xn_res():
        return [("xn", k) for k in range(KD)]

    def std_groups_il(sids, N):
        bs = [load_slot(sid) for sid in sids]
        bks = [next_bank() for _ in sids]
        for k in range(KD):
            for b, (bi, bk, bres) in zip(bs, bks):
                w3 = wring[:, b, :].rearrange("p (k c) -> p k c", c=P)
                tk.op("pe", lambda e, bk=bk, w3=w3, k=k: e.matmul(bk[:, 0:N], lhsT=w3[:, k, :], rhs=xn[:, k, 0:N],
                      start=(k == 0), stop=(k == KD - 1)), reads=[("wr", b), ("xn", k)], writes=[bres])
        return [(bk, bres) for (bi, bk, bres) in bks]

    tk.op("pool", lambda e: e.memset(ones_f[:], 1.0), writes=["ones_f"])
    tk.op("pool", lambda e: e.affine_select(out=ident[:], in_=ones_f[:], pattern=[[1, P]], compare_op=ALU.is_equal,
                                            fill=0.0, base=0, channel_multiplier=-1), reads=["ones_f"], writes=["ident"])
    tk.op("dve", lambda e: e.tensor_copy(out=ones_b[:], in_=ones_f[:]), reads=["ones_f"], writes=["ones_b"])
    tk.op("dve", lambda e: e.memset(epsb[:], EPS), writes=["epsb"])
    tk.op("dve", lambda e: e.memset(histA[:].rearrange("p a b c -> p (a b c)"), 0.0), writes=["histA"])
    tk.op("dve", lambda e: e.memset(histF[:].rearrange("p a b c d -> p (a b c d)"), 0.0), writes=["histF"])
    io_dma(flag[:], flag_d, writes=["flag"])
    io_dma(gb[:, 0, :], gmlp_norm_g[0].partition_broadcast(P), writes=["gb"])
    io_dma(gb[:, 1, :], gmlp_norm_b[0].partition_broadcast(P), writes=["gb"])
    io_dma(biasbc[:].rearrange("p g t -> p (g t)"), gmlp_bias[0].rearrange("g t -> (g t)").partition_broadcast(P),
           writes=["biasbc"])

    def rows_to_fm(dst_ap, rows_ap, nch, dst_res, stage_col):
        sl_ = stage_col % 6
        sres = ("ct", sl_ // 3, sl_ % 3)
        stg = CT[0:nch, sl_ // 3, sl_ % 3, 0:P]
        io_dma(stg, rows_ap.rearrange("(c f) -> c f", f=P), writes=[sres])
        bi, bk, bres = next_bank()
        tk.op("pe", lambda e: e.transpose(bk[:, 0:nch], stg, ident[0:nch, 0:nch]),
              reads=[sres, "ident"], writes=[bres])
        tk.op("act", lambda e: e.activation(out=dst_ap, in_=bk[:, 0:nch], func=AF.Identity),
              reads=[bres], writes=[dst_res])

    col = [0]

    def vec_load(dst_ap, rows_ap, nch, dst_res):
        rows_to_fm(dst_ap, rows_ap, nch, dst_res, col[0])
        col[0] += 1

    for l in range(2):
        vec_load(g_mix[:, l, :], norm_mix_g[l], KD, "gvec")
        vec_load(g_ffn[:, l, :], norm_ffn_g[l], KD, "gvec")
        vec_load(g_mem[:, l, :], norm_mem_g[l], KD, "gvec")
        vec_load(fcb[:, l, :], ffn_conv_b[l], NUP, "fcb")
        for t in range(3):
            vec_load(fcw[:, l, t, :], ffn_conv_w[l, t], NUP, "fcw")
        for r in range(2):
            vec_load(histF[:, 1, l, :, r], cfc[l, r], NUP, "histF")
    vec_load(g_fin[:, :], norm_final_g, KD, "gvec")
    for t in range(3):
        vec_load(caw[:, t, :], conv_a_w[0, t], KD, "caw")
    for r in range(2):
        vec_load(histA[:, 1, :, r], cca[r], KD, "histA")

    for g in range(8):
        stg = XS[:, 1, (g % 8) * P:(g % 8 + 1) * P]
        io_dma(stg, gmlp_ws[0, g], writes=[("XS", 1)])
        bi, bk, bres = next_bank()
        tk.op("pe", lambda e, bk=bk, stg=stg: e.transpose(bk[:, 0:P], stg, ident[:]),
              reads=[("XS", 1), "ident"], writes=[bres])
        tmp = CT[:, 0, 0, 0:P]
        tk.op("act", lambda e, bk=bk, tmp=tmp: e.activation(out=tmp, in_=bk[:, 0:P], func=AF.Identity),
              reads=[bres], writes=[("ct", 0, 0)])
        tmp2 = CT[:, 0, 1, 0:P]
        tk.op("pool", lambda e, tmp=tmp, tmp2=tmp2: e.affine_select(out=tmp2, in_=tmp, pattern=[[1, P]],
              compare_op=ALU.is_ge, fill=0.0, base=0, channel_multiplier=-1), reads=[("ct", 0, 0)], writes=[("ct", 0, 1)])
        tk.op("dve", lambda e, g=g, tmp2=tmp2: e.tensor_copy(out=wmT[:, g, :], in_=tmp2), reads=[("ct", 0, 1)], writes=["wmT"])

    def load_tokens(src_rows_ap, nrows, dstT, c0, dst_name):
        nblk = (nrows + P - 1) // P
        for blk in range(nblk):
            nr = min(P, nrows - blk * P)
            bi_ = blk % 2
            io_dma(XS[0:nr, bi_, :], src_rows_ap[blk * P:blk * P + nr, :], writes=[("XS", bi_)])
            for g4 in range(4):
                bi, bk, bres = next_bank()

                def fn(e, bk=bk, bi_=bi_, nr=nr, g4=g4):
                    for q in range(4):
                        ins = e.transpose(bk[:, q * P:q * P + nr], XS[0:nr, bi_, (4 * g4 + q) * P:(4 * g4 + q + 1) * P],
                                          ident[0:nr, 0:nr])
                    return ins
                tk.op("pe", fn, reads=[("XS", bi_), "ident"], writes=[bres])
                src = bk[:].rearrange("p (q t) -> p q t", q=4)[:, :, 0:nr]
                dst = dstT[:, 4 * g4:4 * g4 + 4, c0 + blk * P:c0 + blk * P + nr]
                eng = "act" if g4 % 2 == 0 else "dve"
                if eng == "act":
                    tk.op("act", lambda e, src=src, dst=dst: e.activation(out=dst, in_=src, func=AF.Identity),
                          reads=[bres], writes=[(dst_name, 4 * g4 + q) for q in range(4)])
                else:
                    tk.op("dve", lambda e, src=src, dst=dst: e.tensor_copy(out=dst, in_=src),
                          reads=[bres], writes=[(dst_name, 4 * g4 + q) for q in range(4)])

    def store_tokens(srcT, c0, nrows, dst_rows_ap, src_name):
        nblk = (nrows + P - 1) // P
        for blk in range(nblk):
            nr = min(P, nrows - blk * P)
            bi_ = blk % 2
            for g4 in range(4):
                bi, bk, bres = next_bank()

                def fn(e, bk=bk, nr=nr, g4=g4, blk=blk):
                    for q in range(4):
                        ins = e.transpose(bk[0:nr, q * P:(q + 1) * P],
                                          srcT[:, 4 * g4 + q, c0 + blk * P:c0 + blk * P + nr], ident[:])
                    return ins
                tk.op("pe", fn, reads=[(src_name, 4 * g4 + q) for q in range(4)] + ["ident"], writes=[bres])
                dst = XS[0:nr, bi_, g4 * 512:(g4 + 1) * 512]
                if g4 % 2 == 0:
                    tk.op("act", lambda e, bk=bk, dst=dst, nr=nr: e.activation(out=dst, in_=bk[0:nr, :], func=AF.Identity),
                          reads=[bres], writes=[("XS", bi_)])
                else:
                    tk.op("dve", lambda e, bk=bk, dst=dst, nr=nr: e.tensor_copy(out=dst, in_=bk[0:nr, :]),
                          reads=[bres], writes=[("XS", bi_)])
            io_dma(dst_rows_ap[blk * P:blk * P + nr, :], XS[0:nr, bi_, :], reads=[("XS", bi_)], writes=[("out", id(dst_rows_ap), blk)])

    class Norm:
        def __init__(self, srcT, src_name, gvec, N, inplace=False):
            self.srcT, self.src_name, self.gvec, self.N, self.inplace = srcT, src_name, gvec, N, inplace
            self.pending = None
            self.n_mm = 0

        def _mm(self, k):
            N = self.N
            first, last = self.n_mm == 0, self.n_mm == KD - 1
            self.n_mm += 1
            bk = banks[7]
            tk.op("pe", lambda e: e.matmul(bk[:, 0:N], lhsT=ones_b[:], rhs=xn[:, k, 0:N], start=first, stop=last),
                  reads=[("xn", k), "ones_b"], writes=[("bank", 7)])

        def feed(self, k):
            N, srcT = self.N, self.srcT
            tk.op("act", lambda e: e.activation(out=xn[:, k, 0:N], in_=srcT[:, k, 0:N], func=AF.Square),
                  reads=[(self.src_name, k)], writes=[("xn", k)])
            if self.pending is not None:
                self._mm(self.pending)
            self.pending = k

        def finish(self):
            N, srcT, gvec, src_name = self.N, self.srcT, self.gvec, self.src_name
            self._mm(self.pending)
            assert self.n_mm == KD
            bk = banks[7]
            tk.op("act", lambda e: e.activation(out=rstd[:, 0:N], in_=bk[:, 0:N], func=AF.Sqrt, scale=1.0 / D, bias=epsb[:]),
                  reads=[("bank", 7), "epsb"], writes=["rstd"])
            tk.op("dve", lambda e: e.reciprocal(out=rstd[:, 0:N], in_=rstd[:, 0:N]), reads=["rstd"], writes=["rstd"])
            for k in range(KD):
                if self.inplace:
                    tk.op("dve", lambda e, k=k: e.scalar_tensor_tensor(out=srcT[:, k, 0:N], in0=srcT[:, k, 0:N],
                          scalar=gvec[:, k:k + 1], in1=rstd[:, 0:N], op0=ALU.mult, op1=ALU.mult),
                          reads=[(src_name, k), "rstd", "gvec"], writes=[(src_name, k)])
                else:
                    tk.op("dve", lambda e, k=k: e.scalar_tensor_tensor(out=xn[:, k, 0:N], in0=srcT[:, k, 0:N],
                          scalar=gvec[:, k:k + 1], in1=rstd[:, 0:N], op0=ALU.mult, op1=ALU.mult),
                          reads=[(src_name, k), "rstd", "gvec"], writes=[("xn", k)])

    def rmsnorm(srcT, src_name, gvec, N, out_fp32_inplace=False):
        nm = Norm(srcT, src_name, gvec, N, out_fp32_inplace)
        for k in range(KD):
            nm.feed(k)
        nm.finish()

    load_tokens(memp, NMEM, memT, 0, "memT")

    HALF = TMAX // 2

    def segs_of_pass(p):
        if p == 0:
            return [dict(c0=0, T=WARM, stream=0, src=xp[0:WARM, :], dst=None, flag_after=True),
                    dict(c0=WARM, T=HALF, stream=0, src=xp[WARM:WARM + HALF, :], dst=y_p[0:HALF, :])], WARM + HALF
        if p == n_pass - 1:
            return [dict(c0=0, T=HALF, stream=0, src=xp[WARM + NP - HALF:WARM + NP, :], dst=y_p[NP - HALF:NP, :]),
                    dict(c0=HALF, T=TS, stream=1, src=xs, dst=y_s)], HALF + TS
        y0 = HALF + (p - 1) * TMAX
        return [dict(c0=0, T=TMAX, stream=0, src=xp[WARM + y0:WARM + y0 + TMAX, :], dst=y_p[y0:y0 + TMAX, :])], TMAX

    segs0, N0 = segs_of_pass(0)
    for sg in segs0:
        load_tokens(sg["src"], sg["T"], xT, sg["c0"], "xT")


    for l in range(2):
        rmsnorm(memT, "memT", g_mem[:, l, :], NMEM)
        for dc in range(8):
            bk, bres = std_group(S[("kT", l)][dc], 16, lambda k: xn[:, k, 0:NMEM], xn_res(), NMEM)
            tk.op("act", lambda e, bk=bk, l=l, dc=dc: e.activation(out=KTp[:, l, dc, :], in_=bk[:, 0:NMEM], func=AF.Identity),
                  reads=[bres], writes=[("KTp", l)])
        for wi, (nm, dstd) in enumerate((("ktok", mk_p), ("vtok", mv_p))):
            for cg in range(2):
                bks = [next_bank() for _ in range(2)]
                for s in range(4):
                    b = load_slot(S[(nm, l)][cg][s])
                    w3 = wring[:, b, :].rearrange("p (k c) -> p k c", c=512)

                    def fn(e, s=s, w3=w3, bks=bks):
                        for mb in range(2):
                            for kk in range(4):
                                ins = e.matmul(bks[mb][1][:, 0:512], lhsT=xn[:, 4 * s + kk, mb * P:(mb + 1) * P],
                                               rhs=w3[:, kk, :], start=(s == 0 and kk == 0), stop=(s == 3 and kk == 3))
                        return ins
                    tk.op("pe", fn, reads=[("wr", b)] + xn_res(), writes=[bks[0][2], bks[1][2]])
                for mb in range(2):
                    dst = XS[:, mb, wi * DXQ + cg * 512:wi * DXQ + (cg + 1) * 512]
                    tk.op("act", lambda e, dst=dst, bk=bks[mb][1]: e.activation(out=dst, in_=bk[:, 0:512], func=AF.Identity),
                          reads=[bks[mb][2]], writes=[("XS", mb)])
        for mb in range(2):
            tk.op("dve", lambda e, l=l, mb=mb: e.tensor_copy(out=Vp[:, l, mb, :], in_=XS[:, mb, DXQ:2 * DXQ]),
                  reads=[("XS", mb)], writes=[("Vp", l)])
            io_dma(mk_p[l, mb * P:(mb + 1) * P, :], XS[:, mb, 0:DXQ], reads=[("XS", mb)], writes=[("mk_p", l, mb)], eng="sp")
            io_dma(mv_p[l, mb * P:(mb + 1) * P, :], XS[:, mb, DXQ:2 * DXQ], reads=[("XS", mb)], writes=[("mv_p", l, mb)], eng="sp")

    def prep_sample_kv(l):
        for mb in range(2):
            io_dma(XS[:, mb, 0:DXQ], cmk[l, mb * P:(mb + 1) * P, :], writes=[("XS", mb)], eng="sp")
            io_dma(XS[:, mb, DXQ:2 * DXQ], cmv[l, mb * P:(mb + 1) * P, :], writes=[("XS", mb)], eng="sp")
            tk.op("dve", lambda e, mb=mb: e.tensor_copy(out=Vs[:, mb, :], in_=XS[:, mb, DXQ:2 * DXQ]),
                  reads=[("XS", mb)], writes=["Vs"])
        for d2 in range(4):
            bi, bk, bres = next_bank()

            def fn(e, bk=bk, d2=d2):
                for dd in range(2):
                    for mb in range(2):
                        ins = e.transpose(bk[:, dd * NMEM + mb * P:dd * NMEM + (mb + 1) * P],
                                          XS[:, mb, (2 * d2 + dd) * P:(2 * d2 + dd + 1) * P], ident[:])
                return ins
            tk.op("pe", fn, reads=[("XS", 0), ("XS", 1), "ident"], writes=[bres])
            tk.op("act", lambda e, bk=bk, d2=d2: e.activation(out=KTs[:, 2 * d2:2 * d2 + 2, :],
                  in_=bk[:].rearrange("p (a m) -> p a m", a=2), func=AF.Identity), reads=[bres], writes=["KTs"])

    def mix_res(ks):
        return [("mix", k) for k in ks]

    def attention_gen(l, segs):
        for sg in segs:
            c0, T, st = sg["c0"], sg["T"], sg["stream"]
            for h in range(4):
                eb = h % 2
                for mc in range(2):
                    bi, bk, bres = next_bank()

                    def fn(e, bk=bk, mc=mc, h=h, st=st, c0=c0, T=T):
                        for dc in range(2):
                            kt = KTp[:, l, 2 * h + dc, mc * P:(mc + 1) * P] if st == 0 else KTs[:, 2 * h + dc, mc * P:(mc + 1) * P]
                            ins = e.matmul(bk[:, 0:T], lhsT=kt, rhs=mixin[:, 24 + 2 * h + dc, c0:c0 + T],
                                           start=(dc == 0), stop=(dc == 1))
                        return ins
                    tk.op("pe", fn, reads=mix_res([24 + 2 * h, 25 + 2 * h]) + [("KTp", l), "KTs"], writes=[bres])
                    tk.op("act", lambda e, bk=bk, mc=mc, eb=eb, T=T: e.activation(out=Eb[:, eb, mc, 0:T], in_=bk[:, 0:T], func=AF.Exp),
                          reads=[bres], writes=[("Eb", eb, mc)])
                yield
                bi, bkd, bresd = next_bank()

                def fnd(e, bkd=bkd, eb=eb, T=T):
                    for mc in range(2):
                        ins = e.matmul(bkd[:, 0:T], lhsT=ones_b[:], rhs=Eb[:, eb, mc, 0:T], start=(mc == 0), stop=(mc == 1))
                    return ins
                tk.op("pe", fnd, reads=[("Eb", eb, 0), ("Eb", eb, 1), "ones_b"], writes=[bresd])
                tk.op("dve", lambda e, bkd=bkd, T=T: e.reciprocal(out=rden[:, 0:T], in_=bkd[:, 0:T]), reads=[bresd], writes=["rden"])
                for dc in range(2):
                    bi, bk, bres = next_bank()

                    def fnv(e, bk=bk, dc=dc, h=h, eb=eb, st=st, T=T):
                        for mc in range(2):
                            vv = Vp[:, l, mc, h * 256 + dc * P:h * 256 + (dc + 1) * P] if st == 0 else Vs[:, mc, h * 256 + dc * P:h * 256 + (dc + 1) * P]
                            ins = e.matmul(bk[:, 0:T], lhsT=vv, rhs=Eb[:, eb, mc, 0:T], start=(mc == 0), stop=(mc == 1))
                        return ins
                    tk.op("pe", fnv, reads=[("Eb", eb, 0), ("Eb", eb, 1), ("Vp", l), "Vs"], writes=[bres])
                    tk.op("dve", lambda e, bk=bk, dc=dc, h=h, c0=c0, T=T: e.tensor_tensor(out=mixin[:, 16 + 2 * h + dc, c0:c0 + T], in0=bk[:, 0:T],
                          in1=rden[:, 0:T], op=ALU.mult), reads=[bres, "rden"], writes=[("mix", 16 + 2 * h + dc)])
                yield

    def out_proj(l, N, feed=None):
        for oc in range(16):
            b1 = load_slot(S[("o1", l)][oc])
            b2 = load_slot(S[("o2", l)][oc])
            bi, bk, bres = next_bank()
            wa = wring[:, b1, :].rearrange("p (k c) -> p k c", c=P)
            wb_ = wring[:, b2, :].rearrange("p (k c) -> p k c", c=P)

            def fn(e, bk=bk, wa=wa, wb_=wb_):
                for k in range(16):
                    e.matmul(bk[:, 0:N], lhsT=wa[:, k, :], rhs=mixin[:, k, 0:N], start=(k == 0), stop=False)
                for k in range(8):
                    ins = e.matmul(bk[:, 0:N], lhsT=wb_[:, k, :], rhs=mixin[:, 16 + k, 0:N], start=False, stop=(k == 7))
                return ins
            tk.op("pe", fn, reads=[("wr", b1), ("wr", b2)] + mix_res(range(24)), writes=[bres])
            tk.op("dve", lambda e, bk=bk, oc=oc: e.tensor_tensor(out=xT[:, oc, 0:N], in0=bk[:, 0:N], in1=xT[:, oc, 0:N], op=ALU.add),
                  reads=[bres, ("xT", oc)], writes=[("xT", oc)])
            if feed is not None:
                feed.feed(oc)

    def q_proj(sl, N):
        first = std_groups_il(sl[0:3], N)
        for qc in range(8):
            if qc < 3:
                bk, bres = first[qc]
            else:
                bk, bres = std_group(sl[qc], 16, lambda k: xn[:, k, 0:N], xn_res(), N)
            tk.op("act", lambda e, bk=bk, qc=qc: e.activation(out=mixin[:, 24 + qc, 0:N], in_=bk[:, 0:N], func=AF.Identity, scale=0.0625),
                  reads=[bres], writes=[("mix", 24 + qc)])

    ct_ctr = [0]

    def mixer_a(segs, N, step):
        for j in range(16):
            step()
            sc, sh, sbg = S["inA"][j]
            bkc, brc = std_group(sc, 16, lambda k: xn[:, k, 0:N], xn_res(), N)
            bkh, brh = std_group(sh, 16, lambda k: xn[:, k, 0:N], xn_res(), N)
            bkb, brb = std_group(sbg, 16, lambda k: xn[:, k, 0:N], xn_res(), N)
            cb = ct_ctr[0] % 2
            ct_ctr[0] += 1
            hsb, ch, tcv = CT[:, cb, 0, :], CT[:, cb, 1, :], CT[:, cb, 2, :]
            tk.op("act", lambda e, bkh=bkh, hsb=hsb: e.activation(out=hsb[:, 0:N], in_=bkh[:, 0:N], func=AF.Identity),
                  reads=[brh], writes=[("ct", cb, 0)])
            for si, sg in enumerate(segs):
                c0, T, st = sg["c0"], sg["T"], sg["stream"]
                off = c0 + 2 * si
                tk.op("dve", lambda e, ch=ch, off=off, st=st, j=j: e.tensor_copy(out=ch[:, off:off + 2], in_=histA[:, st, j, :]),
                      reads=["histA"], writes=[("ct", cb, 1)])
                tk.op("dve", lambda e, ch=ch, off=off, bkc=bkc, hsb=hsb, c0=c0, T=T: e.tensor_tensor(out=ch[:, off + 2:off + 2 + T],
                      in0=bkc[:, c0:c0 + T], in1=hsb[:, c0:c0 + T], op=ALU.mult), reads=[brc, ("ct", cb, 0)], writes=[("ct", cb, 1)])
                tk.op("dve", lambda e, ch=ch, off=off, st=st, j=j, T=T: e.tensor_copy(out=histA[:, st, j, :], in_=ch[:, off + T:off + T + 2]),
                      reads=[("ct", cb, 1)], writes=["histA"])
                if sg.get("flag_after"):
                    tk.op("dve", lambda e, st=st, j=j: e.tensor_scalar(out=histA[:, st, j, :], in0=histA[:, st, j, :], scalar1=flag[:, 0:1],
                          scalar2=None, op0=ALU.mult), reads=["histA", "flag"], writes=["histA"])
                tk.op("dve", lambda e, ch=ch, off=off, tcv=tcv, c0=c0, T=T, j=j: e.tensor_scalar(out=tcv[:, c0:c0 + T], in0=ch[:, off + 2:off + 2 + T],
                      scalar1=caw[:, 2, j:j + 1], scalar2=None, op0=ALU.mult), reads=[("ct", cb, 1), "caw"], writes=[("ct", cb, 2)])
                for tap in (1, 0):
                    tk.op("dve", lambda e, ch=ch, off=off, tcv=tcv, c0=c0, T=T, j=j, tap=tap: e.scalar_tensor_tensor(out=tcv[:, c0:c0 + T],
                          in0=ch[:, off + tap:off + tap + T], scalar=caw[:, tap, j:j + 1], in1=tcv[:, c0:c0 + T], op0=ALU.mult, op1=ALU.add),
                          reads=[("ct", cb, 1), ("ct", cb, 2), "caw"], writes=[("ct", cb, 2)])
            tk.op("dve", lambda e, bkb=bkb, tcv=tcv, j=j: e.tensor_tensor(out=mixin[:, j, 0:N], in0=bkb[:, 0:N], in1=tcv[:, 0:N], op=ALU.mult),
                  reads=[brb, ("ct", cb, 2)], writes=[("mix", j)])

    def mixer_b(segs, N, step):
        def u_proj(js):
            for j in js:
                step()
                bk, bres = std_group(S["uB"][j], 16, lambda k: xn[:, k, 0:N], xn_res(), N)
                tk.op("act", lambda e, bk=bk, j=j: e.activation(out=mixin[:, j, 0:N], in_=bk[:, 0:N], func=AF.Gelu),
                      reads=[bres], writes=[("mix", j)])

        blocks = []
        for sg in segs:
            nblk = (sg["T"] + P - 1) // P
            for blk in range(nblk):
                nr = min(P, sg["T"] - blk * P)
                blocks.append(dict(cs=sg["c0"] + blk * P, nr=nr, sample=(sg["stream"] == 1)))
        halves = [blocks[h0:h0 + 2] for h0 in range(0, len(blocks), 2)]
        usplit = [range(0, 8), range(8, 16)] if len(halves) == 2 else [range(0, 16)]

        def v_proj(half, vb):
            for fg in range(4):
                bks = [next_bank() for _ in half]
                for s_ in range(4):
                    b = load_slot(S["vB"][fg][s_])
                    w3 = wring[:, b, :].rearrange("p (k c) -> p k c", c=512)

                    def fn(e, s_=s_, w3=w3, bks=bks, half=half):
                        for i, bl in enumerate(half):
                            for kk in range(4):
                                ins = e.matmul(bks[i][1][0:bl["nr"], 0:512], lhsT=xn[:, 4 * s_ + kk, bl["cs"]:bl["cs"] + bl["nr"]],
                                               rhs=w3[:, kk, :], start=(s_ == 0 and kk == 0), stop=(s_ == 3 and kk == 3))
                        return ins
                    tk.op("pe", fn, reads=[("wr", b)] + xn_res(), writes=[x[2] for x in bks])
                for i, bl in enumerate(half):
                    dst = XS[0:bl["nr"], i, fg * 512:(fg + 1) * 512]
                    tk.op("act", lambda e, dst=dst, bk=bks[i][1], nr=bl["nr"]: e.activation(out=dst, in_=bk[0:nr, 0:512], func=AF.Gelu),
                          reads=[bks[i][2]], writes=[("XS", i)])
            for i, bl in enumerate(half):
                nr = bl["nr"]
                xv = XS[0:nr, i, :]
                st_ = stt[0:nr, i, :, :]
                for c4 in range(4):
                    tk.op("dve", lambda e, c4=c4, nr=nr, i=i, st_=st_: e.bn_stats(out=st_[:, c4, :], in_=XS[0:nr, i, c4 * 512:(c4 + 1) * 512]),
                          reads=[("XS", i)], writes=[("stt", i)])
                tk.op("dve", lambda e, nr=nr, i=i, st_=st_: e.bn_aggr(out=mv[0:nr, i, :], in_=st_.rearrange("p c s -> p (c s)")),
                      reads=[("stt", i)], writes=[("mv", i)])
                tk.op("act", lambda e, nr=nr, i=i: e.activation(out=lnr[0:nr, i, 0:1], in_=mv[0:nr, i, 1:2], func=AF.Sqrt, scale=1.0, bias=epsb[0:nr, :]),
                      reads=[("mv", i), "epsb"], writes=[("lnr", i)])
                tk.op("dve", lambda e, nr=nr, i=i: e.reciprocal(out=lnr[0:nr, i, 0:1], in_=lnr[0:nr, i, 0:1]), reads=[("lnr", i)], writes=[("lnr", i)])
                tk.op("dve", lambda e, nr=nr, i=i: e.tensor_scalar(out=lnr[0:nr, i, 1:2], in0=mv[0:nr, i, 0:1], scalar1=lnr[0:nr, i, 0:1], scalar2=-1.0,
                      op0=ALU.mult, op1=ALU.mult), reads=[("mv", i), ("lnr", i)], writes=[("lnr", i)])
                tk.op("act", lambda e, xv=xv, nr=nr, i=i: e.activation(out=xv, in_=xv, func=AF.Identity, scale=lnr[0:nr, i, 0:1], bias=lnr[0:nr, i, 1:2]),
                      reads=[("XS", i), ("lnr", i)], writes=[("XS", i)])
                ge = "dve" if i == 0 else "pool"
                tk.op(ge, lambda e, xv=xv, nr=nr: e.tensor_tensor(out=xv, in0=xv, in1=gb[0:nr, 0, :], op=ALU.mult),
                      reads=[("XS", i), "gb"], writes=[("XS", i)])
                if bl["sample"]:
                    tk.op(ge, lambda e, xv=xv, nr=nr: e.tensor_tensor(out=xv, in0=xv, in1=gb[0:nr, 1, :], op=ALU.add),
                          reads=[("XS", i), "gb"], writes=[("XS", i)])
                    tk.op("act", lambda e, xv=xv, nr=nr, i=i: e.activation(out=vn[0:nr, vb + i, :], in_=xv, func=AF.Identity),
                          reads=[("XS", i)], writes=[("vn", vb + i)])
                    io_dma(gv_s, xv, reads=[("XS", i)], writes=["gv_s"])
                else:
                    tk.op(ge, lambda e, xv=xv, nr=nr, i=i: e.tensor_tensor(out=vn[0:nr, vb + i, :], in0=xv, in1=gb[0:nr, 1, :], op=ALU.add),
                          reads=[("XS", i), "gb"], writes=[("vn", vb + i)])

        def mixing(half, vb):
            hc0 = half[0]["cs"]
            for fc in range(16):
                step()
                g = fc // 2
                bi, bk, bres = next_bank()

                def fnm(e, bk=bk, fc=fc, g=g, half=half, hc0=hc0):
                    for i, bl in enumerate(half):
                        nr = bl["nr"]
                        lc = bl["cs"] - hc0
                        ins = e.matmul(bk[:, lc:lc + nr], lhsT=vn[0:nr, vb + i, fc * P:(fc + 1) * P], rhs=wmT[0:nr, g, 0:nr],
                                       start=True, stop=True)
                    return ins
                tk.op("pe", fnm, reads=[("vn", vb + i) for i in range(len(half))] + ["wmT"], writes=[bres])
                cb = ct_ctr[0] % 6
                ct_ctr[0] += 1
                tmp = CT[:, cb // 3, cb % 3, :]
                hn = half[-1]["cs"] + half[-1]["nr"] - hc0
                if all(bl["nr"] == P for bl in half):
                    nb_ = len(half)
                    tk.op("dve", lambda e, bk=bk, tmp=tmp, g=g, nb_=nb_, hn=hn: e.tensor_tensor(
                          out=tmp[:, 0:hn].rearrange("p (a t) -> p a t", a=nb_), in0=bk[:, 0:hn].rearrange("p (a t) -> p a t", a=nb_),
                          in1=biasbc[:, g, :].unsqueeze(1).to_broadcast([P, nb_, P]), op=ALU.add), reads=[bres, "biasbc"], writes=[("ct", cb // 3, cb % 3)])
                else:
                    for i, bl in enumerate(half):
                        nr = bl["nr"]
                        lc = bl["cs"] - hc0
                        tk.op("dve", lambda e, bk=bk, tmp=tmp, lc=lc, nr=nr, g=g: e.tensor_tensor(out=tmp[:, lc:lc + nr], in0=bk[:, lc:lc + nr],
                              in1=biasbc[:, g, 0:nr], op=ALU.add), reads=[bres, "biasbc"], writes=[("ct", cb // 3, cb % 3)])
                tk.op("pool" if fc % 2 else "dve", lambda e, tmp=tmp, fc=fc, hn=hn, hc0=hc0: e.tensor_tensor(out=mixin[:, fc, hc0:hc0 + hn], in0=tmp[:, 0:hn],
                      in1=mixin[:, fc, hc0:hc0 + hn], op=ALU.mult), reads=[("ct", cb // 3, cb % 3), ("mix", fc)], writes=[("mix", fc)])

        assert len(halves) <= 2
        for hi, half in enumerate(halves):
            v_proj(half, 2 * hi)
            u_proj(usplit[hi])
        for hi, half in enumerate(halves):
            mixing(half, 2 * hi)

    def ffn(l, segs, N, feed=None):
        def up(q):
            hb = q % 2
            for jj in range(11):
                j = 11 * q + jj
                cb = ct_ctr[0] % 3
                ct_ctr[0] += 1
                tt = []
                pre = std_groups_il(S[("up", l)][j], N) if j == 0 else None
                info = []
                for ag in range(2):
                    uc = j + ag * NH
                    if pre is not None:
                        bk, bres = pre[ag]
                    else:
                        bk, bres = std_group(S[("up", l)][j][ag], 16, lambda k: xn[:, k, 0:N], xn_res(), N)
                    fi = 2 * cb + ag
                    t = CT[:, fi // 3, fi % 3, :]
                    tres = ("ct", fi // 3, fi % 3)
                    tk.op("act", lambda e, bk=bk, t=t, uc=uc: e.activation(out=t[:, 0:N], in_=bk[:, 0:N], func=AF.Identity,
                          scale=fcw[:, l, 2, uc:uc + 1], bias=fcb[:, l, uc:uc + 1]), reads=[bres, "fcw", "fcb"], writes=[tres])
                    info.append((uc, bk, bres, t, tres))
                    tt.append((t, tres))
                for sg in segs:
                    c0, T, st = sg["c0"], sg["T"], sg["stream"]
                    for stepi in range(6):
                        for (uc, bk, bres, t, tres) in info:
                            hh = histF[:, st, l, uc, :]
                            if stepi == 0:
                                tk.op("dve", lambda e, bk=bk, t=t, uc=uc, c0=c0, T=T: e.scalar_tensor_tensor(out=t[:, c0 + 1:c0 + T], in0=bk[:, c0:c0 + T - 1],
                                      scalar=fcw[:, l, 1, uc:uc + 1], in1=t[:, c0 + 1:c0 + T], op0=ALU.mult, op1=ALU.add),
                                      reads=[bres, tres, "fcw"], writes=[tres])
                            elif stepi == 1:
                                tk.op("dve", lambda e, bk=bk, t=t, uc=uc, c0=c0, T=T: e.scalar_tensor_tensor(out=t[:, c0 + 2:c0 + T], in0=bk[:, c0:c0 + T - 2],
                                      scalar=fcw[:, l, 0, uc:uc + 1], in1=t[:, c0 + 2:c0 + T], op0=ALU.mult, op1=ALU.add),
                                      reads=[bres, tres, "fcw"], writes=[tres])
                            elif stepi == 2:
                                tk.op("dve", lambda e, t=t, uc=uc, c0=c0, hh=hh: e.scalar_tensor_tensor(out=t[:, c0:c0 + 1], in0=hh[:, 1:2],
                                      scalar=fcw[:, l, 1, uc:uc + 1], in1=t[:, c0:c0 + 1], op0=ALU.mult, op1=ALU.add),
                                      reads=[("hF", st, l, uc), tres, "fcw", "histF"], writes=[tres])
                            elif stepi == 3:
                                tk.op("dve", lambda e, t=t, uc=uc, c0=c0, hh=hh: e.scalar_tensor_tensor(out=t[:, c0:c0 + 2], in0=hh[:, 0:2],
                                      scalar=fcw[:, l, 0, uc:uc + 1], in1=t[:, c0:c0 + 2], op0=ALU.mult, op1=ALU.add),
                                      reads=[("hF", st, l, uc), tres, "fcw", "histF"], writes=[tres])
                            elif stepi == 4:
                                tk.op("dve", lambda e, bk=bk, c0=c0, T=T, hh=hh: e.tensor_copy(out=hh, in_=bk[:, c0 + T - 2:c0 + T]),
                                      reads=[bres, "histF"], writes=[("hF", st, l, uc)])
                            elif sg.get("flag_after"):
                                tk.op("dve", lambda e, hh=hh: e.tensor_scalar(out=hh, in0=hh, scalar1=flag[:, 0:1], scalar2=None, op0=ALU.mult),
                                      reads=[("hF", st, l, uc), "flag"], writes=[("hF", st, l, uc)])
                (ta, ra), (tg, rg) = tt
                tk.op("act", lambda e, tg=tg: e.activation(out=tg[:, 0:N], in_=tg[:, 0:N], func=AF.Silu), reads=[rg], writes=[rg])
                tk.op("pool" if cur_pass[0] >= 3 else "dve", lambda e, ta=ta, tg=tg, hb=hb, jj=jj: e.tensor_tensor(out=hid[:, hb, jj, 0:N], in0=ta[:, 0:N], in1=tg[:, 0:N], op=ALU.mult),
                      reads=[ra, rg], writes=[("mix", hb * 11 + jj)])

        def down(q):
            hb = q % 2
            for oc in range(16):
                bk, bres = std_group(S[("dn", l)][q][oc], 11, lambda k: hid[:, hb, k, 0:N], [("mix", hb * 11 + k) for k in range(11)], N)
                tk.op("dve", lambda e, bk=bk, oc=oc: e.tensor_tensor(out=xT[:, oc, 0:N], in0=bk[:, 0:N], in1=xT[:, oc, 0:N], op=ALU.add),
                      reads=[bres, ("xT", oc)], writes=[("xT", oc)])
                if q == 3 and feed is not None:
                    feed.feed(oc)
        up(0); up(1); down(0); up(2); down(1); up(3); down(2); down(3)

    def hist_out(hist_ap, nch, dram_rows, res):
        for r in range(2):
            bi, bk, bres = next_bank()
            tk.op("pe", lambda e, bk=bk, r=r: e.transpose(bk[0:nch, 0:P], hist_ap[:, :, r], ident[:]),
                  reads=list(res) + ["ident", "histA", "histF"], writes=[bres])
            stg = XS[0:nch, r, 0:P]
            tk.op("act", lambda e, bk=bk, stg=stg: e.activation(out=stg, in_=bk[0:nch, 0:P], func=AF.Identity),
                  reads=[bres], writes=[("XS", r)])
            io_dma(dram_rows[r].rearrange("(c f) -> c f", f=P), stg, reads=[("XS", r)], writes=[("hout", id(dram_rows), r)])

    def hF_res(st):
        return [("hF", st, l, uc) for l in range(2) for uc in range(NUP)]

    for p in range(n_pass):
        cur_pass[0] = p
        segs, N = segs_of_pass(p)
        if p > 0:
            for sg in segs:
                load_tokens(sg["src"], sg["T"], xT, sg["c0"], "xT")
        nm = Norm(xT, "xT", g_mix[:, 0, :], N)
        for k in range(KD):
            nm.feed(k)
        for l in range(2):
            if p == n_pass - 1:
                prep_sample_kv(l)
            nm.finish()
            q_proj(S["qA"] if l == 0 else S["qB"], N)
            agen = attention_gen(l, segs)

            def step(agen=agen):
                next(agen, None)
            if l == 0:
                mixer_a(segs, N, step)
            else:
                mixer_b(segs, N, step)
            for _ in agen:
                pass
            nm = Norm(xT, "xT", g_ffn[:, l, :], N)
            out_proj(l, N, feed=nm)
            nm.finish()
            nm = Norm(xT, "xT", g_mix[:, 1, :], N) if l == 0 else Norm(xT, "xT", g_fin, N, inplace=True)
            ffn(l, segs, N, feed=nm)
        nm.finish()
        for sg in segs:
            if sg["dst"] is not None:
                store_tokens(xT, sg["c0"], sg["T"], sg["dst"], "xT")
    hist_out(histA[:, 1, :, :], KD, ca_s, [])
    for l in range(2):
        hist_out(histF[:, 1, l, :, :], NUP, fc_s[l], hF_res(1))
    hist_out(histA[:, 0, :, :], KD, ca_p, [])
    for l in range(2):
        hist_out(histF[:, 0, l, :, :], NUP, fc_p[l], hF_res(0))

    tk._need("sp", tk.final_tokens())
    tk.emit()
    return nc


_W_NAMES = ["norm_mix_g", "norm_mem_g", "w_mem_k", "w_mem_v", "w_in_a", "conv_a_w", "w_in_b", "gmlp_norm_g",
            "gmlp_norm_b", "gmlp_ws", "gmlp_bias", "w_out", "norm_ffn_g", "w_up", "ffn_conv_w", "ffn_conv_b",
            "w_down", "norm_final_g"]


def run(inputs, n_cores):
    x_prompt = np.asarray(inputs["x_prompt"], np.float32)
    x_sample = np.asarray(inputs["x_sample"], np.float32)
    B, SEQ, _ = x_prompt.shape
    DB = x_sample.shape[0]
    assert DB == n_cores and n_cores % B == 0
    cpb = n_cores // B
    per = SEQ // cpb
    n_tiles = per // TMAX
    assert n_tiles * TMAX == per
    nc = build_program(n_tiles)
    wts = {k: np.ascontiguousarray(np.asarray(inputs[k], np.float32)) for k in _W_NAMES}
    in_maps = []
    for c in range(n_cores):
        b, s = c // cpb, c % cpb
        xpc = np.zeros((WARM + per, D), np.float32)
        if s > 0:
            xpc[:] = x_prompt[b, s * per - WARM:(s + 1) * per]
        else:
            xpc[WARM:] = x_prompt[b, 0:per]
        m = dict(wts)
        m["xp"] = xpc
        m["xs"] = np.ascontiguousarray(x_sample[c])
        m["memp"] = np.ascontiguousarray(inputs["mem_prompt"][b], dtype=np.float32)
        m["flag"] = np.full((P, 1), 1.0 if s > 0 else 0.0, np.float32)
        m["cca"] = np.ascontiguousarray(inputs["cache_conv_a"][0, c], dtype=np.float32)
        m["cfc"] = np.ascontiguousarray(inputs["cache_ffn_conv"][:, c], dtype=np.float32)
        m["cmk"] = np.ascontiguousarray(np.asarray(inputs["cache_mem_k"])[:, c].reshape(2, NMEM, DXQ), dtype=np.float32)
        m["cmv"] = np.ascontiguousarray(np.asarray(inputs["cache_mem_v"])[:, c].reshape(2, NMEM, DXQ), dtype=np.float32)
        in_maps.append(m)
    res = run_bass_kernel_spmd(nc, in_maps, core_ids=list(range(n_cores)))
    R = res.results
    y_prompt = np.stack([np.concatenate([R[b * cpb + s]["y_p"] for s in range(cpb)], 0) for b in range(B)])
    y_sample = np.stack([R[c]["y_s"] for c in range(n_cores)])
    last = [b * cpb + cpb - 1 for b in range(B)]
    first = [b * cpb for b in range(B)]
    conv_a_prompt = np.stack([R[c]["ca_p"] for c in last])[None]
    ffn_conv_prompt = np.stack([R[c]["fc_p"] for c in last], 1)
    mem_k_prompt = np.stack([R[c]["mk_p"] for c in first], 1).reshape(2, B, NMEM, 4, 256)
    mem_v_prompt = np.stack([R[c]["mv_p"] for c in first], 1).reshape(2, B, NMEM, 4, 256)
    conv_a_sample = np.stack([R[c]["ca_s"] for c in range(n_cores)])[None]
    ffn_conv_sample = np.stack([R[c]["fc_s"] for c in range(n_cores)], 1)
    gmlp_v_sample = np.stack([R[c]["gv_s"] for c in range(n_cores)])[None]
    outs = (y_prompt, y_sample, conv_a_prompt, ffn_conv_prompt, mem_k_prompt, mem_v_prompt,
            conv_a_sample, ffn_conv_sample, gmlp_v_sample)
    return tuple(np.ascontiguousarray(o, dtype=np.float32) for o in outs)


def kernel(**inputs):
    return run(inputs, 8)
```

```python
import numpy as np
import concourse.bass as bass
import concourse.mybir as mybir
from concourse.bass_utils import run_bass_kernel_spmd
from contextlib import ExitStack

F32 = mybir.dt.float32
BF16 = mybir.dt.bfloat16
AF = mybir.ActivationFunctionType
ALU = mybir.AluOpType

P = 128
D = 2048
KD = 16
DXQ = 1024
NMEM = 256
DFF = 5632
NH = 44
NUP = 88
TMAX = 512
WARM = 256
TS = 32
EPS = 1e-6
NB_RING = 5


class Tracker:
    ENGS = ("pe", "act", "dve", "pool", "sp")

    def __init__(self, nc):
        self.nc = nc
        self.q = {e: [] for e in self.ENGS}
        self.cnt = {e: 0 for e in self.ENGS}
        self.waited = {e: {} for e in self.ENGS}
        self.last_w = {}
        self.reads = {}
        self.sems = {}
        self.sem_keys = [("eng", e) for e in self.ENGS]
        self.dma_rings = {}

    def _need(self, eng, toks):
        need = {}
        for t in toks:
            if t is None:
                continue
            k, v = t
            if eng == "pe" and k == ("eng", "pe"):
                continue
            if need.get(k, 0) < v:
                need[k] = v
        w = self.waited[eng]
        for k, v in need.items():
            if w.get(k, 0) < v:
                w[k] = v
                self.q[eng].append(("wait", k, v))

    def _deps(self, reads, writes):
        toks = []
        for r in reads:
            toks.append(self.last_w.get(r))
        for w in writes:
            toks.append(self.last_w.get(w))
            toks.extend(self.reads.get(w, ()))
        return toks

    def _commit(self, tok, reads, writes):
        for r in reads:
            self.reads.setdefault(r, []).append(tok)
        for w in writes:
            self.last_w[w] = tok
            self.reads[w] = []

    def op(self, eng, fn, reads=(), writes=()):
        self._need(eng, self._deps(reads, writes))
        self.cnt[eng] += 1
        tok = (("eng", eng), self.cnt[eng])
        self.q[eng].append(("op", fn, ("eng", eng), 1))
        self._commit(tok, reads, writes)
        return tok

    def dma_ring(self, name, n):
        keys = [("dma", name, i) for i in range(n)]
        self.sem_keys.extend(keys)
        self.dma_rings[name] = {"keys": keys, "n": 0}

    def dma(self, eng, ring, fn, reads=(), writes=()):
        R = self.dma_rings[ring]
        i = R["n"]
        R["n"] += 1
        k = R["keys"][i % len(R["keys"])]
        prev = i // len(R["keys"])
        toks = self._deps(reads, writes)
        if prev > 0:
            toks.append((k, 16 * prev))
        self._need(eng, toks)
        tok = (k, 16 * (prev + 1))
        self.q[eng].append(("op", fn, k, 16))
        self._commit(tok, reads, writes)
        return tok

    def final_tokens(self):
        toks = list(self.last_w.values())
        for l in self.reads.values():
            toks.extend(l)
        return toks

    def emit(self):
        nc = self.nc
        with ExitStack() as es:
            for k in self.sem_keys:
                self.sems[k] = es.enter_context(nc.semaphore("s_" + "_".join(str(x) for x in k)))
            block = es.enter_context(nc.Block())
            engmap = {"pe": block.tensor, "act": block.scalar, "dve": block.vector,
                      "pool": block.gpsimd, "sp": block.sync}
            for e in self.ENGS:
                items = self.q[e]

                def body(engine, items=items):
                    for it in items:
                        if it[0] == "wait":
                            engine.wait_ge(self.sems[it[1]], it[2])
                        else:
                            it[1](engine).then_inc(self.sems[it[2]], it[3])
                engmap[e](body)


def build_program(n_tiles):
    nc = bass.Bass("TRN2", target_bir_lowering=False)
    tk = Tracker(nc)
    tk.dma_ring("wr", NB_RING)
    tk.dma_ring("cast", 8)
    tk.dma_ring("io", 8)
    tk.dma_ring("io2", 4)
    tk.dma_ring("wb", 6)
    NP = n_tiles * TMAX

    def din(name, shape):
        return nc.dram_tensor(name, list(shape), F32, kind="ExternalInput").ap()

    def dout(name, shape):
        return nc.dram_tensor(name, list(shape), F32, kind="ExternalOutput").ap()

    xp = din("xp", [WARM + NP, D]); xs = din("xs", [TS, D]); memp = din("memp", [NMEM, D])
    flag_d = din("flag", [P, 1])
    cca = din("cca", [2, D]); cfc = din("cfc", [2, 2, 2 * DFF])
    cmk = din("cmk", [2, NMEM, DXQ]); cmv = din("cmv", [2, NMEM, DXQ])
    norm_mix_g = din("norm_mix_g", [2, D]); norm_mem_g = din("norm_mem_g", [2, D])
    w_mem_k = din("w_mem_k", [2, D, DXQ]); w_mem_v = din("w_mem_v", [2, D, DXQ])
    w_in_a = din("w_in_a", [1, D, 3 * D + DXQ]); conv_a_w = din("conv_a_w", [1, 3, D])
    w_in_b = din("w_in_b", [1, D, 2 * D + DXQ])
    gmlp_norm_g = din("gmlp_norm_g", [1, D]); gmlp_norm_b = din("gmlp_norm_b", [1, D])
    gmlp_ws = din("gmlp_ws", [1, 8, P, P]); gmlp_bias = din("gmlp_bias", [1, 8, P])
    w_out = din("w_out", [2, D + DXQ, D]); norm_ffn_g = din("norm_ffn_g", [2, D])
    w_up = din("w_up", [2, D, 2 * DFF]); ffn_conv_w = din("ffn_conv_w", [2, 3, 2 * DFF])
    ffn_conv_b = din("ffn_conv_b", [2, 2 * DFF]); w_down = din("w_down", [2, DFF, D])
    norm_final_g = din("norm_final_g", [D])

    y_p = dout("y_p", [NP, D]); y_s = dout("y_s", [TS, D])
    ca_p = dout("ca_p", [2, D]); fc_p = dout("fc_p", [2, 2, 2 * DFF])
    mk_p = dout("mk_p", [2, NMEM, DXQ]); mv_p = dout("mv_p", [2, NMEM, DXQ])
    ca_s = dout("ca_s", [2, D]); fc_s = dout("fc_s", [2, 2, 2 * DFF])
    gv_s = dout("gv_s", [TS, D])

    slot_specs = []

    def add_slot(kind, W, k0, nk, c0):
        slot_specs.append((kind, W, k0, nk, c0))
        return len(slot_specs) - 1

    S = {}
    for l in range(2):
        S[("kT", l)] = [add_slot("std", w_mem_k[l], 0, 16, dc * P) for dc in range(8)]
        for nm, W in (("ktok", w_mem_k[l]), ("vtok", w_mem_v[l])):
            S[(nm, l)] = [[add_slot("wide", W, 4 * s, 4, cg * 512) for s in range(4)] for cg in range(2)]
    S["inA"] = [[add_slot("std", w_in_a[0], 0, 16, D + j * P), add_slot("std", w_in_a[0], 0, 16, 2 * D + j * P),
                 add_slot("std", w_in_a[0], 0, 16, j * P)] for j in range(16)]
    S["qA"] = [add_slot("std", w_in_a[0], 0, 16, 3 * D + qc * P) for qc in range(8)]
    S["uB"] = [add_slot("std", w_in_b[0], 0, 16, j * P) for j in range(16)]
    S["vB"] = [[add_slot("wide", w_in_b[0], 4 * s, 4, D + fg * 512) for s in range(4)] for fg in range(4)]
    S["qB"] = [add_slot("std", w_in_b[0], 0, 16, 2 * D + qc * P) for qc in range(8)]
    for l in range(2):
        S[("o1", l)] = [add_slot("std", w_out[l], 0, 16, oc * P) for oc in range(16)]
        S[("o2", l)] = [add_slot("std", w_out[l], 16, 8, oc * P) for oc in range(16)]
        S[("up", l)] = [[add_slot("std", w_up[l], 0, 16, j * P), add_slot("std", w_up[l], 0, 16, DFF + j * P)]
                        for j in range(NH)]
        S[("dn", l)] = [[add_slot("std", w_down[l], 11 * q, 11, oc * P) for oc in range(16)] for q in range(4)]
    NSLOT = len(slot_specs)
    wsc = nc.dram_tensor("wsc", [NSLOT, P, 2048], BF16).ap()

    def sb(name, shape, dt=F32):
        return nc.alloc_sbuf_tensor("sb_" + name, list(shape), dt)

    xT = sb("xT", [P, KD, TMAX])
    xn = sb("xn", [P, KD, TMAX], BF16)
    R1 = sb("R1", [P, 32 * TMAX], BF16)
    mixin = R1[:].rearrange("p (k t) -> p k t", t=TMAX)
    hid = R1[:, 0:22 * TMAX].rearrange("p (b k t) -> p b k t", b=2, k=11)
    memT = R1[:, 0:16 * 2 * NMEM].bitcast(F32).rearrange("p (k t) -> p k t", t=NMEM)
    KTp = sb("KTp", [P, 2, 8, NMEM], BF16); Vp = sb("Vp", [P, 2, 2, DXQ], BF16)
    KTs = sb("KTs", [P, 8, NMEM], BF16); Vs = sb("Vs", [P, 2, DXQ], BF16)
    Eb = sb("Eb", [P, 2, 2, TMAX], BF16)
    XS = sb("XS", [P, 2, D])
    vn = sb("vn", [P, 4, D], BF16)
    gb = sb("gb", [P, 2, D])
    wring = sb("wring", [P, NB_RING, 2048], BF16)
    CT = sb("CT", [P, 2, 3, 520])
    rstd = sb("rstd", [P, TMAX]); rden = sb("rden", [P, TMAX])
    histA = sb("histA", [P, 2, 16, 2]); histF = sb("histF", [P, 2, 2, NUP, 2])
    ident = sb("ident", [P, P]); ones_f = sb("ones_f", [P, P]); ones_b = sb("ones_b", [P, P], BF16)
    wmT = sb("wmT", [P, 8, P], BF16); biasbc = sb("biasbc", [P, 8, P])
    g_mix = sb("g_mix", [P, 2, KD]); g_ffn = sb("g_ffn", [P, 2, KD]); g_mem = sb("g_mem", [P, 2, KD])
    g_fin = sb("g_fin", [P, KD])
    caw = sb("caw", [P, 3, KD]); fcw = sb("fcw", [P, 2, 3, NUP]); fcb = sb("fcb", [P, 2, NUP])
    flag = sb("flag", [P, 1]); epsb = sb("epsb", [P, 1])
    stt = sb("stt", [P, 2, 4, 6]); mv = sb("mv", [P, 2, 2]); lnr = sb("lnr", [P, 2, 2])
    banks = [nc.alloc_psum_tensor(f"bank{i}", [P, TMAX], F32) for i in range(8)]
    bank_ctr = [0]

    def next_bank():
        b = bank_ctr[0] % 7
        bank_ctr[0] += 1
        return b, banks[b], ("bank", b)

    ring_ctr = [0]
    cast_done = set()
    cur_pass = [-1]
    n_pass = n_tiles + 1
    wb_pass = {}
    for l in range(2):
        for sid in S[("kT", l)] + [x for nm in ("ktok", "vtok") for row in S[(nm, l)] for x in row]:
            wb_pass[sid] = None
        ffn_sids = [x for pair in S[("up", l)] for x in pair] + [x for row in S[("dn", l)] for x in row]
        for i, sid in enumerate(ffn_sids):
            wb_pass[sid] = i % 3

    def load_slot(sid):
        b = ring_ctr[0] % NB_RING
        ring_ctr[0] += 1
        kind, W, k0, nk, c0 = slot_specs[sid]
        cw = 128 if kind == "std" else 512
        ncol = nk * cw
        if sid not in cast_done:
            src = W[k0 * P:(k0 + nk) * P, c0:c0 + cw].rearrange("(k p) c -> p k c", p=P)
            dst = wring[:, b, 0:ncol].rearrange("p (k c) -> p k c", c=cw)
            tk.dma("pool", "cast", lambda e, dst=dst, src=src: e.dma_start(out=dst, in_=src), writes=[("wr", b)])
            wp = wb_pass.get(sid, 0)
            if wp is not None and cur_pass[0] >= wp and cur_pass[0] < n_pass - 1:
                cast_done.add(sid)
                tk.dma("sp", "wb", lambda e, b=b, sid=sid, ncol=ncol: e.dma_start(out=wsc[sid][:, 0:ncol], in_=wring[:, b, 0:ncol]),
                       reads=[("wr", b)], writes=[("wsc", sid)])
        else:
            tk.dma("sp", "wr", lambda e, b=b, sid=sid, ncol=ncol: e.dma_start(out=wring[:, b, 0:ncol], in_=wsc[sid][:, 0:ncol]),
                   reads=[("wsc", sid)], writes=[("wr", b)])
        return b

    def emit_cast(sid):
        kind, W, k0, nk, c0 = slot_specs[sid]
        cw = 128 if kind == "std" else 512
        src = W[k0 * P:(k0 + nk) * P, c0:c0 + cw].rearrange("(k p) c -> p k c", p=P)
        dst = wsc[sid][:, 0:nk * cw].rearrange("p (k c) -> p k c", c=cw)
        tk.dma("pool", "cast", lambda e: e.dma_start(out=dst, in_=src), writes=[("wsc", sid)])

    def io_dma(out_ap, in_ap, reads=(), writes=(), eng="pool"):
        tk.dma(eng, "io" if eng == "pool" else "io2", lambda e: e.dma_start(out=out_ap, in_=in_ap), reads=reads, writes=writes)

    def std_group(sid, nk, rhs_fn, rhs_res, N):
        b = load_slot(sid)
        bi, bk, bres = next_bank()
        w3 = wring[:, b, :].rearrange("p (k c) -> p k c", c=P)

        def fn(e):
            for k in range(nk):
                ins = e.matmul(bk[:, 0:N], lhsT=w3[:, k, :], rhs=rhs_fn(k), start=(k == 0), stop=(k == nk - 1))
            return ins
        tk.op("pe", fn, reads=[("wr", b)] + list(rhs_res), writes=[bres])
        return bk, bres

    def xn_res():
        return [("xn", k) for k in range(KD)]

    def std_groups_il(sids, N):
        bs = [load_slot(sid) for sid in sids]
        bks = [next_bank() for _ in sids]
        for k in range(KD):
            for b, (bi, bk, bres) in zip(bs, bks):
                w3 = wring[:, b, :].rearrange("p (k c) -> p k c", c=P)
                tk.op("pe", lambda e, bk=bk, w3=w3, k=k: e.matmul(bk[:, 0:N], lhsT=w3[:, k, :], rhs=xn[:, k, 0:N],
                      start=(k == 0), stop=(k == KD - 1)), reads=[("wr", b), ("xn", k)], writes=[bres])
        return [(bk, bres) for (bi, bk, bres) in bks]

    tk.op("pool", lambda e: e.memset(ones_f[:], 1.0), writes=["ones_f"])
    tk.op("pool", lambda e: e.affine_select(out=ident[:], in_=ones_f[:], pattern=[[1, P]], compare_op=ALU.is_equal,
                                            fill=0.0, base=0, channel_multiplier=-1), reads=["ones_f"], writes=["ident"])
    tk.op("dve", lambda e: e.tensor_copy(out=ones_b[:], in_=ones_f[:]), reads=["ones_f"], writes=["ones_b"])
    tk.op("dve", lambda e: e.memset(epsb[:], EPS), writes=["epsb"])
    tk.op("dve", lambda e: e.memset(histA[:].rearrange("p a b c -> p (a b c)"), 0.0), writes=["histA"])
    tk.op("dve", lambda e: e.memset(histF[:].rearrange("p a b c d -> p (a b c d)"), 0.0), writes=["histF"])
    io_dma(flag[:], flag_d, writes=["flag"])
    io_dma(gb[:, 0, :], gmlp_norm_g[0].partition_broadcast(P), writes=["gb"])
    io_dma(gb[:, 1, :], gmlp_norm_b[0].partition_broadcast(P), writes=["gb"])
    io_dma(biasbc[:].rearrange("p g t -> p (g t)"), gmlp_bias[0].rearrange("g t -> (g t)").partition_broadcast(P),
           writes=["biasbc"])

    def rows_to_fm(dst_ap, rows_ap, nch, dst_res, stage_col):
        sl_ = stage_col % 6
        sres = ("ct", sl_ // 3, sl_ % 3)
        stg = CT[0:nch, sl_ // 3, sl_ % 3, 0:P]
        io_dma(stg, rows_ap.rearrange("(c f) -> c f", f=P), writes=[sres])
        bi, bk, bres = next_bank()
        tk.op("pe", lambda e: e.transpose(bk[:, 0:nch], stg, ident[0:nch, 0:nch]),
              reads=[sres, "ident"], writes=[bres])
        tk.op("act", lambda e: e.activation(out=dst_ap, in_=bk[:, 0:nch], func=AF.Identity),
              reads=[bres], writes=[dst_res])

    col = [0]

    def vec_load(dst_ap, rows_ap, nch, dst_res):
        rows_to_fm(dst_ap, rows_ap, nch, dst_res, col[0])
        col[0] += 1

    for l in range(2):
        vec_load(g_mix[:, l, :], norm_mix_g[l], KD, "gvec")
        vec_load(g_ffn[:, l, :], norm_ffn_g[l], KD, "gvec")
        vec_load(g_mem[:, l, :], norm_mem_g[l], KD, "gvec")
        vec_load(fcb[:, l, :], ffn_conv_b[l], NUP, "fcb")
        for t in range(3):
            vec_load(fcw[:, l, t, :], ffn_conv_w[l, t], NUP, "fcw")
        for r in range(2):
            vec_load(histF[:, 1, l, :, r], cfc[l, r], NUP, "histF")
    vec_load(g_fin[:, :], norm_final_g, KD, "gvec")
    for t in range(3):
        vec_load(caw[:, t, :], conv_a_w[0, t], KD, "caw")
    for r in range(2):
        vec_load(histA[:, 1, :, r], cca[r], KD, "histA")

    for g in range(8):
        stg = XS[:, 1, (g % 8) * P:(g % 8 + 1) * P]
        io_dma(stg, gmlp_ws[0, g], writes=[("XS", 1)])
        bi, bk, bres = next_bank()
        tk.op("pe", lambda e, bk=bk, stg=stg: e.transpose(bk[:, 0:P], stg, ident[:]),
              reads=[("XS", 1), "ident"], writes=[bres])
        tmp = CT[:, 0, 0, 0:P]
        tk.op("act", lambda e, bk=bk, tmp=tmp: e.activation(out=tmp, in_=bk[:, 0:P], func=AF.Identity),
              reads=[bres], writes=[("ct", 0, 0)])
        tmp2 = CT[:, 0, 1, 0:P]
        tk.op("pool", lambda e, tmp=tmp, tmp2=tmp2: e.affine_select(out=tmp2, in_=tmp, pattern=[[1, P]],
              compare_op=ALU.is_ge, fill=0.0, base=0, channel_multiplier=-1), reads=[("ct", 0, 0)], writes=[("ct", 0, 1)])
        tk.op("dve", lambda e, g=g, tmp2=tmp2: e.tensor_copy(out=wmT[:, g, :], in_=tmp2), reads=[("ct", 0, 1)], writes=["wmT"])

    def load_tokens(src_rows_ap, nrows, dstT, c0, dst_name):
        nblk = (nrows + P - 1) // P
        for blk in range(nblk):
            nr = min(P, nrows - blk * P)
            bi_ = blk % 2
            io_dma(XS[0:nr, bi_, :], src_rows_ap[blk * P:blk * P + nr, :], writes=[("XS", bi_)])
            for g4 in range(4):
                bi, bk, bres = next_bank()

                def fn(e, bk=bk, bi_=bi_, nr=nr, g4=g4):
                    for q in range(4):
                        ins = e.transpose(bk[:, q * P:q * P + nr], XS[0:nr, bi_, (4 * g4 + q) * P:(4 * g4 + q + 1) * P],
                                          ident[0:nr, 0:nr])
                    return ins
                tk.op("pe", fn, reads=[("XS", bi_), "ident"], writes=[bres])
                src = bk[:].rearrange("p (q t) -> p q t", q=4)[:, :, 0:nr]
                dst = dstT[:, 4 * g4:4 * g4 + 4, c0 + blk * P:c0 + blk * P + nr]
                eng = "act" if g4 % 2 == 0 else "dve"
                if eng == "act":
                    tk.op("act", lambda e, src=src, dst=dst: e.activation(out=dst, in_=src, func=AF.Identity),
                          reads=[bres], writes=[(dst_name, 4 * g4 + q) for q in range(4)])
                else:
                    tk.op("dve", lambda e, src=src, dst=dst: e.tensor_copy(out=dst, in_=src),
                          reads=[bres], writes=[(dst_name, 4 * g4 + q) for q in range(4)])

    def store_tokens(srcT, c0, nrows, dst_rows_ap, src_name):
        nblk = (nrows + P - 1) // P
        for blk in range(nblk):
            nr = min(P, nrows - blk * P)
            bi_ = blk % 2
            for g4 in range(4):
                bi, bk, bres = next_bank()

                def fn(e, bk=bk, nr=nr, g4=g4, blk=blk):
                    for q in range(4):
                        ins = e.transpose(bk[0:nr, q * P:(q + 1) * P],
                                          srcT[:, 4 * g4 + q, c0 + blk * P:c0 + blk * P + nr], ident[:])
                    return ins
                tk.op("pe", fn, reads=[(src_name, 4 * g4 + q) for q in range(4)] + ["ident"], writes=[bres])
                dst = XS[0:nr, bi_, g4 * 512:(g4 + 1) * 512]
                if g4 % 2 == 0:
                    tk.op("act", lambda e, bk=bk, dst=dst, nr=nr: e.activation(out=dst, in_=bk[0:nr, :], func=AF.Identity),
                          reads=[bres], writes=[("XS", bi_)])
                else:
                    tk.op("dve", lambda e, bk=bk, dst=dst, nr=nr: e.tensor_copy(out=dst, in_=bk[0:nr, :]),
                          reads=[bres], writes=[("XS", bi_)])
            io_dma(dst_rows_ap[blk * P:blk * P + nr, :], XS[0:nr, bi_, :], reads=[("XS", bi_)], writes=[("out", id(dst_rows_ap), blk)])

    class Norm:
        def __init__(self, srcT, src_name, gvec, N, inplace=False):
            self.srcT, self.src_name, self.gvec, self.N, self.inplace = srcT, src_name, gvec, N, inplace
            self.pending = None
            self.n_mm = 0

        def _mm(self, k):
            N = self.N
            first, last = self.n_mm == 0, self.n_mm == KD - 1
            self.n_mm += 1
            bk = banks[7]
            tk.op("pe", lambda e: e.matmul(bk[:, 0:N], lhsT=ones_b[:], rhs=xn[:, k, 0:N], start=first, stop=last),
                  reads=[("xn", k), "ones_b"], writes=[("bank", 7)])

        def feed(self, k):
            N, srcT = self.N, self.srcT
            tk.op("act", lambda e: e.activation(out=xn[:, k, 0:N], in_=srcT[:, k, 0:N], func=AF.Square),
                  reads=[(self.src_name, k)], writes=[("xn", k)])
            if self.pending is not None:
                self._mm(self.pending)
            self.pending = k

        def finish(self):
            N, srcT, gvec, src_name = self.N, self.srcT, self.gvec, self.src_name
            self._mm(self.pending)
            assert self.n_mm == KD
            bk = banks[7]
            tk.op("act", lambda e: e.activation(out=rstd[:, 0:N], in_=bk[:, 0:N], func=AF.Sqrt, scale=1.0 / D, bias=epsb[:]),
                  reads=[("bank", 7), "epsb"], writes=["rstd"])
            tk.op("dve", lambda e: e.reciprocal(out=rstd[:, 0:N], in_=rstd[:, 0:N]), reads=["rstd"], writes=["rstd"])
            for k in range(KD):
                if self.inplace:
                    tk.op("dve", lambda e, k=k: e.scalar_tensor_tensor(out=srcT[:, k, 0:N], in0=srcT[:, k, 0:N],
                          scalar=gvec[:, k:k + 1], in1=rstd[:, 0:N], op0=ALU.mult, op1=ALU.mult),
                          reads=[(src_name, k), "rstd", "gvec"], writes=[(src_name, k)])
                else:
                    tk.op("dve", lambda e, k=k: e.scalar_tensor_tensor(out=xn[:, k, 0:N], in0=srcT[:, k, 0:N],
                          scalar=gvec[:, k:k + 1], in1=rstd[:, 0:N], op0=ALU.mult, op1=ALU.mult),
                          reads=[(src_name, k), "rstd", "gvec"], writes=[("xn", k)])

    def rmsnorm(srcT, src_name, gvec, N, out_fp32_inplace=False):
        nm = Norm(srcT, src_name, gvec, N, out_fp32_inplace)
        for k in range(KD):
            nm.feed(k)
        nm.finish()

    load_tokens(memp, NMEM, memT, 0, "memT")

    HALF = TMAX // 2

    def segs_of_pass(p):
        if p == 0:
            return [dict(c0=0, T=WARM, stream=0, src=xp[0:WARM, :], dst=None, flag_after=True),
                    dict(c0=WARM, T=HALF, stream=0, src=xp[WARM:WARM + HALF, :], dst=y_p[0:HALF, :])], WARM + HALF
        if p == n_pass - 1:
            return [dict(c0=0, T=HALF, stream=0, src=xp[WARM + NP - HALF:WARM + NP, :], dst=y_p[NP - HALF:NP, :]),
                    dict(c0=HALF, T=TS, stream=1, src=xs, dst=y_s)], HALF + TS
        y0 = HALF + (p - 1) * TMAX
        return [dict(c0=0, T=TMAX, stream=0, src=xp[WARM + y0:WARM + y0 + TMAX, :], dst=y_p[y0:y0 + TMAX, :])], TMAX

    segs0, N0 = segs_of_pass(0)
    for sg in segs0:
        load_tokens(sg["src"], sg["T"], xT, sg["c0"], "xT")


    for l in range(2):
        rmsnorm(memT, "memT", g_mem[:, l, :], NMEM)
        for dc in range(8):
            bk, bres = std_group(S[("kT", l)][dc], 16, lambda k: xn[:, k, 0:NMEM], xn_res(), NMEM)
            tk.op("act", lambda e, bk=bk, l=l, dc=dc: e.activation(out=KTp[:, l, dc, :], in_=bk[:, 0:NMEM], func=AF.Identity),
                  reads=[bres], writes=[("KTp", l)])
        for wi, (nm, dstd) in enumerate((("ktok", mk_p), ("vtok", mv_p))):
            for cg in range(2):
                bks = [next_bank() for _ in range(2)]
                for s in range(4):
                    b = load_slot(S[(nm, l)][cg][s])
                    w3 = wring[:, b, :].rearrange("p (k c) -> p k c", c=512)

                    def fn(e, s=s, w3=w3, bks=bks):
                        for mb in range(2):
                            for kk in range(4):
                                ins = e.matmul(bks[mb][1][:, 0:512], lhsT=xn[:, 4 * s + kk, mb * P:(mb + 1) * P],
                                               rhs=w3[:, kk, :], start=(s == 0 and kk == 0), stop=(s == 3 and kk == 3))
                        return ins
                    tk.op("pe", fn, reads=[("wr", b)] + xn_res(), writes=[bks[0][2], bks[1][2]])
                for mb in range(2):
                    dst = XS[:, mb, wi * DXQ + cg * 512:wi * DXQ + (cg + 1) * 512]
                    tk.op("act", lambda e, dst=dst, bk=bks[mb][1]: e.activation(out=dst, in_=bk[:, 0:512], func=AF.Identity),
                          reads=[bks[mb][2]], writes=[("XS", mb)])
        for mb in range(2):
            tk.op("dve", lambda e, l=l, mb=mb: e.tensor_copy(out=Vp[:, l, mb, :], in_=XS[:, mb, DXQ:2 * DXQ]),
                  reads=[("XS", mb)], writes=[("Vp", l)])
            io_dma(mk_p[l, mb * P:(mb + 1) * P, :], XS[:, mb, 0:DXQ], reads=[("XS", mb)], writes=[("mk_p", l, mb)], eng="sp")
            io_dma(mv_p[l, mb * P:(mb + 1) * P, :], XS[:, mb, DXQ:2 * DXQ], reads=[("XS", mb)], writes=[("mv_p", l, mb)], eng="sp")

    def prep_sample_kv(l):
        for mb in range(2):
            io_dma(XS[:, mb, 0:DXQ], cmk[l, mb * P:(mb + 1) * P, :], writes=[("XS", mb)], eng="sp")
            io_dma(XS[:, mb, DXQ:2 * DXQ], cmv[l, mb * P:(mb + 1) * P, :], writes=[("XS", mb)], eng="sp")
            tk.op("dve", lambda e, mb=mb: e.tensor_copy(out=Vs[:, mb, :], in_=XS[:, mb, DXQ:2 * DXQ]),
                  reads=[("XS", mb)], writes=["Vs"])
        for d2 in range(4):
            bi, bk, bres = next_bank()

            def fn(e, bk=bk, d2=d2):
                for dd in range(2):
                    for mb in range(2):
                        ins = e.transpose(bk[:, dd * NMEM + mb * P:dd * NMEM + (mb + 1) * P],
                                          XS[:, mb, (2 * d2 + dd) * P:(2 * d2 + dd + 1) * P], ident[:])
                return ins
            tk.op("pe", fn, reads=[("XS", 0), ("XS", 1), "ident"], writes=[bres])
            tk.op("act", lambda e, bk=bk, d2=d2: e.activation(out=KTs[:, 2 * d2:2 * d2 + 2, :],
                  in_=bk[:].rearrange("p (a m) -> p a m", a=2), func=AF.Identity), reads=[bres], writes=["KTs"])

    def mix_res(ks):
        return [("mix", k) for k in ks]

    def attention_gen(l, segs):
        for sg in segs:
            c0, T, st = sg["c0"], sg["T"], sg["stream"]
            for h in range(4):
                eb = h % 2
                for mc in range(2):
                    bi, bk, bres = next_bank()

                    def fn(e, bk=bk, mc=mc, h=h, st=st, c0=c0, T=T):
                        for dc in range(2):
                            kt = KTp[:, l, 2 * h + dc, mc * P:(mc + 1) * P] if st == 0 else KTs[:, 2 * h + dc, mc * P:(mc + 1) * P]
                            ins = e.matmul(bk[:, 0:T], lhsT=kt, rhs=mixin[:, 24 + 2 * h + dc, c0:c0 + T],
                                           start=(dc == 0), stop=(dc == 1))
                        return ins
                    tk.op("pe", fn, reads=mix_res([24 + 2 * h, 25 + 2 * h]) + [("KTp", l), "KTs"], writes=[bres])
                    tk.op("act", lambda e, bk=bk, mc=mc, eb=eb, T=T: e.activation(out=Eb[:, eb, mc, 0:T], in_=bk[:, 0:T], func=AF.Exp),
                          reads=[bres], writes=[("Eb", eb, mc)])
                yield
                bi, bkd, bresd = next_bank()

                def fnd(e, bkd=bkd, eb=eb, T=T):
                    for mc in range(2):
                        ins = e.matmul(bkd[:, 0:T], lhsT=ones_b[:], rhs=Eb[:, eb, mc, 0:T], start=(mc == 0), stop=(mc == 1))
                    return ins
                tk.op("pe", fnd, reads=[("Eb", eb, 0), ("Eb", eb, 1), "ones_b"], writes=[bresd])
                tk.op("dve", lambda e, bkd=bkd, T=T: e.reciprocal(out=rden[:, 0:T], in_=bkd[:, 0:T]), reads=[bresd], writes=["rden"])
                for dc in range(2):
                    bi, bk, bres = next_bank()

                    def fnv(e, bk=bk, dc=dc, h=h, eb=eb, st=st, T=T):
                        for mc in range(2):
                            vv = Vp[:, l, mc, h * 256 + dc * P:h * 256 + (dc + 1) * P] if st == 0 else Vs[:, mc, h * 256 + dc * P:h * 256 + (dc + 1) * P]
                            ins = e.matmul(bk[:, 0:T], lhsT=vv, rhs=Eb[:, eb, mc, 0:T], start=(mc == 0), stop=(mc == 1))
                        return ins
                    tk.op("pe", fnv, reads=[("Eb", eb, 0), ("Eb", eb, 1), ("Vp", l), "Vs"], writes=[bres])
                    tk.op("dve", lambda e, bk=bk, dc=dc, h=h, c0=c0, T=T: e.tensor_tensor(out=mixin[:, 16 + 2 * h + dc, c0:c0 + T], in0=bk[:, 0:T],
                          in1=rden[:, 0:T], op=ALU.mult), reads=[bres, "rden"], writes=[("mix", 16 + 2 * h + dc)])
                yield

    def out_proj(l, N, feed=None):
        for oc in range(16):
            b1 = load_slot(S[("o1", l)][oc])
            b2 = load_slot(S[("o2", l)][oc])
            bi, bk, bres = next_bank()
            wa = wring[:, b1, :].rearrange("p (k c) -> p k c", c=P)
            wb_ = wring[:, b2, :].rearrange("p (k c) -> p k c", c=P)

            def fn(e, bk=bk, wa=wa, wb_=wb_):
                for k in range(16):
                    e.matmul(bk[:, 0:N], lhsT=wa[:, k, :], rhs=mixin[:, k, 0:N], start=(k == 0), stop=False)
                for k in range(8):
                    ins = e.matmul(bk[:, 0:N], lhsT=wb_[:, k, :], rhs=mixin[:, 16 + k, 0:N], start=False, stop=(k == 7))
                return ins
            tk.op("pe", fn, reads=[("wr", b1), ("wr", b2)] + mix_res(range(24)), writes=[bres])
            tk.op("dve", lambda e, bk=bk, oc=oc: e.tensor_tensor(out=xT[:, oc, 0:N], in0=bk[:, 0:N], in1=xT[:, oc, 0:N], op=ALU.add),
                  reads=[bres, ("xT", oc)], writes=[("xT", oc)])
            if feed is not None:
                feed.feed(oc)

    def q_proj(sl, N):
        first = std_groups_il(sl[0:3], N)
        for qc in range(8):
            if qc < 3:
                bk, bres = first[qc]
            else:
                bk, bres = std_group(sl[qc], 16, lambda k: xn[:, k, 0:N], xn_res(), N)
            tk.op("act", lambda e, bk=bk, qc=qc: e.activation(out=mixin[:, 24 + qc, 0:N], in_=bk[:, 0:N], func=AF.Identity, scale=0.0625),
                  reads=[bres], writes=[("mix", 24 + qc)])

    ct_ctr = [0]

    def mixer_a(segs, N, step):
        for j in range(16):
            step()
            sc, sh, sbg = S["inA"][j]
            bkc, brc = std_group(sc, 16, lambda k: xn[:, k, 0:N], xn_res(), N)
            bkh, brh = std_group(sh, 16, lambda k: xn[:, k, 0:N], xn_res(), N)
            bkb, brb = std_group(sbg, 16, lambda k: xn[:, k, 0:N], xn_res(), N)
            cb = ct_ctr[0] % 2
            ct_ctr[0] += 1
            hsb, ch, tcv = CT[:, cb, 0, :], CT[:, cb, 1, :], CT[:, cb, 2, :]
            tk.op("act", lambda e, bkh=bkh, hsb=hsb: e.activation(out=hsb[:, 0:N], in_=bkh[:, 0:N], func=AF.Identity),
                  reads=[brh], writes=[("ct", cb, 0)])
            for si, sg in enumerate(segs):
                c0, T, st = sg["c0"], sg["T"], sg["stream"]
                off = c0 + 2 * si
                tk.op("dve", lambda e, ch=ch, off=off, st=st, j=j: e.tensor_copy(out=ch[:, off:off + 2], in_=histA[:, st, j, :]),
                      reads=["histA"], writes=[("ct", cb, 1)])
                tk.op("dve", lambda e, ch=ch, off=off, bkc=bkc, hsb=hsb, c0=c0, T=T: e.tensor_tensor(out=ch[:, off + 2:off + 2 + T],
                      in0=bkc[:, c0:c0 + T], in1=hsb[:, c0:c0 + T], op=ALU.mult), reads=[brc, ("ct", cb, 0)], writes=[("ct", cb, 1)])
                tk.op("dve", lambda e, ch=ch, off=off, st=st, j=j, T=T: e.tensor_copy(out=histA[:, st, j, :], in_=ch[:, off + T:off + T + 2]),
                      reads=[("ct", cb, 1)], writes=["histA"])
                if sg.get("flag_after"):
                    tk.op("dve", lambda e, st=st, j=j: e.tensor_scalar(out=histA[:, st, j, :], in0=histA[:, st, j, :], scalar1=flag[:, 0:1],
                          scalar2=None, op0=ALU.mult), reads=["histA", "flag"], writes=["histA"])
                tk.op("dve", lambda e, ch=ch, off=off, tcv=tcv, c0=c0, T=T, j=j: e.tensor_scalar(out=tcv[:, c0:c0 + T], in0=ch[:, off + 2:off + 2 + T],
                      scalar1=caw[:, 2, j:j + 1], scalar2=None, op0=ALU.mult), reads=[("ct", cb, 1), "caw"], writes=[("ct", cb, 2)])
                for tap in (1, 0):
                    tk.op("dve", lambda e, ch=ch, off=off, tcv=tcv, c0=c0, T=T, j=j, tap=tap: e.scalar_tensor_tensor(out=tcv[:, c0:c0 + T],
                          in0=ch[:, off + tap:off + tap + T], scalar=caw[:, tap, j:j + 1], in1=tcv[:, c0:c0 + T], op0=ALU.mult, op1=ALU.add),
                          reads=[("ct", cb, 1), ("ct", cb, 2), "caw"], writes=[("ct", cb, 2)])
            tk.op("dve", lambda e, bkb=bkb, tcv=tcv, j=j: e.tensor_tensor(out=mixin[:, j, 0:N], in0=bkb[:, 0:N], in1=tcv[:, 0:N], op=ALU.mult),
                  reads=[brb, ("ct", cb, 2)], writes=[("mix", j)])

    def mixer_b(segs, N, step):
        def u_proj(js):
            for j in js:
                step()
                bk, bres = std_group(S["uB"][j], 16, lambda k: xn[:, k, 0:N], xn_res(), N)
                tk.op("act", lambda e, bk=bk, j=j: e.activation(out=mixin[:, j, 0:N], in_=bk[:, 0:N], func=AF.Gelu),
                      reads=[bres], writes=[("mix", j)])

        blocks = []
        for sg in segs:
            nblk = (sg["T"] + P - 1) // P
            for blk in range(nblk):
                nr = min(P, sg["T"] - blk * P)
                blocks.append(dict(cs=sg["c0"] + blk * P, nr=nr, sample=(sg["stream"] == 1)))
        halves = [blocks[h0:h0 + 2] for h0 in range(0, len(blocks), 2)]
        usplit = [range(0, 8), range(8, 16)] if len(halves) == 2 else [range(0, 16)]

        def v_proj(half, vb):
            for fg in range(4):
                bks = [next_bank() for _ in half]
                for s_ in range(4):
                    b = load_slot(S["vB"][fg][s_])
                    w3 = wring[:, b, :].rearrange("p (k c) -> p k c", c=512)

                    def fn(e, s_=s_, w3=w3, bks=bks, half=half):
                        for i, bl in enumerate(half):
                            for kk in range(4):
                                ins = e.matmul(bks[i][1][0:bl["nr"], 0:512], lhsT=xn[:, 4 * s_ + kk, bl["cs"]:bl["cs"] + bl["nr"]],
                                               rhs=w3[:, kk, :], start=(s_ == 0 and kk == 0), stop=(s_ == 3 and kk == 3))
                        return ins
                    tk.op("pe", fn, reads=[("wr", b)] + xn_res(), writes=[x[2] for x in bks])
                for i, bl in enumerate(half):
                    dst = XS[0:bl["nr"], i, fg * 512:(fg + 1) * 512]
                    tk.op("act", lambda e, dst=dst, bk=bks[i][1], nr=bl["nr"]: e.activation(out=dst, in_=bk[0:nr, 0:512], func=AF.Gelu),
                          reads=[bks[i][2]], writes=[("XS", i)])
            for i, bl in enumerate(half):
                nr = bl["nr"]
                xv = XS[0:nr, i, :]
                st_ = stt[0:nr, i, :, :]
                for c4 in range(4):
                    tk.op("dve", lambda e, c4=c4, nr=nr, i=i, st_=st_: e.bn_stats(out=st_[:, c4, :], in_=XS[0:nr, i, c4 * 512:(c4 + 1) * 512]),
                          reads=[("XS", i)], writes=[("stt", i)])
                tk.op("dve", lambda e, nr=nr, i=i, st_=st_: e.bn_aggr(out=mv[0:nr, i, :], in_=st_.rearrange("p c s -> p (c s)")),
                      reads=[("stt", i)], writes=[("mv", i)])
                tk.op("act", lambda e, nr=nr, i=i: e.activation(out=lnr[0:nr, i, 0:1], in_=mv[0:nr, i, 1:2], func=AF.Sqrt, scale=1.0, bias=epsb[0:nr, :]),
                      reads=[("mv", i), "epsb"], writes=[("lnr", i)])
                tk.op("dve", lambda e, nr=nr, i=i: e.reciprocal(out=lnr[0:nr, i, 0:1], in_=lnr[0:nr, i, 0:1]), reads=[("lnr", i)], writes=[("lnr", i)])
                tk.op("dve", lambda e, nr=nr, i=i: e.tensor_scalar(out=lnr[0:nr, i, 1:2], in0=mv[0:nr, i, 0:1], scalar1=lnr[0:nr, i, 0:1], scalar2=-1.0,
                      op0=ALU.mult, op1=ALU.mult), reads=[("mv", i), ("lnr", i)], writes=[("lnr", i)])
                tk.op("act", lambda e, xv=xv, nr=nr, i=i: e.activation(out=xv, in_=xv, func=AF.Identity, scale=lnr[0:nr, i, 0:1], bias=lnr[0:nr, i, 1:2]),
                      reads=[("XS", i), ("lnr", i)], writes=[("XS", i)])
                ge = "dve" if (i == 0 or cur_pass[0] == 0) else "pool"
                tk.op(ge, lambda e, xv=xv, nr=nr: e.tensor_tensor(out=xv, in0=xv, in1=gb[0:nr, 0, :], op=ALU.mult),
                      reads=[("XS", i), "gb"], writes=[("XS", i)])
                if bl["sample"]:
                    tk.op(ge, lambda e, xv=xv, nr=nr: e.tensor_tensor(out=xv, in0=xv, in1=gb[0:nr, 1, :], op=ALU.add),
                          reads=[("XS", i), "gb"], writes=[("XS", i)])
                    tk.op("act", lambda e, xv=xv, nr=nr, i=i: e.activation(out=vn[0:nr, vb + i, :], in_=xv, func=AF.Identity),
                          reads=[("XS", i)], writes=[("vn", vb + i)])
                    io_dma(gv_s, xv, reads=[("XS", i)], writes=["gv_s"])
                else:
                    tk.op(ge, lambda e, xv=xv, nr=nr, i=i: e.tensor_tensor(out=vn[0:nr, vb + i, :], in0=xv, in1=gb[0:nr, 1, :], op=ALU.add),
                          reads=[("XS", i), "gb"], writes=[("vn", vb + i)])

        def mixing(half, vb):
            hc0 = half[0]["cs"]
            for fc in range(16):
                step()
                g = fc // 2
                bi, bk, bres = next_bank()

                def fnm(e, bk=bk, fc=fc, g=g, half=half, hc0=hc0):
                    for i, bl in enumerate(half):
                        nr = bl["nr"]
                        lc = bl["cs"] - hc0
                        ins = e.matmul(bk[:, lc:lc + nr], lhsT=vn[0:nr, vb + i, fc * P:(fc + 1) * P], rhs=wmT[0:nr, g, 0:nr],
                                       start=True, stop=True)
                    return ins
                tk.op("pe", fnm, reads=[("vn", vb + i) for i in range(len(half))] + ["wmT"], writes=[bres])
                cb = ct_ctr[0] % 6
                ct_ctr[0] += 1
                tmp = CT[:, cb // 3, cb % 3, :]
                hn = half[-1]["cs"] + half[-1]["nr"] - hc0
                if all(bl["nr"] == P for bl in half):
                    nb_ = len(half)
                    tk.op("dve", lambda e, bk=bk, tmp=tmp, g=g, nb_=nb_, hn=hn: e.tensor_tensor(
                          out=tmp[:, 0:hn].rearrange("p (a t) -> p a t", a=nb_), in0=bk[:, 0:hn].rearrange("p (a t) -> p a t", a=nb_),
                          in1=biasbc[:, g, :].unsqueeze(1).to_broadcast([P, nb_, P]), op=ALU.add), reads=[bres, "biasbc"], writes=[("ct", cb // 3, cb % 3)])
                else:
                    for i, bl in enumerate(half):
                        nr = bl["nr"]
                        lc = bl["cs"] - hc0
                        tk.op("dve", lambda e, bk=bk, tmp=tmp, lc=lc, nr=nr, g=g: e.tensor_tensor(out=tmp[:, lc:lc + nr], in0=bk[:, lc:lc + nr],
                              in1=biasbc[:, g, 0:nr], op=ALU.add), reads=[bres, "biasbc"], writes=[("ct", cb // 3, cb % 3)])
                tk.op("pool" if (fc % 2 and cur_pass[0] > 0) else "dve", lambda e, tmp=tmp, fc=fc, hn=hn, hc0=hc0: e.tensor_tensor(out=mixin[:, fc, hc0:hc0 + hn], in0=tmp[:, 0:hn],
                      in1=mixin[:, fc, hc0:hc0 + hn], op=ALU.mult), reads=[("ct", cb // 3, cb % 3), ("mix", fc)], writes=[("mix", fc)])

        assert len(halves) <= 2
        for hi, half in enumerate(halves):
            v_proj(half, 2 * hi)
            u_proj(usplit[hi])
        for hi, half in enumerate(halves):
            mixing(half, 2 * hi)

    def ffn(l, segs, N, feed=None):
        def up(q):
            hb = q % 2
            for jj in range(11):
                j = 11 * q + jj
                cb = ct_ctr[0] % 3
                ct_ctr[0] += 1
                tt = []
                pre = std_groups_il(S[("up", l)][j], N) if j == 0 else None
                info = []
                for ag in range(2):
                    uc = j + ag * NH
                    if pre is not None:
                        bk, bres = pre[ag]
                    else:
                        bk, bres = std_group(S[("up", l)][j][ag], 16, lambda k: xn[:, k, 0:N], xn_res(), N)
                    fi = 2 * cb + ag
                    t = CT[:, fi // 3, fi % 3, :]
                    tres = ("ct", fi // 3, fi % 3)
                    tk.op("act", lambda e, bk=bk, t=t, uc=uc: e.activation(out=t[:, 0:N], in_=bk[:, 0:N], func=AF.Identity,
                          scale=fcw[:, l, 2, uc:uc + 1], bias=fcb[:, l, uc:uc + 1]), reads=[bres, "fcw", "fcb"], writes=[tres])
                    info.append((uc, bk, bres, t, tres))
                    tt.append((t, tres))
                for sg in segs:
                    c0, T, st = sg["c0"], sg["T"], sg["stream"]
                    for stepi in range(6):
                        for (uc, bk, bres, t, tres) in info:
                            hh = histF[:, st, l, uc, :]
                            if stepi == 0:
                                tk.op("dve", lambda e, bk=bk, t=t, uc=uc, c0=c0, T=T: e.scalar_tensor_tensor(out=t[:, c0 + 1:c0 + T], in0=bk[:, c0:c0 + T - 1],
                                      scalar=fcw[:, l, 1, uc:uc + 1], in1=t[:, c0 + 1:c0 + T], op0=ALU.mult, op1=ALU.add),
                                      reads=[bres, tres, "fcw"], writes=[tres])
                            elif stepi == 1:
                                tk.op("dve", lambda e, bk=bk, t=t, uc=uc, c0=c0, T=T: e.scalar_tensor_tensor(out=t[:, c0 + 2:c0 + T], in0=bk[:, c0:c0 + T - 2],
                                      scalar=fcw[:, l, 0, uc:uc + 1], in1=t[:, c0 + 2:c0 + T], op0=ALU.mult, op1=ALU.add),
                                      reads=[bres, tres, "fcw"], writes=[tres])
                            elif stepi == 2:
                                tk.op("dve", lambda e, t=t, uc=uc, c0=c0, hh=hh: e.scalar_tensor_tensor(out=t[:, c0:c0 + 1], in0=hh[:, 1:2],
                                      scalar=fcw[:, l, 1, uc:uc + 1], in1=t[:, c0:c0 + 1], op0=ALU.mult, op1=ALU.add),
                                      reads=[("hF", st, l, uc), tres, "fcw", "histF"], writes=[tres])
                            elif stepi == 3:
                                tk.op("dve", lambda e, t=t, uc=uc, c0=c0, hh=hh: e.scalar_tensor_tensor(out=t[:, c0:c0 + 2], in0=hh[:, 0:2],
                                      scalar=fcw[:, l, 0, uc:uc + 1], in1=t[:, c0:c0 + 2], op0=ALU.mult, op1=ALU.add),
                                      reads=[("hF", st, l, uc), tres, "fcw", "histF"], writes=[tres])
                            elif stepi == 4:
                                tk.op("dve", lambda e, bk=bk, c0=c0, T=T, hh=hh: e.tensor_copy(out=hh, in_=bk[:, c0 + T - 2:c0 + T]),
                                      reads=[bres, "histF"], writes=[("hF", st, l, uc)])
                            elif sg.get("flag_after"):
                                tk.op("dve", lambda e, hh=hh: e.tensor_scalar(out=hh, in0=hh, scalar1=flag[:, 0:1], scalar2=None, op0=ALU.mult),
                                      reads=[("hF", st, l, uc), "flag"], writes=[("hF", st, l, uc)])
                (ta, ra), (tg, rg) = tt
                tk.op("act", lambda e, tg=tg: e.activation(out=tg[:, 0:N], in_=tg[:, 0:N], func=AF.Silu), reads=[rg], writes=[rg])
                tk.op("pool" if cur_pass[0] >= 3 else "dve", lambda e, ta=ta, tg=tg, hb=hb, jj=jj: e.tensor_tensor(out=hid[:, hb, jj, 0:N], in0=ta[:, 0:N], in1=tg[:, 0:N], op=ALU.mult),
                      reads=[ra, rg], writes=[("mix", hb * 11 + jj)])

        def down(q):
            hb = q % 2
            for oc in range(16):
                bk, bres = std_group(S[("dn", l)][q][oc], 11, lambda k: hid[:, hb, k, 0:N], [("mix", hb * 11 + k) for k in range(11)], N)
                tk.op("dve", lambda e, bk=bk, oc=oc: e.tensor_tensor(out=xT[:, oc, 0:N], in0=bk[:, 0:N], in1=xT[:, oc, 0:N], op=ALU.add),
                      reads=[bres, ("xT", oc)], writes=[("xT", oc)])
                if q == 3 and feed is not None:
                    feed.feed(oc)
        up(0); up(1); down(0); up(2); down(1); up(3); down(2); down(3)

    def hist_out(hist_ap, nch, dram_rows, res):
        for r in range(2):
            bi, bk, bres = next_bank()
            tk.op("pe", lambda e, bk=bk, r=r: e.transpose(bk[0:nch, 0:P], hist_ap[:, :, r], ident[:]),
                  reads=list(res) + ["ident", "histA", "histF"], writes=[bres])
            stg = XS[0:nch, r, 0:P]
            tk.op("act", lambda e, bk=bk, stg=stg: e.activation(out=stg, in_=bk[0:nch, 0:P], func=AF.Identity),
                  reads=[bres], writes=[("XS", r)])
            io_dma(dram_rows[r].rearrange("(c f) -> c f", f=P), stg, reads=[("XS", r)], writes=[("hout", id(dram_rows), r)])

    def hF_res(st):
        return [("hF", st, l, uc) for l in range(2) for uc in range(NUP)]

    for p in range(n_pass):
        cur_pass[0] = p
        segs, N = segs_of_pass(p)
        if p > 0:
            for sg in segs:
                load_tokens(sg["src"], sg["T"], xT, sg["c0"], "xT")
        nm = Norm(xT, "xT", g_mix[:, 0, :], N)
        for k in range(KD):
            nm.feed(k)
        for l in range(2):
            if p == n_pass - 1:
                prep_sample_kv(l)
            nm.finish()
            q_proj(S["qA"] if l == 0 else S["qB"], N)
            agen = attention_gen(l, segs)

            def step(agen=agen):
                next(agen, None)
            if l == 0:
                mixer_a(segs, N, step)
            else:
                mixer_b(segs, N, step)
            for _ in agen:
                pass
            nm = Norm(xT, "xT", g_ffn[:, l, :], N)
            out_proj(l, N, feed=nm)
            nm.finish()
            nm = Norm(xT, "xT", g_mix[:, 1, :], N) if l == 0 else Norm(xT, "xT", g_fin, N, inplace=True)
            ffn(l, segs, N, feed=nm)
        nm.finish()
        for sg in segs:
            if sg["dst"] is not None:
                store_tokens(xT, sg["c0"], sg["T"], sg["dst"], "xT")
    hist_out(histA[:, 1, :, :], KD, ca_s, [])
    for l in range(2):
        hist_out(histF[:, 1, l, :, :], NUP, fc_s[l], hF_res(1))
    hist_out(histA[:, 0, :, :], KD, ca_p, [])
    for l in range(2):
        hist_out(histF[:, 0, l, :, :], NUP, fc_p[l], hF_res(0))

    tk._need("sp", tk.final_tokens())
    tk.emit()
    return nc


_W_NAMES = ["norm_mix_g", "norm_mem_g", "w_mem_k", "w_mem_v", "w_in_a", "conv_a_w", "w_in_b", "gmlp_norm_g",
            "gmlp_norm_b", "gmlp_ws", "gmlp_bias", "w_out", "norm_ffn_g", "w_up", "ffn_conv_w", "ffn_conv_b",
            "w_down", "norm_final_g"]


def run(inputs, n_cores):
    x_prompt = np.asarray(inputs["x_prompt"], np.float32)
    x_sample = np.asarray(inputs["x_sample"], np.float32)
    B, SEQ, _ = x_prompt.shape
    DB = x_sample.shape[0]
    assert DB == n_cores and n_cores % B == 0
    cpb = n_cores // B
    per = SEQ // cpb
    n_tiles = per // TMAX
    assert n_tiles * TMAX == per
    nc = build_program(n_tiles)
    wts = {k: np.ascontiguousarray(np.asarray(inputs[k], np.float32)) for k in _W_NAMES}
    in_maps = []
    for c in range(n_cores):
        b, s = c // cpb, c % cpb
        xpc = np.zeros((WARM + per, D), np.float32)
        if s > 0:
            xpc[:] = x_prompt[b, s * per - WARM:(s + 1) * per]
        else:
            xpc[WARM:] = x_prompt[b, 0:per]
        m = dict(wts)
        m["xp"] = xpc
        m["xs"] = np.ascontiguousarray(x_sample[c])
        m["memp"] = np.ascontiguousarray(inputs["mem_prompt"][b], dtype=np.float32)
        m["flag"] = np.full((P, 1), 1.0 if s > 0 else 0.0, np.float32)
        m["cca"] = np.ascontiguousarray(inputs["cache_conv_a"][0, c], dtype=np.float32)
        m["cfc"] = np.ascontiguousarray(inputs["cache_ffn_conv"][:, c], dtype=np.float32)
        m["cmk"] = np.ascontiguousarray(np.asarray(inputs["cache_mem_k"])[:, c].reshape(2, NMEM, DXQ), dtype=np.float32)
        m["cmv"] = np.ascontiguousarray(np.asarray(inputs["cache_mem_v"])[:, c].reshape(2, NMEM, DXQ), dtype=np.float32)
        in_maps.append(m)
    res = run_bass_kernel_spmd(nc, in_maps, core_ids=list(range(n_cores)))
    R = res.results
    y_prompt = np.stack([np.concatenate([R[b * cpb + s]["y_p"] for s in range(cpb)], 0) for b in range(B)])
    y_sample = np.stack([R[c]["y_s"] for c in range(n_cores)])
    last = [b * cpb + cpb - 1 for b in range(B)]
    first = [b * cpb for b in range(B)]
    conv_a_prompt = np.stack([R[c]["ca_p"] for c in last])[None]
    ffn_conv_prompt = np.stack([R[c]["fc_p"] for c in last], 1)
    mem_k_prompt = np.stack([R[c]["mk_p"] for c in first], 1).reshape(2, B, NMEM, 4, 256)
    mem_v_prompt = np.stack([R[c]["mv_p"] for c in first], 1).reshape(2, B, NMEM, 4, 256)
    conv_a_sample = np.stack([R[c]["ca_s"] for c in range(n_cores)])[None]
    ffn_conv_sample = np.stack([R[c]["fc_s"] for c in range(n_cores)], 1)
    gmlp_v_sample = np.stack([R[c]["gv_s"] for c in range(n_cores)])[None]
    outs = (y_prompt, y_sample, conv_a_prompt, ffn_conv_prompt, mem_k_prompt, mem_v_prompt,
            conv_a_sample, ffn_conv_sample, gmlp_v_sample)
    return tuple(np.ascontiguousarray(o, dtype=np.float32) for o in outs)


def kernel(**inputs):
    return run(inputs, 8)
```

```python
import numpy as np
import concourse.bass as bass
import concourse.mybir as mybir
from concourse.bass_utils import run_bass_kernel_spmd
from contextlib import ExitStack

F32 = mybir.dt.float32
BF16 = mybir.dt.bfloat16
AF = mybir.ActivationFunctionType
ALU = mybir.AluOpType

P = 128
D = 2048
KD = 16
DXQ = 1024
NMEM = 256
DFF = 5632
NH = 44
NUP = 88
TMAX = 512
WARM = 256
TS = 32
EPS = 1e-6
NB_RING = 5


class Tracker:
    ENGS = ("pe", "act", "dve", "pool", "sp")

    def __init__(self, nc):
        self.nc = nc
        self.q = {e: [] for e in self.ENGS}
        self.cnt = {e: 0 for e in self.ENGS}
        self.waited = {e: {} for e in self.ENGS}
        self.last_w = {}
        self.reads = {}
        self.sems = {}
        self.sem_keys = [("eng", e) for e in self.ENGS]
        self.dma_rings = {}

    def _need(self, eng, toks):
        need = {}
        for t in toks:
            if t is None:
                continue
            k, v = t
            if eng == "pe" and k == ("eng", "pe"):
                continue
            if need.get(k, 0) < v:
                need[k] = v
        w = self.waited[eng]
        for k, v in need.items():
            if w.get(k, 0) < v:
                w[k] = v
                self.q[eng].append(("wait", k, v))

    def _deps(self, reads, writes):
        toks = []
        for r in reads:
            toks.append(self.last_w.get(r))
        for w in writes:
            toks.append(self.last_w.get(w))
            toks.extend(self.reads.get(w, ()))
        return toks

    def _commit(self, tok, reads, writes):
        for r in reads:
            self.reads.setdefault(r, []).append(tok)
        for w in writes:
            self.last_w[w] = tok
            self.reads[w] = []

    def op(self, eng, fn, reads=(), writes=()):
        self._need(eng, self._deps(reads, writes))
        self.cnt[eng] += 1
        tok = (("eng", eng), self.cnt[eng])
        self.q[eng].append(("op", fn, ("eng", eng), 1))
        self._commit(tok, reads, writes)
        return tok

    def dma_ring(self, name, n):
        keys = [("dma", name, i) for i in range(n)]
        self.sem_keys.extend(keys)
        self.dma_rings[name] = {"keys": keys, "n": 0}

    def dma(self, eng, ring, fn, reads=(), writes=()):
        R = self.dma_rings[ring]
        i = R["n"]
        R["n"] += 1
        k = R["keys"][i % len(R["keys"])]
        prev = i // len(R["keys"])
        toks = self._deps(reads, writes)
        if prev > 0:
            toks.append((k, 16 * prev))
        self._need(eng, toks)
        tok = (k, 16 * (prev + 1))
        self.q[eng].append(("op", fn, k, 16))
        self._commit(tok, reads, writes)
        return tok

    def final_tokens(self):
        toks = list(self.last_w.values())
        for l in self.reads.values():
            toks.extend(l)
        return toks

    def emit(self):
        nc = self.nc
        with ExitStack() as es:
            for k in self.sem_keys:
                self.sems[k] = es.enter_context(nc.semaphore("s_" + "_".join(str(x) for x in k)))
            block = es.enter_context(nc.Block())
            engmap = {"pe": block.tensor, "act": block.scalar, "dve": block.vector,
                      "pool": block.gpsimd, "sp": block.sync}
            for e in self.ENGS:
                items = self.q[e]

                def body(engine, items=items):
                    for it in items:
                        if it[0] == "wait":
                            engine.wait_ge(self.sems[it[1]], it[2])
                        else:
                            it[1](engine).then_inc(self.sems[it[2]], it[3])
                engmap[e](body)


def build_program(n_tiles):
    nc = bass.Bass("TRN2", target_bir_lowering=False)
    tk = Tracker(nc)
    tk.dma_ring("wr", NB_RING)
    tk.dma_ring("cast", 8)
    tk.dma_ring("io", 8)
    tk.dma_ring("io2", 4)
    tk.dma_ring("wb", 6)
    NP = n_tiles * TMAX

    def din(name, shape):
        return nc.dram_tensor(name, list(shape), F32, kind="ExternalInput").ap()

    def dout(name, shape):
        return nc.dram_tensor(name, list(shape), F32, kind="ExternalOutput").ap()

    xp = din("xp", [WARM + NP, D]); xs = din("xs", [TS, D]); memp = din("memp", [NMEM, D])
    flag_d = din("flag", [P, 1])
    cca = din("cca", [2, D]); cfc = din("cfc", [2, 2, 2 * DFF])
    cmk = din("cmk", [2, NMEM, DXQ]); cmv = din("cmv", [2, NMEM, DXQ])
    norm_mix_g = din("norm_mix_g", [2, D]); norm_mem_g = din("norm_mem_g", [2, D])
    w_mem_k = din("w_mem_k", [2, D, DXQ]); w_mem_v = din("w_mem_v", [2, D, DXQ])
    w_in_a = din("w_in_a", [1, D, 3 * D + DXQ]); conv_a_w = din("conv_a_w", [1, 3, D])
    w_in_b = din("w_in_b", [1, D, 2 * D + DXQ])
    gmlp_norm_g = din("gmlp_norm_g", [1, D]); gmlp_norm_b = din("gmlp_norm_b", [1, D])
    gmlp_ws = din("gmlp_ws", [1, 8, P, P]); gmlp_bias = din("gmlp_bias", [1, 8, P])
    w_out = din("w_out", [2, D + DXQ, D]); norm_ffn_g = din("norm_ffn_g", [2, D])
    w_up = din("w_up", [2, D, 2 * DFF]); ffn_conv_w = din("ffn_conv_w", [2, 3, 2 * DFF])
    ffn_conv_b = din("ffn_conv_b", [2, 2 * DFF]); w_down = din("w_down", [2, DFF, D])
    norm_final_g = din("norm_final_g", [D])

    y_p = dout("y_p", [NP, D]); y_s = dout("y_s", [TS, D])
    ca_p = dout("ca_p", [2, D]); fc_p = dout("fc_p", [2, 2, 2 * DFF])
    mk_p = dout("mk_p", [2, NMEM, DXQ]); mv_p = dout("mv_p", [2, NMEM, DXQ])
    ca_s = dout("ca_s", [2, D]); fc_s = dout("fc_s", [2, 2, 2 * DFF])
    gv_s = dout("gv_s", [TS, D])

    slot_specs = []

    def add_slot(kind, W, k0, nk, c0):
        slot_specs.append((kind, W, k0, nk, c0))
        return len(slot_specs) - 1

    S = {}
    for l in range(2):
        S[("kT", l)] = [add_slot("std", w_mem_k[l], 0, 16, dc * P) for dc in range(8)]
        for nm, W in (("ktok", w_mem_k[l]), ("vtok", w_mem_v[l])):
            S[(nm, l)] = [[add_slot("wide", W, 4 * s, 4, cg * 512) for s in range(4)] for cg in range(2)]
    S["inA"] = [[add_slot("std", w_in_a[0], 0, 16, D + j * P), add_slot("std", w_in_a[0], 0, 16, 2 * D + j * P),
                 add_slot("std", w_in_a[0], 0, 16, j * P)] for j in range(16)]
    S["qA"] = [add_slot("std", w_in_a[0], 0, 16, 3 * D + qc * P) for qc in range(8)]
    S["uB"] = [add_slot("std", w_in_b[0], 0, 16, j * P) for j in range(16)]
    S["vB"] = [[add_slot("wide", w_in_b[0], 4 * s, 4, D + fg * 512) for s in range(4)] for fg in range(4)]
    S["qB"] = [add_slot("std", w_in_b[0], 0, 16, 2 * D + qc * P) for qc in range(8)]
    for l in range(2):
        S[("o1", l)] = [add_slot("std", w_out[l], 0, 16, oc * P) for oc in range(16)]
        S[("o2", l)] = [add_slot("std", w_out[l], 16, 8, oc * P) for oc in range(16)]
        S[("up", l)] = [[add_slot("std", w_up[l], 0, 16, j * P), add_slot("std", w_up[l], 0, 16, DFF + j * P)]
                        for j in range(NH)]
        S[("dn", l)] = [[add_slot("std", w_down[l], 11 * q, 11, oc * P) for oc in range(16)] for q in range(4)]
    NSLOT = len(slot_specs)
    wsc = nc.dram_tensor("wsc", [NSLOT, P, 2048], BF16).ap()

    def sb(name, shape, dt=F32):
        return nc.alloc_sbuf_tensor("sb_" + name, list(shape), dt)

    xT = sb("xT", [P, KD, TMAX])
    xn = sb("xn", [P, KD, TMAX], BF16)
    R1 = sb("R1", [P, 32 * TMAX], BF16)
    mixin = R1[:].rearrange("p (k t) -> p k t", t=TMAX)
    hid = R1[:, 0:22 * TMAX].rearrange("p (b k t) -> p b k t", b=2, k=11)
    memT = R1[:, 0:16 * 2 * NMEM].bitcast(F32).rearrange("p (k t) -> p k t", t=NMEM)
    KTp = sb("KTp", [P, 2, 8, NMEM], BF16); Vp = sb("Vp", [P, 2, 2, DXQ], BF16)
    KTs = sb("KTs", [P, 8, NMEM], BF16); Vs = sb("Vs", [P, 2, DXQ], BF16)
    Eb = sb("Eb", [P, 2, 2, TMAX], BF16)
    XS = sb("XS", [P, 2, D])
    vn = sb("vn", [P, 4, D], BF16)
    gb = sb("gb", [P, 2, D])
    wring = sb("wring", [P, NB_RING, 2048], BF16)
    CT = sb("CT", [P, 2, 3, 520])
    rstd = sb("rstd", [P, TMAX]); rden = sb("rden", [P, TMAX])
    histA = sb("histA", [P, 2, 16, 2]); histF = sb("histF", [P, 2, 2, NUP, 2])
    ident = sb("ident", [P, P]); ones_f = sb("ones_f", [P, P]); ones_b = sb("ones_b", [P, P], BF16)
    wmT = sb("wmT", [P, 8, P], BF16); biasbc = sb("biasbc", [P, 8, P])
    g_mix = sb("g_mix", [P, 2, KD]); g_ffn = sb("g_ffn", [P, 2, KD]); g_mem = sb("g_mem", [P, 2, KD])
    g_fin = sb("g_fin", [P, KD])
    caw = sb("caw", [P, 3, KD]); fcw = sb("fcw", [P, 2, 3, NUP]); fcb = sb("fcb", [P, 2, NUP])
    flag = sb("flag", [P, 1]); epsb = sb("epsb", [P, 1])
    stt = sb("stt", [P, 2, 4, 6]); mv = sb("mv", [P, 2, 2]); lnr = sb("lnr", [P, 2, 2])
    banks = [nc.alloc_psum_tensor(f"bank{i}", [P, TMAX], F32) for i in range(8)]
    bank_ctr = [0]

    def next_bank():
        b = bank_ctr[0] % 7
        bank_ctr[0] += 1
        return b, banks[b], ("bank", b)

    ring_ctr = [0]
    cast_done = set()
    cur_pass = [-1]
    n_pass = n_tiles + 1
    wb_pass = {}
    for l in range(2):
        for sid in S[("kT", l)] + [x for nm in ("ktok", "vtok") for row in S[(nm, l)] for x in row]:
            wb_pass[sid] = None
        ffn_sids = [x for pair in S[("up", l)] for x in pair] + [x for row in S[("dn", l)] for x in row]
        for i, sid in enumerate(ffn_sids):
            wb_pass[sid] = i % 3

    def load_slot(sid):
        b = ring_ctr[0] % NB_RING
        ring_ctr[0] += 1
        kind, W, k0, nk, c0 = slot_specs[sid]
        cw = 128 if kind == "std" else 512
        ncol = nk * cw
        if sid not in cast_done:
            src = W[k0 * P:(k0 + nk) * P, c0:c0 + cw].rearrange("(k p) c -> p k c", p=P)
            dst = wring[:, b, 0:ncol].rearrange("p (k c) -> p k c", c=cw)
            tk.dma("pool", "cast", lambda e, dst=dst, src=src: e.dma_start(out=dst, in_=src), writes=[("wr", b)])
            wp = wb_pass.get(sid, 0)
            if wp is not None and cur_pass[0] >= wp and cur_pass[0] < n_pass - 1:
                cast_done.add(sid)
                tk.dma("sp", "wb", lambda e, b=b, sid=sid, ncol=ncol: e.dma_start(out=wsc[sid][:, 0:ncol], in_=wring[:, b, 0:ncol]),
                       reads=[("wr", b)], writes=[("wsc", sid)])
        else:
            tk.dma("sp", "wr", lambda e, b=b, sid=sid, ncol=ncol: e.dma_start(out=wring[:, b, 0:ncol], in_=wsc[sid][:, 0:ncol]),
                   reads=[("wsc", sid)], writes=[("wr", b)])
        return b

    def emit_cast(sid):
        kind, W, k0, nk, c0 = slot_specs[sid]
        cw = 128 if kind == "std" else 512
        src = W[k0 * P:(k0 + nk) * P, c0:c0 + cw].rearrange("(k p) c -> p k c", p=P)
        dst = wsc[sid][:, 0:nk * cw].rearrange("p (k c) -> p k c", c=cw)
        tk.dma("pool", "cast", lambda e: e.dma_start(out=dst, in_=src), writes=[("wsc", sid)])

    def io_dma(out_ap, in_ap, reads=(), writes=(), eng="pool"):
        tk.dma(eng, "io" if eng == "pool" else "io2", lambda e: e.dma_start(out=out_ap, in_=in_ap), reads=reads, writes=writes)

    def std_group(sid, nk, rhs_fn, rhs_res, N):
        b = load_slot(sid)
        bi, bk, bres = next_bank()
        w3 = wring[:, b, :].rearrange("p (k c) -> p k c", c=P)

        def fn(e):
            for k in range(nk):
                ins = e.matmul(bk[:, 0:N], lhsT=w3[:, k, :], rhs=rhs_fn(k), start=(k == 0), stop=(k == nk - 1))
            return ins
        tk.op("pe", fn, reads=[("wr", b)] + list(rhs_res), writes=[bres])
        return bk, bres

    def xn_res():
        return [("xn", k) for k in range(KD)]

    def std_groups_il(sids, N):
        bs = [load_slot(sid) for sid in sids]
        bks = [next_bank() for _ in sids]
        for k in range(KD):
            for b, (bi, bk, bres) in zip(bs, bks):
                w3 = wring[:, b, :].rearrange("p (k c) -> p k c", c=P)
                tk.op("pe", lambda e, bk=bk, w3=w3, k=k: e.matmul(bk[:, 0:N], lhsT=w3[:, k, :], rhs=xn[:, k, 0:N],
                      start=(k == 0), stop=(k == KD - 1)), reads=[("wr", b), ("xn", k)], writes=[bres])
        return [(bk, bres) for (bi, bk, bres) in bks]

    tk.op("pool", lambda e: e.memset(ones_f[:], 1.0), writes=["ones_f"])
    tk.op("pool", lambda e: e.affine_select(out=ident[:], in_=ones_f[:], pattern=[[1, P]], compare_op=ALU.is_equal,
                                            fill=0.0, base=0, channel_multiplier=-1), reads=["ones_f"], writes=["ident"])
    tk.op("dve", lambda e: e.tensor_copy(out=ones_b[:], in_=ones_f[:]), reads=["ones_f"], writes=["ones_b"])
    tk.op("dve", lambda e: e.memset(epsb[:], EPS), writes=["epsb"])
    tk.op("dve", lambda e: e.memset(histA[:].rearrange("p a b c -> p (a b c)"), 0.0), writes=["histA"])
    tk.op("dve", lambda e: e.memset(histF[:].rearrange("p a b c d -> p (a b c d)"), 0.0), writes=["histF"])
    io_dma(flag[:], flag_d, writes=["flag"])
    io_dma(gb[:, 0, :], gmlp_norm_g[0].partition_broadcast(P), writes=["gb"])
    io_dma(gb[:, 1, :], gmlp_norm_b[0].partition_broadcast(P), writes=["gb"])
    io_dma(biasbc[:].rearrange("p g t -> p (g t)"), gmlp_bias[0].rearrange("g t -> (g t)").partition_broadcast(P),
           writes=["biasbc"])

    def rows_to_fm(dst_ap, rows_ap, nch, dst_res, stage_col):
        sl_ = stage_col % 6
        sres = ("ct", sl_ // 3, sl_ % 3)
        stg = CT[0:nch, sl_ // 3, sl_ % 3, 0:P]
        io_dma(stg, rows_ap.rearrange("(c f) -> c f", f=P), writes=[sres])
        bi, bk, bres = next_bank()
        tk.op("pe", lambda e: e.transpose(bk[:, 0:nch], stg, ident[0:nch, 0:nch]),
              reads=[sres, "ident"], writes=[bres])
        tk.op("act", lambda e: e.activation(out=dst_ap, in_=bk[:, 0:nch], func=AF.Identity),
              reads=[bres], writes=[dst_res])

    col = [0]

    def vec_load(dst_ap, rows_ap, nch, dst_res):
        rows_to_fm(dst_ap, rows_ap, nch, dst_res, col[0])
        col[0] += 1

    for l in range(2):
        vec_load(g_mix[:, l, :], norm_mix_g[l], KD, "gvec")
        vec_load(g_ffn[:, l, :], norm_ffn_g[l], KD, "gvec")
        vec_load(g_mem[:, l, :], norm_mem_g[l], KD, "gvec")
        vec_load(fcb[:, l, :], ffn_conv_b[l], NUP, "fcb")
        for t in range(3):
            vec_load(fcw[:, l, t, :], ffn_conv_w[l, t], NUP, "fcw")
        for r in range(2):
            vec_load(histF[:, 1, l, :, r], cfc[l, r], NUP, "histF")
    vec_load(g_fin[:, :], norm_final_g, KD, "gvec")
    for t in range(3):
        vec_load(caw[:, t, :], conv_a_w[0, t], KD, "caw")
    for r in range(2):
        vec_load(histA[:, 1, :, r], cca[r], KD, "histA")

    for g in range(8):
        stg = XS[:, 1, (g % 8) * P:(g % 8 + 1) * P]
        io_dma(stg, gmlp_ws[0, g], writes=[("XS", 1)])
        bi, bk, bres = next_bank()
        tk.op("pe", lambda e, bk=bk, stg=stg: e.transpose(bk[:, 0:P], stg, ident[:]),
              reads=[("XS", 1), "ident"], writes=[bres])
        tmp = CT[:, 0, 0, 0:P]
        tk.op("act", lambda e, bk=bk, tmp=tmp: e.activation(out=tmp, in_=bk[:, 0:P], func=AF.Identity),
              reads=[bres], writes=[("ct", 0, 0)])
        tmp2 = CT[:, 0, 1, 0:P]
        tk.op("pool", lambda e, tmp=tmp, tmp2=tmp2: e.affine_select(out=tmp2, in_=tmp, pattern=[[1, P]],
              compare_op=ALU.is_ge, fill=0.0, base=0, channel_multiplier=-1), reads=[("ct", 0, 0)], writes=[("ct", 0, 1)])
        tk.op("dve", lambda e, g=g, tmp2=tmp2: e.tensor_copy(out=wmT[:, g, :], in_=tmp2), reads=[("ct", 0, 1)], writes=["wmT"])

    def load_tokens(src_rows_ap, nrows, dstT, c0, dst_name):
        nblk = (nrows + P - 1) // P
        for blk in range(nblk):
            nr = min(P, nrows - blk * P)
            bi_ = blk % 2
            io_dma(XS[0:nr, bi_, :], src_rows_ap[blk * P:blk * P + nr, :], writes=[("XS", bi_)])
            for g4 in range(4):
                bi, bk, bres = next_bank()

                def fn(e, bk=bk, bi_=bi_, nr=nr, g4=g4):
                    for q in range(4):
                        ins = e.transpose(bk[:, q * P:q * P + nr], XS[0:nr, bi_, (4 * g4 + q) * P:(4 * g4 + q + 1) * P],
                                          ident[0:nr, 0:nr])
                    return ins
                tk.op("pe", fn, reads=[("XS", bi_), "ident"], writes=[bres])
                src = bk[:].rearrange("p (q t) -> p q t", q=4)[:, :, 0:nr]
                dst = dstT[:, 4 * g4:4 * g4 + 4, c0 + blk * P:c0 + blk * P + nr]
                eng = "act" if g4 % 2 == 0 else "dve"
                if eng == "act":
                    tk.op("act", lambda e, src=src, dst=dst: e.activation(out=dst, in_=src, func=AF.Identity),
                          reads=[bres], writes=[(dst_name, 4 * g4 + q) for q in range(4)])
                else:
                    tk.op("dve", lambda e, src=src, dst=dst: e.tensor_copy(out=dst, in_=src),
                          reads=[bres], writes=[(dst_name, 4 * g4 + q) for q in range(4)])

    def store_tokens(srcT, c0, nrows, dst_rows_ap, src_name):
        nblk = (nrows + P - 1) // P
        for blk in range(nblk):
            nr = min(P, nrows - blk * P)
            bi_ = blk % 2
            for g4 in range(4):
                bi, bk, bres = next_bank()

                def fn(e, bk=bk, nr=nr, g4=g4, blk=blk):
                    for q in range(4):
                        ins = e.transpose(bk[0:nr, q * P:(q + 1) * P],
                                          srcT[:, 4 * g4 + q, c0 + blk * P:c0 + blk * P + nr], ident[:])
                    return ins
                tk.op("pe", fn, reads=[(src_name, 4 * g4 + q) for q in range(4)] + ["ident"], writes=[bres])
                dst = XS[0:nr, bi_, g4 * 512:(g4 + 1) * 512]
                if g4 % 2 == 0:
                    tk.op("act", lambda e, bk=bk, dst=dst, nr=nr: e.activation(out=dst, in_=bk[0:nr, :], func=AF.Identity),
                          reads=[bres], writes=[("XS", bi_)])
                else:
                    tk.op("dve", lambda e, bk=bk, dst=dst, nr=nr: e.tensor_copy(out=dst, in_=bk[0:nr, :]),
                          reads=[bres], writes=[("XS", bi_)])
            io_dma(dst_rows_ap[blk * P:blk * P + nr, :], XS[0:nr, bi_, :], reads=[("XS", bi_)], writes=[("out", id(dst_rows_ap), blk)])

    class Norm:
        def __init__(self, srcT, src_name, gvec, N, inplace=False):
            self.srcT, self.src_name, self.gvec, self.N, self.inplace = srcT, src_name, gvec, N, inplace
            self.pending = None
            self.n_mm = 0

        def _mm(self, k):
            N = self.N
            first, last = self.n_mm == 0, self.n_mm == KD - 1
            self.n_mm += 1
            bk = banks[7]
            tk.op("pe", lambda e: e.matmul(bk[:, 0:N], lhsT=ones_b[:], rhs=xn[:, k, 0:N], start=first, stop=last),
                  reads=[("xn", k), "ones_b"], writes=[("bank", 7)])

        def feed(self, k):
            N, srcT = self.N, self.srcT
            tk.op("act", lambda e: e.activation(out=xn[:, k, 0:N], in_=srcT[:, k, 0:N], func=AF.Square),
                  reads=[(self.src_name, k)], writes=[("xn", k)])
            if self.pending is not None:
                self._mm(self.pending)
            self.pending = k

        def finish(self):
            N, srcT, gvec, src_name = self.N, self.srcT, self.gvec, self.src_name
            self._mm(self.pending)
            assert self.n_mm == KD
            bk = banks[7]
            tk.op("act", lambda e: e.activation(out=rstd[:, 0:N], in_=bk[:, 0:N], func=AF.Sqrt, scale=1.0 / D, bias=epsb[:]),
                  reads=[("bank", 7), "epsb"], writes=["rstd"])
            tk.op("dve", lambda e: e.reciprocal(out=rstd[:, 0:N], in_=rstd[:, 0:N]), reads=["rstd"], writes=["rstd"])
            for k in range(KD):
                if self.inplace:
                    tk.op("dve", lambda e, k=k: e.scalar_tensor_tensor(out=srcT[:, k, 0:N], in0=srcT[:, k, 0:N],
                          scalar=gvec[:, k:k + 1], in1=rstd[:, 0:N], op0=ALU.mult, op1=ALU.mult),
                          reads=[(src_name, k), "rstd", "gvec"], writes=[(src_name, k)])
                else:
                    tk.op("dve", lambda e, k=k: e.scalar_tensor_tensor(out=xn[:, k, 0:N], in0=srcT[:, k, 0:N],
                          scalar=gvec[:, k:k + 1], in1=rstd[:, 0:N], op0=ALU.mult, op1=ALU.mult),
                          reads=[(src_name, k), "rstd", "gvec"], writes=[("xn", k)])

    def rmsnorm(srcT, src_name, gvec, N, out_fp32_inplace=False):
        nm = Norm(srcT, src_name, gvec, N, out_fp32_inplace)
        for k in range(KD):
            nm.feed(k)
        nm.finish()

    load_tokens(memp, NMEM, memT, 0, "memT")

    HALF = TMAX // 2

    def segs_of_pass(p):
        if p == 0:
            return [dict(c0=0, T=WARM, stream=0, src=xp[0:WARM, :], dst=None, flag_after=True),
                    dict(c0=WARM, T=HALF, stream=0, src=xp[WARM:WARM + HALF, :], dst=y_p[0:HALF, :])], WARM + HALF
        if p == n_pass - 1:
            return [dict(c0=0, T=HALF, stream=0, src=xp[WARM + NP - HALF:WARM + NP, :], dst=y_p[NP - HALF:NP, :]),
                    dict(c0=HALF, T=TS, stream=1, src=xs, dst=y_s)], HALF + TS
        y0 = HALF + (p - 1) * TMAX
        return [dict(c0=0, T=TMAX, stream=0, src=xp[WARM + y0:WARM + y0 + TMAX, :], dst=y_p[y0:y0 + TMAX, :])], TMAX

    segs0, N0 = segs_of_pass(0)
    for sg in segs0:
        load_tokens(sg["src"], sg["T"], xT, sg["c0"], "xT")


    for l in range(2):
        rmsnorm(memT, "memT", g_mem[:, l, :], NMEM)
        for wi, (nm, dstd) in enumerate((("ktok", mk_p), ("vtok", mv_p))):
            for cg in range(2):
                bks = [next_bank() for _ in range(2)]
                for s in range(4):
                    b = load_slot(S[(nm, l)][cg][s])
                    w3 = wring[:, b, :].rearrange("p (k c) -> p k c", c=512)

                    def fn(e, s=s, w3=w3, bks=bks):
                        for mb in range(2):
                            for kk in range(4):
                                ins = e.matmul(bks[mb][1][:, 0:512], lhsT=xn[:, 4 * s + kk, mb * P:(mb + 1) * P],
                                               rhs=w3[:, kk, :], start=(s == 0 and kk == 0), stop=(s == 3 and kk == 3))
                        return ins
                    tk.op("pe", fn, reads=[("wr", b)] + xn_res(), writes=[bks[0][2], bks[1][2]])
                for mb in range(2):
                    dst = XS[:, mb, wi * DXQ + cg * 512:wi * DXQ + (cg + 1) * 512]
                    tk.op("act", lambda e, dst=dst, bk=bks[mb][1]: e.activation(out=dst, in_=bk[:, 0:512], func=AF.Identity),
                          reads=[bks[mb][2]], writes=[("XS", mb)])
        for d2 in range(4):
            bi, bk, bres = next_bank()

            def fnkt(e, bk=bk, d2=d2):
                for dd in range(2):
                    for mb in range(2):
                        ins = e.transpose(bk[:, dd * NMEM + mb * P:dd * NMEM + (mb + 1) * P],
                                          XS[:, mb, (2 * d2 + dd) * P:(2 * d2 + dd + 1) * P], ident[:])
                return ins
            tk.op("pe", fnkt, reads=[("XS", 0), ("XS", 1), "ident"], writes=[bres])
            tk.op("act", lambda e, bk=bk, d2=d2, l=l: e.activation(out=KTp[:, l, 2 * d2:2 * d2 + 2, :],
                  in_=bk[:].rearrange("p (a m) -> p a m", a=2), func=AF.Identity), reads=[bres], writes=[("KTp", l)])
        for mb in range(2):
            tk.op("dve", lambda e, l=l, mb=mb: e.tensor_copy(out=Vp[:, l, mb, :], in_=XS[:, mb, DXQ:2 * DXQ]),
                  reads=[("XS", mb)], writes=[("Vp", l)])
            io_dma(mk_p[l, mb * P:(mb + 1) * P, :], XS[:, mb, 0:DXQ], reads=[("XS", mb)], writes=[("mk_p", l, mb)], eng="sp")
            io_dma(mv_p[l, mb * P:(mb + 1) * P, :], XS[:, mb, DXQ:2 * DXQ], reads=[("XS", mb)], writes=[("mv_p", l, mb)], eng="sp")

    def prep_sample_kv(l):
        for mb in range(2):
            io_dma(XS[:, mb, 0:DXQ], cmk[l, mb * P:(mb + 1) * P, :], writes=[("XS", mb)], eng="sp")
            io_dma(XS[:, mb, DXQ:2 * DXQ], cmv[l, mb * P:(mb + 1) * P, :], writes=[("XS", mb)], eng="sp")
            tk.op("dve", lambda e, mb=mb: e.tensor_copy(out=Vs[:, mb, :], in_=XS[:, mb, DXQ:2 * DXQ]),
                  reads=[("XS", mb)], writes=["Vs"])
        for d2 in range(4):
            bi, bk, bres = next_bank()

            def fn(e, bk=bk, d2=d2):
                for dd in range(2):
                    for mb in range(2):
                        ins = e.transpose(bk[:, dd * NMEM + mb * P:dd * NMEM + (mb + 1) * P],
                                          XS[:, mb, (2 * d2 + dd) * P:(2 * d2 + dd + 1) * P], ident[:])
                return ins
            tk.op("pe", fn, reads=[("XS", 0), ("XS", 1), "ident"], writes=[bres])
            tk.op("act", lambda e, bk=bk, d2=d2: e.activation(out=KTs[:, 2 * d2:2 * d2 + 2, :],
                  in_=bk[:].rearrange("p (a m) -> p a m", a=2), func=AF.Identity), reads=[bres], writes=["KTs"])

    def mix_res(ks):
        return [("mix", k) for k in ks]

    def attention_gen(l, segs):
        for sg in segs:
            c0, T, st = sg["c0"], sg["T"], sg["stream"]
            for h in range(4):
                eb = h % 2
                for mc in range(2):
                    bi, bk, bres = next_bank()

                    def fn(e, bk=bk, mc=mc, h=h, st=st, c0=c0, T=T):
                        for dc in range(2):
                            kt = KTp[:, l, 2 * h + dc, mc * P:(mc + 1) * P] if st == 0 else KTs[:, 2 * h + dc, mc * P:(mc + 1) * P]
                            ins = e.matmul(bk[:, 0:T], lhsT=kt, rhs=mixin[:, 24 + 2 * h + dc, c0:c0 + T],
                                           start=(dc == 0), stop=(dc == 1))
                        return ins
                    tk.op("pe", fn, reads=mix_res([24 + 2 * h, 25 + 2 * h]) + [("KTp", l), "KTs"], writes=[bres])
                    tk.op("act", lambda e, bk=bk, mc=mc, eb=eb, T=T: e.activation(out=Eb[:, eb, mc, 0:T], in_=bk[:, 0:T], func=AF.Exp),
                          reads=[bres], writes=[("Eb", eb, mc)])
                yield
                bi, bkd, bresd = next_bank()

                def fnd(e, bkd=bkd, eb=eb, T=T):
                    for mc in range(2):
                        ins = e.matmul(bkd[:, 0:T], lhsT=ones_b[:], rhs=Eb[:, eb, mc, 0:T], start=(mc == 0), stop=(mc == 1))
                    return ins
                tk.op("pe", fnd, reads=[("Eb", eb, 0), ("Eb", eb, 1), "ones_b"], writes=[bresd])
                tk.op("dve", lambda e, bkd=bkd, T=T: e.reciprocal(out=rden[:, 0:T], in_=bkd[:, 0:T]), reads=[bresd], writes=["rden"])
                for dc in range(2):
                    bi, bk, bres = next_bank()

                    def fnv(e, bk=bk, dc=dc, h=h, eb=eb, st=st, T=T):
                        for mc in range(2):
                            vv = Vp[:, l, mc, h * 256 + dc * P:h * 256 + (dc + 1) * P] if st == 0 else Vs[:, mc, h * 256 + dc * P:h * 256 + (dc + 1) * P]
                            ins = e.matmul(bk[:, 0:T], lhsT=vv, rhs=Eb[:, eb, mc, 0:T], start=(mc == 0), stop=(mc == 1))
                        return ins
                    tk.op("pe", fnv, reads=[("Eb", eb, 0), ("Eb", eb, 1), ("Vp", l), "Vs"], writes=[bres])
                    tk.op("dve", lambda e, bk=bk, dc=dc, h=h, c0=c0, T=T: e.tensor_tensor(out=mixin[:, 16 + 2 * h + dc, c0:c0 + T], in0=bk[:, 0:T],
                          in1=rden[:, 0:T], op=ALU.mult), reads=[bres, "rden"], writes=[("mix", 16 + 2 * h + dc)])
                yield

    def out_proj(l, N, feed=None):
        for oc in range(16):
            b1 = load_slot(S[("o1", l)][oc])
            b2 = load_slot(S[("o2", l)][oc])
            bi, bk, bres = next_bank()
            wa = wring[:, b1, :].rearrange("p (k c) -> p k c", c=P)
            wb_ = wring[:, b2, :].rearrange("p (k c) -> p k c", c=P)

            def fn(e, bk=bk, wa=wa, wb_=wb_):
                for k in range(16):
                    e.matmul(bk[:, 0:N], lhsT=wa[:, k, :], rhs=mixin[:, k, 0:N], start=(k == 0), stop=False)
                for k in range(8):
                    ins = e.matmul(bk[:, 0:N], lhsT=wb_[:, k, :], rhs=mixin[:, 16 + k, 0:N], start=False, stop=(k == 7))
                return ins
            tk.op("pe", fn, reads=[("wr", b1), ("wr", b2)] + mix_res(range(24)), writes=[bres])
            tk.op("dve", lambda e, bk=bk, oc=oc: e.tensor_tensor(out=xT[:, oc, 0:N], in0=bk[:, 0:N], in1=xT[:, oc, 0:N], op=ALU.add),
                  reads=[bres, ("xT", oc)], writes=[("xT", oc)])
            if feed is not None:
                feed.feed(oc)

    def q_proj(sl, N):
        first = std_groups_il(sl[0:3], N)
        for qc in range(8):
            if qc < 3:
                bk, bres = first[qc]
            else:
                bk, bres = std_group(sl[qc], 16, lambda k: xn[:, k, 0:N], xn_res(), N)
            tk.op("act", lambda e, bk=bk, qc=qc: e.activation(out=mixin[:, 24 + qc, 0:N], in_=bk[:, 0:N], func=AF.Identity, scale=0.0625),
                  reads=[bres], writes=[("mix", 24 + qc)])

    ct_ctr = [0]

    def mixer_a(segs, N, step):
        for j in range(16):
            step()
            sc, sh, sbg = S["inA"][j]
            bkc, brc = std_group(sc, 16, lambda k: xn[:, k, 0:N], xn_res(), N)
            bkh, brh = std_group(sh, 16, lambda k: xn[:, k, 0:N], xn_res(), N)
            bkb, brb = std_group(sbg, 16, lambda k: xn[:, k, 0:N], xn_res(), N)
            cb = ct_ctr[0] % 2
            ct_ctr[0] += 1
            hsb, ch, tcv = CT[:, cb, 0, :], CT[:, cb, 1, :], CT[:, cb, 2, :]
            tk.op("act", lambda e, bkh=bkh, hsb=hsb: e.activation(out=hsb[:, 0:N], in_=bkh[:, 0:N], func=AF.Identity),
                  reads=[brh], writes=[("ct", cb, 0)])
            for si, sg in enumerate(segs):
                c0, T, st = sg["c0"], sg["T"], sg["stream"]
                off = c0 + 2 * si
                tk.op("dve", lambda e, ch=ch, off=off, st=st, j=j: e.tensor_copy(out=ch[:, off:off + 2], in_=histA[:, st, j, :]),
                      reads=["histA"], writes=[("ct", cb, 1)])
                tk.op("dve", lambda e, ch=ch, off=off, bkc=bkc, hsb=hsb, c0=c0, T=T: e.tensor_tensor(out=ch[:, off + 2:off + 2 + T],
                      in0=bkc[:, c0:c0 + T], in1=hsb[:, c0:c0 + T], op=ALU.mult), reads=[brc, ("ct", cb, 0)], writes=[("ct", cb, 1)])
                tk.op("dve", lambda e, ch=ch, off=off, st=st, j=j, T=T: e.tensor_copy(out=histA[:, st, j, :], in_=ch[:, off + T:off + T + 2]),
                      reads=[("ct", cb, 1)], writes=["histA"])
                if sg.get("flag_after"):
                    tk.op("dve", lambda e, st=st, j=j: e.tensor_scalar(out=histA[:, st, j, :], in0=histA[:, st, j, :], scalar1=flag[:, 0:1],
                          scalar2=None, op0=ALU.mult), reads=["histA", "flag"], writes=["histA"])
                tk.op("dve", lambda e, ch=ch, off=off, tcv=tcv, c0=c0, T=T, j=j: e.tensor_scalar(out=tcv[:, c0:c0 + T], in0=ch[:, off + 2:off + 2 + T],
                      scalar1=caw[:, 2, j:j + 1], scalar2=None, op0=ALU.mult), reads=[("ct", cb, 1), "caw"], writes=[("ct", cb, 2)])
                for tap in (1, 0):
                    tk.op("dve", lambda e, ch=ch, off=off, tcv=tcv, c0=c0, T=T, j=j, tap=tap: e.scalar_tensor_tensor(out=tcv[:, c0:c0 + T],
                          in0=ch[:, off + tap:off + tap + T], scalar=caw[:, tap, j:j + 1], in1=tcv[:, c0:c0 + T], op0=ALU.mult, op1=ALU.add),
                          reads=[("ct", cb, 1), ("ct", cb, 2), "caw"], writes=[("ct", cb, 2)])
            tk.op("dve", lambda e, bkb=bkb, tcv=tcv, j=j: e.tensor_tensor(out=mixin[:, j, 0:N], in0=bkb[:, 0:N], in1=tcv[:, 0:N], op=ALU.mult),
                  reads=[brb, ("ct", cb, 2)], writes=[("mix", j)])

    def mixer_b(segs, N, step):
        def u_proj(js):
            for j in js:
                step()
                bk, bres = std_group(S["uB"][j], 16, lambda k: xn[:, k, 0:N], xn_res(), N)
                tk.op("act", lambda e, bk=bk, j=j: e.activation(out=mixin[:, j, 0:N], in_=bk[:, 0:N], func=AF.Gelu),
                      reads=[bres], writes=[("mix", j)])

        blocks = []
        for sg in segs:
            nblk = (sg["T"] + P - 1) // P
            for blk in range(nblk):
                nr = min(P, sg["T"] - blk * P)
                blocks.append(dict(cs=sg["c0"] + blk * P, nr=nr, sample=(sg["stream"] == 1)))
        halves = [blocks[h0:h0 + 2] for h0 in range(0, len(blocks), 2)]
        usplit = [range(0, 8), range(8, 16)] if len(halves) == 2 else [range(0, 16)]

        def v_proj(half, vb):
            for fg in range(4):
                bks = [next_bank() for _ in half]
                for s_ in range(4):
                    b = load_slot(S["vB"][fg][s_])
                    w3 = wring[:, b, :].rearrange("p (k c) -> p k c", c=512)

                    def fn(e, s_=s_, w3=w3, bks=bks, half=half):
                        for i, bl in enumerate(half):
                            for kk in range(4):
                                ins = e.matmul(bks[i][1][0:bl["nr"], 0:512], lhsT=xn[:, 4 * s_ + kk, bl["cs"]:bl["cs"] + bl["nr"]],
                                               rhs=w3[:, kk, :], start=(s_ == 0 and kk == 0), stop=(s_ == 3 and kk == 3))
                        return ins
                    tk.op("pe", fn, reads=[("wr", b)] + xn_res(), writes=[x[2] for x in bks])
                for i, bl in enumerate(half):
                    dst = XS[0:bl["nr"], i, fg * 512:(fg + 1) * 512]
                    tk.op("act", lambda e, dst=dst, bk=bks[i][1], nr=bl["nr"]: e.activation(out=dst, in_=bk[0:nr, 0:512], func=AF.Gelu),
                          reads=[bks[i][2]], writes=[("XS", i)])
            for i, bl in enumerate(half):
                nr = bl["nr"]
                xv = XS[0:nr, i, :]
                st_ = stt[0:nr, i, :, :]
                for c4 in range(4):
                    tk.op("dve", lambda e, c4=c4, nr=nr, i=i, st_=st_: e.bn_stats(out=st_[:, c4, :], in_=XS[0:nr, i, c4 * 512:(c4 + 1) * 512]),
                          reads=[("XS", i)], writes=[("stt", i)])
                tk.op("dve", lambda e, nr=nr, i=i, st_=st_: e.bn_aggr(out=mv[0:nr, i, :], in_=st_.rearrange("p c s -> p (c s)")),
                      reads=[("stt", i)], writes=[("mv", i)])
                tk.op("act", lambda e, nr=nr, i=i: e.activation(out=lnr[0:nr, i, 0:1], in_=mv[0:nr, i, 1:2], func=AF.Sqrt, scale=1.0, bias=epsb[0:nr, :]),
                      reads=[("mv", i), "epsb"], writes=[("lnr", i)])
                tk.op("dve", lambda e, nr=nr, i=i: e.reciprocal(out=lnr[0:nr, i, 0:1], in_=lnr[0:nr, i, 0:1]), reads=[("lnr", i)], writes=[("lnr", i)])
                tk.op("dve", lambda e, nr=nr, i=i: e.tensor_scalar(out=lnr[0:nr, i, 1:2], in0=mv[0:nr, i, 0:1], scalar1=lnr[0:nr, i, 0:1], scalar2=-1.0,
                      op0=ALU.mult, op1=ALU.mult), reads=[("mv", i), ("lnr", i)], writes=[("lnr", i)])
                tk.op("act", lambda e, xv=xv, nr=nr, i=i: e.activation(out=xv, in_=xv, func=AF.Identity, scale=lnr[0:nr, i, 0:1], bias=lnr[0:nr, i, 1:2]),
                      reads=[("XS", i), ("lnr", i)], writes=[("XS", i)])
                ge = "dve" if i == 0 else "pool"
                tk.op(ge, lambda e, xv=xv, nr=nr: e.tensor_tensor(out=xv, in0=xv, in1=gb[0:nr, 0, :], op=ALU.mult),
                      reads=[("XS", i), "gb"], writes=[("XS", i)])
                if bl["sample"]:
                    tk.op(ge, lambda e, xv=xv, nr=nr: e.tensor_tensor(out=xv, in0=xv, in1=gb[0:nr, 1, :], op=ALU.add),
                          reads=[("XS", i), "gb"], writes=[("XS", i)])
                    tk.op("act", lambda e, xv=xv, nr=nr, i=i: e.activation(out=vn[0:nr, vb + i, :], in_=xv, func=AF.Identity),
                          reads=[("XS", i)], writes=[("vn", vb + i)])
                    io_dma(gv_s, xv, reads=[("XS", i)], writes=["gv_s"])
                else:
                    tk.op(ge, lambda e, xv=xv, nr=nr, i=i: e.tensor_tensor(out=vn[0:nr, vb + i, :], in0=xv, in1=gb[0:nr, 1, :], op=ALU.add),
                          reads=[("XS", i), "gb"], writes=[("vn", vb + i)])

        def mixing(half, vb):
            hc0 = half[0]["cs"]
            for fc in range(16):
                step()
                g = fc // 2
                bi, bk, bres = next_bank()

                def fnm(e, bk=bk, fc=fc, g=g, half=half, hc0=hc0):
                    for i, bl in enumerate(half):
                        nr = bl["nr"]
                        lc = bl["cs"] - hc0
                        ins = e.matmul(bk[:, lc:lc + nr], lhsT=vn[0:nr, vb + i, fc * P:(fc + 1) * P], rhs=wmT[0:nr, g, 0:nr],
                                       start=True, stop=True)
                    return ins
                tk.op("pe", fnm, reads=[("vn", vb + i) for i in range(len(half))] + ["wmT"], writes=[bres])
                cb = ct_ctr[0] % 6
                ct_ctr[0] += 1
                tmp = CT[:, cb // 3, cb % 3, :]
                hn = half[-1]["cs"] + half[-1]["nr"] - hc0
                if all(bl["nr"] == P for bl in half):
                    nb_ = len(half)
                    tk.op("dve", lambda e, bk=bk, tmp=tmp, g=g, nb_=nb_, hn=hn: e.tensor_tensor(
                          out=tmp[:, 0:hn].rearrange("p (a t) -> p a t", a=nb_), in0=bk[:, 0:hn].rearrange("p (a t) -> p a t", a=nb_),
                          in1=biasbc[:, g, :].unsqueeze(1).to_broadcast([P, nb_, P]), op=ALU.add), reads=[bres, "biasbc"], writes=[("ct", cb // 3, cb % 3)])
                else:
                    for i, bl in enumerate(half):
                        nr = bl["nr"]
                        lc = bl["cs"] - hc0
                        tk.op("dve", lambda e, bk=bk, tmp=tmp, lc=lc, nr=nr, g=g: e.tensor_tensor(out=tmp[:, lc:lc + nr], in0=bk[:, lc:lc + nr],
                              in1=biasbc[:, g, 0:nr], op=ALU.add), reads=[bres, "biasbc"], writes=[("ct", cb // 3, cb % 3)])
                tk.op("pool" if fc % 2 else "dve", lambda e, tmp=tmp, fc=fc, hn=hn, hc0=hc0: e.tensor_tensor(out=mixin[:, fc, hc0:hc0 + hn], in0=tmp[:, 0:hn],
                      in1=mixin[:, fc, hc0:hc0 + hn], op=ALU.mult), reads=[("ct", cb // 3, cb % 3), ("mix", fc)], writes=[("mix", fc)])

        assert len(halves) <= 2
        for hi, half in enumerate(halves):
            v_proj(half, 2 * hi)
            u_proj(usplit[hi])
        for hi, half in enumerate(halves):
            mixing(half, 2 * hi)

    def ffn(l, segs, N, feed=None):
        def up(q):
            hb = q % 2
            for jj in range(11):
                j = 11 * q + jj
                cb = ct_ctr[0] % 3
                ct_ctr[0] += 1
                tt = []
                pre = std_groups_il(S[("up", l)][j], N) if j == 0 else None
                info = []
                for ag in range(2):
                    uc = j + ag * NH
                    if pre is not None:
                        bk, bres = pre[ag]
                    else:
                        bk, bres = std_group(S[("up", l)][j][ag], 16, lambda k: xn[:, k, 0:N], xn_res(), N)
                    fi = 2 * cb + ag
                    t = CT[:, fi // 3, fi % 3, :]
                    tres = ("ct", fi // 3, fi % 3)
                    tk.op("act", lambda e, bk=bk, t=t, uc=uc: e.activation(out=t[:, 0:N], in_=bk[:, 0:N], func=AF.Identity,
                          scale=fcw[:, l, 2, uc:uc + 1], bias=fcb[:, l, uc:uc + 1]), reads=[bres, "fcw", "fcb"], writes=[tres])
                    info.append((uc, bk, bres, t, tres))
                    tt.append((t, tres))
                for sg in segs:
                    c0, T, st = sg["c0"], sg["T"], sg["stream"]
                    for stepi in range(6):
                        for (uc, bk, bres, t, tres) in info:
                            hh = histF[:, st, l, uc, :]
                            if stepi == 0:
                                tk.op("dve", lambda e, bk=bk, t=t, uc=uc, c0=c0, T=T: e.scalar_tensor_tensor(out=t[:, c0 + 1:c0 + T], in0=bk[:, c0:c0 + T - 1],
                                      scalar=fcw[:, l, 1, uc:uc + 1], in1=t[:, c0 + 1:c0 + T], op0=ALU.mult, op1=ALU.add),
                                      reads=[bres, tres, "fcw"], writes=[tres])
                            elif stepi == 1:
                                tk.op("dve", lambda e, bk=bk, t=t, uc=uc, c0=c0, T=T: e.scalar_tensor_tensor(out=t[:, c0 + 2:c0 + T], in0=bk[:, c0:c0 + T - 2],
                                      scalar=fcw[:, l, 0, uc:uc + 1], in1=t[:, c0 + 2:c0 + T], op0=ALU.mult, op1=ALU.add),
                                      reads=[bres, tres, "fcw"], writes=[tres])
                            elif stepi == 2:
                                tk.op("dve", lambda e, t=t, uc=uc, c0=c0, hh=hh: e.scalar_tensor_tensor(out=t[:, c0:c0 + 1], in0=hh[:, 1:2],
                                      scalar=fcw[:, l, 1, uc:uc + 1], in1=t[:, c0:c0 + 1], op0=ALU.mult, op1=ALU.add),
                                      reads=[("hF", st, l, uc), tres, "fcw", "histF"], writes=[tres])
                            elif stepi == 3:
                                tk.op("dve", lambda e, t=t, uc=uc, c0=c0, hh=hh: e.scalar_tensor_tensor(out=t[:, c0:c0 + 2], in0=hh[:, 0:2],
                                      scalar=fcw[:, l, 0, uc:uc + 1], in1=t[:, c0:c0 + 2], op0=ALU.mult, op1=ALU.add),
                                      reads=[("hF", st, l, uc), tres, "fcw", "histF"], writes=[tres])
                            elif stepi == 4:
                                tk.op("dve", lambda e, bk=bk, c0=c0, T=T, hh=hh: e.tensor_copy(out=hh, in_=bk[:, c0 + T - 2:c0 + T]),
                                      reads=[bres, "histF"], writes=[("hF", st, l, uc)])
                            elif sg.get("flag_after"):
                                tk.op("dve", lambda e, hh=hh: e.tensor_scalar(out=hh, in0=hh, scalar1=flag[:, 0:1], scalar2=None, op0=ALU.mult),
                                      reads=[("hF", st, l, uc), "flag"], writes=[("hF", st, l, uc)])
                (ta, ra), (tg, rg) = tt
                tk.op("act", lambda e, tg=tg: e.activation(out=tg[:, 0:N], in_=tg[:, 0:N], func=AF.Silu), reads=[rg], writes=[rg])
                tk.op("pool" if cur_pass[0] >= 3 else "dve", lambda e, ta=ta, tg=tg, hb=hb, jj=jj: e.tensor_tensor(out=hid[:, hb, jj, 0:N], in0=ta[:, 0:N], in1=tg[:, 0:N], op=ALU.mult),
                      reads=[ra, rg], writes=[("mix", hb * 11 + jj)])

        def down(q):
            hb = q % 2
            for oc in range(16):
                bk, bres = std_group(S[("dn", l)][q][oc], 11, lambda k: hid[:, hb, k, 0:N], [("mix", hb * 11 + k) for k in range(11)], N)
                tk.op("dve", lambda e, bk=bk, oc=oc: e.tensor_tensor(out=xT[:, oc, 0:N], in0=bk[:, 0:N], in1=xT[:, oc, 0:N], op=ALU.add),
                      reads=[bres, ("xT", oc)], writes=[("xT", oc)])
                if q == 3 and feed is not None:
                    feed.feed(oc)
        up(0); up(1); down(0); up(2); down(1); up(3); down(2); down(3)

    def hist_out(hist_ap, nch, dram_rows, res):
        for r in range(2):
            bi, bk, bres = next_bank()
            tk.op("pe", lambda e, bk=bk, r=r: e.transpose(bk[0:nch, 0:P], hist_ap[:, :, r], ident[:]),
                  reads=list(res) + ["ident", "histA", "histF"], writes=[bres])
            stg = XS[0:nch, r, 0:P]
            tk.op("act", lambda e, bk=bk, stg=stg: e.activation(out=stg, in_=bk[0:nch, 0:P], func=AF.Identity),
                  reads=[bres], writes=[("XS", r)])
            io_dma(dram_rows[r].rearrange("(c f) -> c f", f=P), stg, reads=[("XS", r)], writes=[("hout", id(dram_rows), r)])

    def hF_res(st):
        return [("hF", st, l, uc) for l in range(2) for uc in range(NUP)]

    for p in range(n_pass):
        cur_pass[0] = p
        segs, N = segs_of_pass(p)
        if p > 0:
            for sg in segs:
                load_tokens(sg["src"], sg["T"], xT, sg["c0"], "xT")
        nm = Norm(xT, "xT", g_mix[:, 0, :], N)
        for k in range(KD):
            nm.feed(k)
        for l in range(2):
            if p == n_pass - 1:
                prep_sample_kv(l)
            nm.finish()
            q_proj(S["qA"] if l == 0 else S["qB"], N)
            agen = attention_gen(l, segs)

            def step(agen=agen):
                next(agen, None)
            if l == 0:
                mixer_a(segs, N, step)
            else:
                mixer_b(segs, N, step)
            for _ in agen:
                pass
            nm = Norm(xT, "xT", g_ffn[:, l, :], N)
            out_proj(l, N, feed=nm)
            nm.finish()
            nm = Norm(xT, "xT", g_mix[:, 1, :], N) if l == 0 else Norm(xT, "xT", g_fin, N, inplace=True)
            ffn(l, segs, N, feed=nm)
        nm.finish()
        for sg in segs:
            if sg["dst"] is not None:
                store_tokens(xT, sg["c0"], sg["T"], sg["dst"], "xT")
    hist_out(histA[:, 1, :, :], KD, ca_s, [])
    for l in range(2):
        hist_out(histF[:, 1, l, :, :], NUP, fc_s[l], hF_res(1))
    hist_out(histA[:, 0, :, :], KD, ca_p, [])
    for l in range(2):
        hist_out(histF[:, 0, l, :, :], NUP, fc_p[l], hF_res(0))

    tk._need("sp", tk.final_tokens())
    tk.emit()
    return nc


_W_NAMES = ["norm_mix_g", "norm_mem_g", "w_mem_k", "w_mem_v", "w_in_a", "conv_a_w", "w_in_b", "gmlp_norm_g",
            "gmlp_norm_b", "gmlp_ws", "gmlp_bias", "w_out", "norm_ffn_g", "w_up", "ffn_conv_w", "ffn_conv_b",
            "w_down", "norm_final_g"]


def run(inputs, n_cores):
    x_prompt = np.asarray(inputs["x_prompt"], np.float32)
    x_sample = np.asarray(inputs["x_sample"], np.float32)
    B, SEQ, _ = x_prompt.shape
    DB = x_sample.shape[0]
    assert DB == n_cores and n_cores % B == 0
    cpb = n_cores // B
    per = SEQ // cpb
    n_tiles = per // TMAX
    assert n_tiles * TMAX == per
    nc = build_program(n_tiles)
    wts = {k: np.ascontiguousarray(np.asarray(inputs[k], np.float32)) for k in _W_NAMES}
    in_maps = []
    for c in range(n_cores):
        b, s = c // cpb, c % cpb
        xpc = np.zeros((WARM + per, D), np.float32)
        if s > 0:
            xpc[:] = x_prompt[b, s * per - WARM:(s + 1) * per]
        else:
            xpc[WARM:] = x_prompt[b, 0:per]
        m = dict(wts)
        m["xp"] = xpc
        m["xs"] = np.ascontiguousarray(x_sample[c])
        m["memp"] = np.ascontiguousarray(inputs["mem_prompt"][b], dtype=np.float32)
        m["flag"] = np.full((P, 1), 1.0 if s > 0 else 0.0, np.float32)
        m["cca"] = np.ascontiguousarray(inputs["cache_conv_a"][0, c], dtype=np.float32)
        m["cfc"] = np.ascontiguousarray(inputs["cache_ffn_conv"][:, c], dtype=np.float32)
        m["cmk"] = np.ascontiguousarray(np.asarray(inputs["cache_mem_k"])[:, c].reshape(2, NMEM, DXQ), dtype=np.float32)
        m["cmv"] = np.ascontiguousarray(np.asarray(inputs["cache_mem_v"])[:, c].reshape(2, NMEM, DXQ), dtype=np.float32)
        in_maps.append(m)
    res = run_bass_kernel_spmd(nc, in_maps, core_ids=list(range(n_cores)))
    R = res.results
    y_prompt = np.stack([np.concatenate([R[b * cpb + s]["y_p"] for s in range(cpb)], 0) for b in range(B)])
    y_sample = np.stack([R[c]["y_s"] for c in range(n_cores)])
    last = [b * cpb + cpb - 1 for b in range(B)]
    first = [b * cpb for b in range(B)]
    conv_a_prompt = np.stack([R[c]["ca_p"] for c in last])[None]
    ffn_conv_prompt = np.stack([R[c]["fc_p"] for c in last], 1)
    mem_k_prompt = np.stack([R[c]["mk_p"] for c in first], 1).reshape(2, B, NMEM, 4, 256)
    mem_v_prompt = np.stack([R[c]["mv_p"] for c in first], 1).reshape(2, B, NMEM, 4, 256)
    conv_a_sample = np.stack([R[c]["ca_s"] for c in range(n_cores)])[None]
    ffn_conv_sample = np.stack([R[c]["fc_s"] for c in range(n_cores)], 1)
    gmlp_v_sample = np.stack([R[c]["gv_s"] for c in range(n_cores)])[None]
    outs = (y_prompt, y_sample, conv_a_prompt, ffn_conv_prompt, mem_k_prompt, mem_v_prompt,
            conv_a_sample, ffn_conv_sample, gmlp_v_sample)
    return tuple(np.ascontiguousarray(o, dtype=np.float32) for o in outs)


def kernel(**inputs):
    return run(inputs, 8)
```

```python
import numpy as np
import concourse.bass as bass
import concourse.mybir as mybir
from concourse.bass_utils import run_bass_kernel_spmd
from contextlib import ExitStack

F32 = mybir.dt.float32
BF16 = mybir.dt.bfloat16
AF = mybir.ActivationFunctionType
ALU = mybir.AluOpType

P = 128
D = 2048
KD = 16
DXQ = 1024
NMEM = 256
DFF = 5632
NH = 44
NUP = 88
TMAX = 512
WARM = 256
TS = 32
EPS = 1e-6
NB_RING = 5


class Tracker:
    ENGS = ("pe", "act", "dve", "pool", "sp")

    def __init__(self, nc):
        self.nc = nc
        self.q = {e: [] for e in self.ENGS}
        self.cnt = {e: 0 for e in self.ENGS}
        self.waited = {e: {} for e in self.ENGS}
        self.last_w = {}
        self.reads = {}
        self.sems = {}
        self.sem_keys = [("eng", e) for e in self.ENGS]
        self.dma_rings = {}

    def _need(self, eng, toks):
        need = {}
        for t in toks:
            if t is None:
                continue
            k, v = t
            if eng == "pe" and k == ("eng", "pe"):
                continue
            if need.get(k, 0) < v:
                need[k] = v
        w = self.waited[eng]
        for k, v in need.items():
            if w.get(k, 0) < v:
                w[k] = v
                self.q[eng].append(("wait", k, v))

    def _deps(self, reads, writes):
        toks = []
        for r in reads:
            toks.append(self.last_w.get(r))
        for w in writes:
            toks.append(self.last_w.get(w))
            toks.extend(self.reads.get(w, ()))
        return toks

    def _commit(self, tok, reads, writes):
        for r in reads:
            self.reads.setdefault(r, []).append(tok)
        for w in writes:
            self.last_w[w] = tok
            self.reads[w] = []

    def op(self, eng, fn, reads=(), writes=()):
        self._need(eng, self._deps(reads, writes))
        self.cnt[eng] += 1
        tok = (("eng", eng), self.cnt[eng])
        self.q[eng].append(("op", fn, ("eng", eng), 1))
        self._commit(tok, reads, writes)
        return tok

    def dma_ring(self, name, n):
        keys = [("dma", name, i) for i in range(n)]
        self.sem_keys.extend(keys)
        self.dma_rings[name] = {"keys": keys, "n": 0}

    def dma(self, eng, ring, fn, reads=(), writes=()):
        R = self.dma_rings[ring]
        i = R["n"]
        R["n"] += 1
        k = R["keys"][i % len(R["keys"])]
        prev = i // len(R["keys"])
        toks = self._deps(reads, writes)
        if prev > 0:
            toks.append((k, 16 * prev))
        self._need(eng, toks)
        tok = (k, 16 * (prev + 1))
        self.q[eng].append(("op", fn, k, 16))
        self._commit(tok, reads, writes)
        return tok

    def final_tokens(self):
        toks = list(self.last_w.values())
        for l in self.reads.values():
            toks.extend(l)
        return toks

    def emit(self):
        nc = self.nc
        with ExitStack() as es:
            for k in self.sem_keys:
                self.sems[k] = es.enter_context(nc.semaphore("s_" + "_".join(str(x) for x in k)))
            block = es.enter_context(nc.Block())
            engmap = {"pe": block.tensor, "act": block.scalar, "dve": block.vector,
                      "pool": block.gpsimd, "sp": block.sync}
            for e in self.ENGS:
                items = self.q[e]

                def body(engine, items=items):
                    for it in items:
                        if it[0] == "wait":
                            engine.wait_ge(self.sems[it[1]], it[2])
                        else:
                            it[1](engine).then_inc(self.sems[it[2]], it[3])
                engmap[e](body)


def build_program(n_tiles):
    nc = bass.Bass("TRN2", target_bir_lowering=False)
    tk = Tracker(nc)
    tk.dma_ring("wr", NB_RING)
    tk.dma_ring("cast", 8)
    tk.dma_ring("io", 8)
    tk.dma_ring("io2", 4)
    tk.dma_ring("wb", 6)
    NP = n_tiles * TMAX

    def din(name, shape):
        return nc.dram_tensor(name, list(shape), F32, kind="ExternalInput").ap()

    def dout(name, shape):
        return nc.dram_tensor(name, list(shape), F32, kind="ExternalOutput").ap()

    xp = din("xp", [WARM + NP, D]); xs = din("xs", [TS, D]); memp = din("memp", [NMEM, D])
    flag_d = din("flag", [P, 1])
    cca = din("cca", [2, D]); cfc = din("cfc", [2, 2, 2 * DFF])
    cmk = din("cmk", [2, NMEM, DXQ]); cmv = din("cmv", [2, NMEM, DXQ])
    norm_mix_g = din("norm_mix_g", [2, D]); norm_mem_g = din("norm_mem_g", [2, D])
    w_mem_k = din("w_mem_k", [2, D, DXQ]); w_mem_v = din("w_mem_v", [2, D, DXQ])
    w_in_a = din("w_in_a", [1, D, 3 * D + DXQ]); conv_a_w = din("conv_a_w", [1, 3, D])
    w_in_b = din("w_in_b", [1, D, 2 * D + DXQ])
    gmlp_norm_g = din("gmlp_norm_g", [1, D]); gmlp_norm_b = din("gmlp_norm_b", [1, D])
    gmlp_ws = din("gmlp_ws", [1, 8, P, P]); gmlp_bias = din("gmlp_bias", [1, 8, P])
    w_out = din("w_out", [2, D + DXQ, D]); norm_ffn_g = din("norm_ffn_g", [2, D])
    w_up = din("w_up", [2, D, 2 * DFF]); ffn_conv_w = din("ffn_conv_w", [2, 3, 2 * DFF])
    ffn_conv_b = din("ffn_conv_b", [2, 2 * DFF]); w_down = din("w_down", [2, DFF, D])
    norm_final_g = din("norm_final_g", [D])

    y_p = dout("y_p", [NP, D]); y_s = dout("y_s", [TS, D])
    ca_p = dout("ca_p", [2, D]); fc_p = dout("fc_p", [2, 2, 2 * DFF])
    mk_p = dout("mk_p", [2, NMEM, DXQ]); mv_p = dout("mv_p", [2, NMEM, DXQ])
    ca_s = dout("ca_s", [2, D]); fc_s = dout("fc_s", [2, 2, 2 * DFF])
    gv_s = dout("gv_s", [TS, D])

    slot_specs = []

    def add_slot(kind, W, k0, nk, c0):
        slot_specs.append((kind, W, k0, nk, c0))
        return len(slot_specs) - 1

    S = {}
    for l in range(2):
        S[("kT", l)] = [add_slot("std", w_mem_k[l], 0, 16, dc * P) for dc in range(8)]
        for nm, W in (("ktok", w_mem_k[l]), ("vtok", w_mem_v[l])):
            S[(nm, l)] = [[add_slot("wide", W, 4 * s, 4, cg * 512) for s in range(4)] for cg in range(2)]
    S["inA"] = [[add_slot("std", w_in_a[0], 0, 16, D + j * P), add_slot("std", w_in_a[0], 0, 16, 2 * D + j * P),
                 add_slot("std", w_in_a[0], 0, 16, j * P)] for j in range(16)]
    S["qA"] = [add_slot("std", w_in_a[0], 0, 16, 3 * D + qc * P) for qc in range(8)]
    S["uB"] = [add_slot("std", w_in_b[0], 0, 16, j * P) for j in range(16)]
    S["vB"] = [[add_slot("wide", w_in_b[0], 4 * s, 4, D + fg * 512) for s in range(4)] for fg in range(4)]
    S["qB"] = [add_slot("std", w_in_b[0], 0, 16, 2 * D + qc * P) for qc in range(8)]
    for l in range(2):
        S[("o1", l)] = [add_slot("std", w_out[l], 0, 16, oc * P) for oc in range(16)]
        S[("o2", l)] = [add_slot("std", w_out[l], 16, 8, oc * P) for oc in range(16)]
        S[("up", l)] = [[add_slot("std", w_up[l], 0, 16, j * P), add_slot("std", w_up[l], 0, 16, DFF + j * P)]
                        for j in range(NH)]
        S[("dn", l)] = [[add_slot("std", w_down[l], 11 * q, 11, oc * P) for oc in range(16)] for q in range(4)]
    NSLOT = len(slot_specs)
    wsc = nc.dram_tensor("wsc", [NSLOT, P, 2048], BF16).ap()

    def sb(name, shape, dt=F32):
        return nc.alloc_sbuf_tensor("sb_" + name, list(shape), dt)

    xT = sb("xT", [P, KD, TMAX])
    xn = sb("xn", [P, KD, TMAX], BF16)
    R1 = sb("R1", [P, 32 * TMAX], BF16)
    mixin = R1[:].rearrange("p (k t) -> p k t", t=TMAX)
    hid = R1[:, 0:22 * TMAX].rearrange("p (b k t) -> p b k t", b=2, k=11)
    memT = R1[:, 0:16 * 2 * NMEM].bitcast(F32).rearrange("p (k t) -> p k t", t=NMEM)
    KTp = sb("KTp", [P, 2, 8, NMEM], BF16); Vp = sb("Vp", [P, 2, 2, DXQ], BF16)
    KTs = sb("KTs", [P, 8, NMEM], BF16); Vs = sb("Vs", [P, 2, DXQ], BF16)
    Eb = sb("Eb", [P, 2, 2, TMAX], BF16)
    XS = sb("XS", [P, 2, D])
    vn = sb("vn", [P, 4, D], BF16)
    gb = sb("gb", [P, 2, D])
    wring = sb("wring", [P, NB_RING, 2048], BF16)
    CT = sb("CT", [P, 2, 3, 520])
    rstd = sb("rstd", [P, TMAX]); rden = sb("rden", [P, TMAX])
    histA = sb("histA", [P, 2, 16, 2]); histF = sb("histF", [P, 2, 2, NUP, 2])
    ident = sb("ident", [P, P]); ones_f = sb("ones_f", [P, P]); ones_b = sb("ones_b", [P, P], BF16)
    wmT = sb("wmT", [P, 8, P], BF16); biasbc = sb("biasbc", [P, 8, P])
    g_mix = sb("g_mix", [P, 2, KD]); g_ffn = sb("g_ffn", [P, 2, KD]); g_mem = sb("g_mem", [P, 2, KD])
    g_fin = sb("g_fin", [P, KD])
    caw = sb("caw", [P, 3, KD]); fcw = sb("fcw", [P, 2, 3, NUP]); fcb = sb("fcb", [P, 2, NUP])
    flag = sb("flag", [P, 1]); epsb = sb("epsb", [P, 1])
    stt = sb("stt", [P, 2, 4, 6]); mv = sb("mv", [P, 2, 2]); lnr = sb("lnr", [P, 2, 2])
    banks = [nc.alloc_psum_tensor(f"bank{i}", [P, TMAX], F32) for i in range(8)]
    bank_ctr = [0]

    def next_bank():
        b = bank_ctr[0] % 7
        bank_ctr[0] += 1
        return b, banks[b], ("bank", b)

    ring_ctr = [0]
    cast_done = set()
    cur_pass = [-1]
    n_pass = n_tiles + 1
    wb_pass = {}
    for l in range(2):
        for sid in S[("kT", l)] + [x for nm in ("ktok", "vtok") for row in S[(nm, l)] for x in row]:
            wb_pass[sid] = None
        ffn_sids = [x for pair in S[("up", l)] for x in pair] + [x for row in S[("dn", l)] for x in row]
        for i, sid in enumerate(ffn_sids):
            wb_pass[sid] = i % 3

    def load_slot(sid):
        b = ring_ctr[0] % NB_RING
        ring_ctr[0] += 1
        kind, W, k0, nk, c0 = slot_specs[sid]
        cw = 128 if kind == "std" else 512
        ncol = nk * cw
        if sid not in cast_done:
            src = W[k0 * P:(k0 + nk) * P, c0:c0 + cw].rearrange("(k p) c -> p k c", p=P)
            dst = wring[:, b, 0:ncol].rearrange("p (k c) -> p k c", c=cw)
            tk.dma("pool", "cast", lambda e, dst=dst, src=src: e.dma_start(out=dst, in_=src), writes=[("wr", b)])
            wp = wb_pass.get(sid, 0)
            if wp is not None and cur_pass[0] >= wp and cur_pass[0] < n_pass - 1:
                cast_done.add(sid)
                tk.dma("sp", "wb", lambda e, b=b, sid=sid, ncol=ncol: e.dma_start(out=wsc[sid][:, 0:ncol], in_=wring[:, b, 0:ncol]),
                       reads=[("wr", b)], writes=[("wsc", sid)])
        else:
            tk.dma("sp", "wr", lambda e, b=b, sid=sid, ncol=ncol: e.dma_start(out=wring[:, b, 0:ncol], in_=wsc[sid][:, 0:ncol]),
                   reads=[("wsc", sid)], writes=[("wr", b)])
        return b

    def emit_cast(sid):
        kind, W, k0, nk, c0 = slot_specs[sid]
        cw = 128 if kind == "std" else 512
        src = W[k0 * P:(k0 + nk) * P, c0:c0 + cw].rearrange("(k p) c -> p k c", p=P)
        dst = wsc[sid][:, 0:nk * cw].rearrange("p (k c) -> p k c", c=cw)
        tk.dma("pool", "cast", lambda e: e.dma_start(out=dst, in_=src), writes=[("wsc", sid)])

    def io_dma(out_ap, in_ap, reads=(), writes=(), eng="pool"):
        tk.dma(eng, "io" if eng == "pool" else "io2", lambda e: e.dma_start(out=out_ap, in_=in_ap), reads=reads, writes=writes)

    def std_group(sid, nk, rhs_fn, rhs_res, N):
        b = load_slot(sid)
        bi, bk, bres = next_bank()
        w3 = wring[:, b, :].rearrange("p (k c) -> p k c", c=P)

        def fn(e):
            for k in range(nk):
                ins = e.matmul(bk[:, 0:N], lhsT=w3[:, k, :], rhs=rhs_fn(k), start=(k == 0), stop=(k == nk - 1))
            return ins
        tk.op("pe", fn, reads=[("wr", b)] + list(rhs_res), writes=[bres])
        return bk, bres

    def xn_res():
        return [("xn", k) for k in range(KD)]

    def std_groups_il(sids, N):
        bs = [load_slot(sid) for sid in sids]
        bks = [next_bank() for _ in sids]
        for k in range(KD):
            for b, (bi, bk, bres) in zip(bs, bks):
                w3 = wring[:, b, :].rearrange("p (k c) -> p k c", c=P)
                tk.op("pe", lambda e, bk=bk, w3=w3, k=k: e.matmul(bk[:, 0:N], lhsT=w3[:, k, :], rhs=xn[:, k, 0:N],
                      start=(k == 0), stop=(k == KD - 1)), reads=[("wr", b), ("xn", k)], writes=[bres])
        return [(bk, bres) for (bi, bk, bres) in bks]

    tk.op("pool", lambda e: e.memset(ones_f[:], 1.0), writes=["ones_f"])
    tk.op("pool", lambda e: e.affine_select(out=ident[:], in_=ones_f[:], pattern=[[1, P]], compare_op=ALU.is_equal,
                                            fill=0.0, base=0, channel_multiplier=-1), reads=["ones_f"], writes=["ident"])
    tk.op("dve", lambda e: e.tensor_copy(out=ones_b[:], in_=ones_f[:]), reads=["ones_f"], writes=["ones_b"])
    tk.op("dve", lambda e: e.memset(epsb[:], EPS), writes=["epsb"])
    tk.op("dve", lambda e: e.memset(histA[:].rearrange("p a b c -> p (a b c)"), 0.0), writes=["histA"])
    tk.op("dve", lambda e: e.memset(histF[:].rearrange("p a b c d -> p (a b c d)"), 0.0), writes=["histF"])
    io_dma(flag[:], flag_d, writes=["flag"])
    io_dma(gb[:, 0, :], gmlp_norm_g[0].partition_broadcast(P), writes=["gb"])
    io_dma(gb[:, 1, :], gmlp_norm_b[0].partition_broadcast(P), writes=["gb"])
    io_dma(biasbc[:].rearrange("p g t -> p (g t)"), gmlp_bias[0].rearrange("g t -> (g t)").partition_broadcast(P),
           writes=["biasbc"])

    def rows_to_fm(dst_ap, rows_ap, nch, dst_res, stage_col):
        sl_ = stage_col % 6
        sres = ("ct", sl_ // 3, sl_ % 3)
        stg = CT[0:nch, sl_ // 3, sl_ % 3, 0:P]
        io_dma(stg, rows_ap.rearrange("(c f) -> c f", f=P), writes=[sres])
        bi, bk, bres = next_bank()
        tk.op("pe", lambda e: e.transpose(bk[:, 0:nch], stg, ident[0:nch, 0:nch]),
              reads=[sres, "ident"], writes=[bres])
        tk.op("act", lambda e: e.activation(out=dst_ap, in_=bk[:, 0:nch], func=AF.Identity),
              reads=[bres], writes=[dst_res])

    col = [0]

    def vec_load(dst_ap, rows_ap, nch, dst_res):
        rows_to_fm(dst_ap, rows_ap, nch, dst_res, col[0])
        col[0] += 1

    for l in range(2):
        vec_load(g_mix[:, l, :], norm_mix_g[l], KD, "gvec")
        vec_load(g_ffn[:, l, :], norm_ffn_g[l], KD, "gvec")
        vec_load(g_mem[:, l, :], norm_mem_g[l], KD, "gvec")
        vec_load(fcb[:, l, :], ffn_conv_b[l], NUP, "fcb")
        for t in range(3):
            vec_load(fcw[:, l, t, :], ffn_conv_w[l, t], NUP, "fcw")
        for r in range(2):
            vec_load(histF[:, 1, l, :, r], cfc[l, r], NUP, "histF")
    vec_load(g_fin[:, :], norm_final_g, KD, "gvec")
    for t in range(3):
        vec_load(caw[:, t, :], conv_a_w[0, t], KD, "caw")
    for r in range(2):
        vec_load(histA[:, 1, :, r], cca[r], KD, "histA")

    for g in range(8):
        stg = XS[:, 1, (g % 8) * P:(g % 8 + 1) * P]
        io_dma(stg, gmlp_ws[0, g], writes=[("XS", 1)])
        bi, bk, bres = next_bank()
        tk.op("pe", lambda e, bk=bk, stg=stg: e.transpose(bk[:, 0:P], stg, ident[:]),
              reads=[("XS", 1), "ident"], writes=[bres])
        tmp = CT[:, 0, 0, 0:P]
        tk.op("act", lambda e, bk=bk, tmp=tmp: e.activation(out=tmp, in_=bk[:, 0:P], func=AF.Identity),
              reads=[bres], writes=[("ct", 0, 0)])
        tmp2 = CT[:, 0, 1, 0:P]
        tk.op("pool", lambda e, tmp=tmp, tmp2=tmp2: e.affine_select(out=tmp2, in_=tmp, pattern=[[1, P]],
              compare_op=ALU.is_ge, fill=0.0, base=0, channel_multiplier=-1), reads=[("ct", 0, 0)], writes=[("ct", 0, 1)])
        tk.op("dve", lambda e, g=g, tmp2=tmp2: e.tensor_copy(out=wmT[:, g, :], in_=tmp2), reads=[("ct", 0, 1)], writes=["wmT"])

    def load_tokens(src_rows_ap, nrows, dstT, c0, dst_name):
        nblk = (nrows + P - 1) // P
        for blk in range(nblk):
            nr = min(P, nrows - blk * P)
            bi_ = blk % 2
            io_dma(XS[0:nr, bi_, :], src_rows_ap[blk * P:blk * P + nr, :], writes=[("XS", bi_)],
                   eng=("sp" if cur_pass[0] >= 1 else "pool"))
            for g4 in range(4):
                bi, bk, bres = next_bank()

                def fn(e, bk=bk, bi_=bi_, nr=nr, g4=g4):
                    for q in range(4):
                        ins = e.transpose(bk[:, q * P:q * P + nr], XS[0:nr, bi_, (4 * g4 + q) * P:(4 * g4 + q + 1) * P],
                                          ident[0:nr, 0:nr])
                    return ins
                tk.op("pe", fn, reads=[("XS", bi_), "ident"], writes=[bres])
                src = bk[:].rearrange("p (q t) -> p q t", q=4)[:, :, 0:nr]
                dst = dstT[:, 4 * g4:4 * g4 + 4, c0 + blk * P:c0 + blk * P + nr]
                eng = "act" if g4 % 2 == 0 else "dve"
                if eng == "act":
                    tk.op("act", lambda e, src=src, dst=dst: e.activation(out=dst, in_=src, func=AF.Identity),
                          reads=[bres], writes=[(dst_name, 4 * g4 + q) for q in range(4)])
                else:
                    tk.op("dve", lambda e, src=src, dst=dst: e.tensor_copy(out=dst, in_=src),
                          reads=[bres], writes=[(dst_name, 4 * g4 + q) for q in range(4)])

    def store_tokens(srcT, c0, nrows, dst_rows_ap, src_name):
        nblk = (nrows + P - 1) // P
        for blk in range(nblk):
            nr = min(P, nrows - blk * P)
            bi_ = blk % 2
            for g4 in range(4):
                bi, bk, bres = next_bank()

                def fn(e, bk=bk, nr=nr, g4=g4, blk=blk):
                    for q in range(4):
                        ins = e.transpose(bk[0:nr, q * P:(q + 1) * P],
                                          srcT[:, 4 * g4 + q, c0 + blk * P:c0 + blk * P + nr], ident[:])
                    return ins
                tk.op("pe", fn, reads=[(src_name, 4 * g4 + q) for q in range(4)] + ["ident"], writes=[bres])
                dst = XS[0:nr, bi_, g4 * 512:(g4 + 1) * 512]
                if g4 % 2 == 0:
                    tk.op("act", lambda e, bk=bk, dst=dst, nr=nr: e.activation(out=dst, in_=bk[0:nr, :], func=AF.Identity),
                          reads=[bres], writes=[("XS", bi_)])
                else:
                    tk.op("dve", lambda e, bk=bk, dst=dst, nr=nr: e.tensor_copy(out=dst, in_=bk[0:nr, :]),
                          reads=[bres], writes=[("XS", bi_)])
            io_dma(dst_rows_ap[blk * P:blk * P + nr, :], XS[0:nr, bi_, :], reads=[("XS", bi_)], writes=[("out", id(dst_rows_ap), blk)],
                   eng=("sp" if cur_pass[0] >= 1 else "pool"))

    class Norm:
        def __init__(self, srcT, src_name, gvec, N, inplace=False):
            self.srcT, self.src_name, self.gvec, self.N, self.inplace = srcT, src_name, gvec, N, inplace
            self.pending = None
            self.n_mm = 0

        def _mm(self, k):
            N = self.N
            first, last = self.n_mm == 0, self.n_mm == KD - 1
            self.n_mm += 1
            bk = banks[7]
            tk.op("pe", lambda e: e.matmul(bk[:, 0:N], lhsT=ones_b[:], rhs=xn[:, k, 0:N], start=first, stop=last),
                  reads=[("xn", k), "ones_b"], writes=[("bank", 7)])

        def feed(self, k):
            N, srcT = self.N, self.srcT
            tk.op("act", lambda e: e.activation(out=xn[:, k, 0:N], in_=srcT[:, k, 0:N], func=AF.Square),
                  reads=[(self.src_name, k)], writes=[("xn", k)])
            if self.pending is not None:
                self._mm(self.pending)
            self.pending = k

        def finish(self):
            N, srcT, gvec, src_name = self.N, self.srcT, self.gvec, self.src_name
            self._mm(self.pending)
            assert self.n_mm == KD
            bk = banks[7]
            tk.op("act", lambda e: e.activation(out=rstd[:, 0:N], in_=bk[:, 0:N], func=AF.Sqrt, scale=1.0 / D, bias=epsb[:]),
                  reads=[("bank", 7), "epsb"], writes=["rstd"])
            tk.op("dve", lambda e: e.reciprocal(out=rstd[:, 0:N], in_=rstd[:, 0:N]), reads=["rstd"], writes=["rstd"])
            for k in range(KD):
                if self.inplace:
                    tk.op("dve", lambda e, k=k: e.scalar_tensor_tensor(out=srcT[:, k, 0:N], in0=srcT[:, k, 0:N],
                          scalar=gvec[:, k:k + 1], in1=rstd[:, 0:N], op0=ALU.mult, op1=ALU.mult),
                          reads=[(src_name, k), "rstd", "gvec"], writes=[(src_name, k)])
                else:
                    tk.op("dve", lambda e, k=k: e.scalar_tensor_tensor(out=xn[:, k, 0:N], in0=srcT[:, k, 0:N],
                          scalar=gvec[:, k:k + 1], in1=rstd[:, 0:N], op0=ALU.mult, op1=ALU.mult),
                          reads=[(src_name, k), "rstd", "gvec"], writes=[("xn", k)])

    def rmsnorm(srcT, src_name, gvec, N, out_fp32_inplace=False):
        nm = Norm(srcT, src_name, gvec, N, out_fp32_inplace)
        for k in range(KD):
            nm.feed(k)
        nm.finish()

    load_tokens(memp, NMEM, memT, 0, "memT")

    HALF = TMAX // 2

    def segs_of_pass(p):
        if p == 0:
            return [dict(c0=0, T=WARM, stream=0, src=xp[0:WARM, :], dst=None, flag_after=True),
                    dict(c0=WARM, T=HALF, stream=0, src=xp[WARM:WARM + HALF, :], dst=y_p[0:HALF, :])], WARM + HALF
        if p == n_pass - 1:
            return [dict(c0=0, T=HALF, stream=0, src=xp[WARM + NP - HALF:WARM + NP, :], dst=y_p[NP - HALF:NP, :]),
                    dict(c0=HALF, T=TS, stream=1, src=xs, dst=y_s)], HALF + TS
        y0 = HALF + (p - 1) * TMAX
        return [dict(c0=0, T=TMAX, stream=0, src=xp[WARM + y0:WARM + y0 + TMAX, :], dst=y_p[y0:y0 + TMAX, :])], TMAX

    segs0, N0 = segs_of_pass(0)
    for sg in segs0:
        load_tokens(sg["src"], sg["T"], xT, sg["c0"], "xT")


    for l in range(2):
        rmsnorm(memT, "memT", g_mem[:, l, :], NMEM)
        for wi, (nm, dstd) in enumerate((("ktok", mk_p), ("vtok", mv_p))):
            for cg in range(2):
                bks = [next_bank() for _ in range(2)]
                for s in range(4):
                    b = load_slot(S[(nm, l)][cg][s])
                    w3 = wring[:, b, :].rearrange("p (k c) -> p k c", c=512)

                    def fn(e, s=s, w3=w3, bks=bks):
                        for mb in range(2):
                            for kk in range(4):
                                ins = e.matmul(bks[mb][1][:, 0:512], lhsT=xn[:, 4 * s + kk, mb * P:(mb + 1) * P],
                                               rhs=w3[:, kk, :], start=(s == 0 and kk == 0), stop=(s == 3 and kk == 3))
                        return ins
                    tk.op("pe", fn, reads=[("wr", b)] + xn_res(), writes=[bks[0][2], bks[1][2]])
                for mb in range(2):
                    dst = XS[:, mb, wi * DXQ + cg * 512:wi * DXQ + (cg + 1) * 512]
                    tk.op("act", lambda e, dst=dst, bk=bks[mb][1]: e.activation(out=dst, in_=bk[:, 0:512], func=AF.Identity),
                          reads=[bks[mb][2]], writes=[("XS", mb)])
        for d2 in range(4):
            bi, bk, bres = next_bank()

            def fnkt(e, bk=bk, d2=d2):
                for dd in range(2):
                    for mb in range(2):
                        ins = e.transpose(bk[:, dd * NMEM + mb * P:dd * NMEM + (mb + 1) * P],
                                          XS[:, mb, (2 * d2 + dd) * P:(2 * d2 + dd + 1) * P], ident[:])
                return ins
            tk.op("pe", fnkt, reads=[("XS", 0), ("XS", 1), "ident"], writes=[bres])
            tk.op("act", lambda e, bk=bk, d2=d2, l=l: e.activation(out=KTp[:, l, 2 * d2:2 * d2 + 2, :],
                  in_=bk[:].rearrange("p (a m) -> p a m", a=2), func=AF.Identity), reads=[bres], writes=[("KTp", l)])
        for mb in range(2):
            tk.op("dve", lambda e, l=l, mb=mb: e.tensor_copy(out=Vp[:, l, mb, :], in_=XS[:, mb, DXQ:2 * DXQ]),
                  reads=[("XS", mb)], writes=[("Vp", l)])
            io_dma(mk_p[l, mb * P:(mb + 1) * P, :], XS[:, mb, 0:DXQ], reads=[("XS", mb)], writes=[("mk_p", l, mb)], eng="sp")
            io_dma(mv_p[l, mb * P:(mb + 1) * P, :], XS[:, mb, DXQ:2 * DXQ], reads=[("XS", mb)], writes=[("mv_p", l, mb)], eng="sp")

    def prep_sample_kv(l):
        for mb in range(2):
            io_dma(XS[:, mb, 0:DXQ], cmk[l, mb * P:(mb + 1) * P, :], writes=[("XS", mb)], eng="sp")
            io_dma(XS[:, mb, DXQ:2 * DXQ], cmv[l, mb * P:(mb + 1) * P, :], writes=[("XS", mb)], eng="sp")
            tk.op("dve", lambda e, mb=mb: e.tensor_copy(out=Vs[:, mb, :], in_=XS[:, mb, DXQ:2 * DXQ]),
                  reads=[("XS", mb)], writes=["Vs"])
        for d2 in range(4):
            bi, bk, bres = next_bank()

            def fn(e, bk=bk, d2=d2):
                for dd in range(2):
                    for mb in range(2):
                        ins = e.transpose(bk[:, dd * NMEM + mb * P:dd * NMEM + (mb + 1) * P],
                                          XS[:, mb, (2 * d2 + dd) * P:(2 * d2 + dd + 1) * P], ident[:])
                return ins
            tk.op("pe", fn, reads=[("XS", 0), ("XS", 1), "ident"], writes=[bres])
            tk.op("act", lambda e, bk=bk, d2=d2: e.activation(out=KTs[:, 2 * d2:2 * d2 + 2, :],
                  in_=bk[:].rearrange("p (a m) -> p a m", a=2), func=AF.Identity), reads=[bres], writes=["KTs"])

    def mix_res(ks):
        return [("mix", k) for k in ks]

    def attention_gen(l, segs):
        for sg in segs:
            c0, T, st = sg["c0"], sg["T"], sg["stream"]
            for h in range(4):
                eb = h % 2
                for mc in range(2):
                    bi, bk, bres = next_bank()

                    def fn(e, bk=bk, mc=mc, h=h, st=st, c0=c0, T=T):
                        for dc in range(2):
                            kt = KTp[:, l, 2 * h + dc, mc * P:(mc + 1) * P] if st == 0 else KTs[:, 2 * h + dc, mc * P:(mc + 1) * P]
                            ins = e.matmul(bk[:, 0:T], lhsT=kt, rhs=mixin[:, 24 + 2 * h + dc, c0:c0 + T],
                                           start=(dc == 0), stop=(dc == 1))
                        return ins
                    tk.op("pe", fn, reads=mix_res([24 + 2 * h, 25 + 2 * h]) + [("KTp", l), "KTs"], writes=[bres])
                    tk.op("act", lambda e, bk=bk, mc=mc, eb=eb, T=T: e.activation(out=Eb[:, eb, mc, 0:T], in_=bk[:, 0:T], func=AF.Exp),
                          reads=[bres], writes=[("Eb", eb, mc)])
                yield
                bi, bkd, bresd = next_bank()

                def fnd(e, bkd=bkd, eb=eb, T=T):
                    for mc in range(2):
                        ins = e.matmul(bkd[:, 0:T], lhsT=ones_b[:], rhs=Eb[:, eb, mc, 0:T], start=(mc == 0), stop=(mc == 1))
                    return ins
                tk.op("pe", fnd, reads=[("Eb", eb, 0), ("Eb", eb, 1), "ones_b"], writes=[bresd])
                tk.op("dve", lambda e, bkd=bkd, T=T: e.reciprocal(out=rden[:, 0:T], in_=bkd[:, 0:T]), reads=[bresd], writes=["rden"])
                for dc in range(2):
                    bi, bk, bres = next_bank()

                    def fnv(e, bk=bk, dc=dc, h=h, eb=eb, st=st, T=T):
                        for mc in range(2):
                            vv = Vp[:, l, mc, h * 256 + dc * P:h * 256 + (dc + 1) * P] if st == 0 else Vs[:, mc, h * 256 + dc * P:h * 256 + (dc + 1) * P]
                            ins = e.matmul(bk[:, 0:T], lhsT=vv, rhs=Eb[:, eb, mc, 0:T], start=(mc == 0), stop=(mc == 1))
                        return ins
                    tk.op("pe", fnv, reads=[("Eb", eb, 0), ("Eb", eb, 1), ("Vp", l), "Vs"], writes=[bres])
                    tk.op("dve", lambda e, bk=bk, dc=dc, h=h, c0=c0, T=T: e.tensor_tensor(out=mixin[:, 16 + 2 * h + dc, c0:c0 + T], in0=bk[:, 0:T],
                          in1=rden[:, 0:T], op=ALU.mult), reads=[bres, "rden"], writes=[("mix", 16 + 2 * h + dc)])
                yield

    def out_proj(l, N, feed=None):
        for oc in range(16):
            b1 = load_slot(S[("o1", l)][oc])
            b2 = load_slot(S[("o2", l)][oc])
            bi, bk, bres = next_bank()
            wa = wring[:, b1, :].rearrange("p (k c) -> p k c", c=P)
            wb_ = wring[:, b2, :].rearrange("p (k c) -> p k c", c=P)

            def fn(e, bk=bk, wa=wa, wb_=wb_):
                for k in range(16):
                    e.matmul(bk[:, 0:N], lhsT=wa[:, k, :], rhs=mixin[:, k, 0:N], start=(k == 0), stop=False)
                for k in range(8):
                    ins = e.matmul(bk[:, 0:N], lhsT=wb_[:, k, :], rhs=mixin[:, 16 + k, 0:N], start=False, stop=(k == 7))
                return ins
            tk.op("pe", fn, reads=[("wr", b1), ("wr", b2)] + mix_res(range(24)), writes=[bres])
            tk.op("dve", lambda e, bk=bk, oc=oc: e.tensor_tensor(out=xT[:, oc, 0:N], in0=bk[:, 0:N], in1=xT[:, oc, 0:N], op=ALU.add),
                  reads=[bres, ("xT", oc)], writes=[("xT", oc)])
            if feed is not None:
                feed.feed(oc)

    def q_proj(sl, N):
        first = std_groups_il(sl[0:3], N)
        for qc in range(8):
            if qc < 3:
                bk, bres = first[qc]
            else:
                bk, bres = std_group(sl[qc], 16, lambda k: xn[:, k, 0:N], xn_res(), N)
            tk.op("act", lambda e, bk=bk, qc=qc: e.activation(out=mixin[:, 24 + qc, 0:N], in_=bk[:, 0:N], func=AF.Identity, scale=0.0625),
                  reads=[bres], writes=[("mix", 24 + qc)])

    ct_ctr = [0]

    def mixer_a(segs, N, step):
        for j in range(16):
            step()
            sc, sh, sbg = S["inA"][j]
            bkc, brc = std_group(sc, 16, lambda k: xn[:, k, 0:N], xn_res(), N)
            bkh, brh = std_group(sh, 16, lambda k: xn[:, k, 0:N], xn_res(), N)
            bkb, brb = std_group(sbg, 16, lambda k: xn[:, k, 0:N], xn_res(), N)
            cb = ct_ctr[0] % 2
            ct_ctr[0] += 1
            hsb, ch, tcv = CT[:, cb, 0, :], CT[:, cb, 1, :], CT[:, cb, 2, :]
            tk.op("act", lambda e, bkh=bkh, hsb=hsb: e.activation(out=hsb[:, 0:N], in_=bkh[:, 0:N], func=AF.Identity),
                  reads=[brh], writes=[("ct", cb, 0)])
            for si, sg in enumerate(segs):
                c0, T, st = sg["c0"], sg["T"], sg["stream"]
                off = c0 + 2 * si
                tk.op("dve", lambda e, ch=ch, off=off, st=st, j=j: e.tensor_copy(out=ch[:, off:off + 2], in_=histA[:, st, j, :]),
                      reads=["histA"], writes=[("ct", cb, 1)])
                tk.op("dve", lambda e, ch=ch, off=off, bkc=bkc, hsb=hsb, c0=c0, T=T: e.tensor_tensor(out=ch[:, off + 2:off + 2 + T],
                      in0=bkc[:, c0:c0 + T], in1=hsb[:, c0:c0 + T], op=ALU.mult), reads=[brc, ("ct", cb, 0)], writes=[("ct", cb, 1)])
                tk.op("dve", lambda e, ch=ch, off=off, st=st, j=j, T=T: e.tensor_copy(out=histA[:, st, j, :], in_=ch[:, off + T:off + T + 2]),
                      reads=[("ct", cb, 1)], writes=["histA"])
                if sg.get("flag_after"):
                    tk.op("dve", lambda e, st=st, j=j: e.tensor_scalar(out=histA[:, st, j, :], in0=histA[:, st, j, :], scalar1=flag[:, 0:1],
                          scalar2=None, op0=ALU.mult), reads=["histA", "flag"], writes=["histA"])
                tk.op("dve", lambda e, ch=ch, off=off, tcv=tcv, c0=c0, T=T, j=j: e.tensor_scalar(out=tcv[:, c0:c0 + T], in0=ch[:, off + 2:off + 2 + T],
                      scalar1=caw[:, 2, j:j + 1], scalar2=None, op0=ALU.mult), reads=[("ct", cb, 1), "caw"], writes=[("ct", cb, 2)])
                for tap in (1, 0):
                    tk.op("dve", lambda e, ch=ch, off=off, tcv=tcv, c0=c0, T=T, j=j, tap=tap: e.scalar_tensor_tensor(out=tcv[:, c0:c0 + T],
                          in0=ch[:, off + tap:off + tap + T], scalar=caw[:, tap, j:j + 1], in1=tcv[:, c0:c0 + T], op0=ALU.mult, op1=ALU.add),
                          reads=[("ct", cb, 1), ("ct", cb, 2), "caw"], writes=[("ct", cb, 2)])
            tk.op("dve", lambda e, bkb=bkb, tcv=tcv, j=j: e.tensor_tensor(out=mixin[:, j, 0:N], in0=bkb[:, 0:N], in1=tcv[:, 0:N], op=ALU.mult),
                  reads=[brb, ("ct", cb, 2)], writes=[("mix", j)])

    def mixer_b(segs, N, step):
        def u_proj(js):
            for j in js:
                step()
                bk, bres = std_group(S["uB"][j], 16, lambda k: xn[:, k, 0:N], xn_res(), N)
                tk.op("act", lambda e, bk=bk, j=j: e.activation(out=mixin[:, j, 0:N], in_=bk[:, 0:N], func=AF.Gelu),
                      reads=[bres], writes=[("mix", j)])

        blocks = []
        for sg in segs:
            nblk = (sg["T"] + P - 1) // P
            for blk in range(nblk):
                nr = min(P, sg["T"] - blk * P)
                blocks.append(dict(cs=sg["c0"] + blk * P, nr=nr, sample=(sg["stream"] == 1)))
        halves = [blocks[h0:h0 + 2] for h0 in range(0, len(blocks), 2)]
        usplit = [range(0, 8), range(8, 16)] if len(halves) == 2 else [range(0, 16)]

        def v_proj(half, vb):
            for fg in range(4):
                bks = [next_bank() for _ in half]
                for s_ in range(4):
                    b = load_slot(S["vB"][fg][s_])
                    w3 = wring[:, b, :].rearrange("p (k c) -> p k c", c=512)

                    def fn(e, s_=s_, w3=w3, bks=bks, half=half):
                        for i, bl in enumerate(half):
                            for kk in range(4):
                                ins = e.matmul(bks[i][1][0:bl["nr"], 0:512], lhsT=xn[:, 4 * s_ + kk, bl["cs"]:bl["cs"] + bl["nr"]],
                                               rhs=w3[:, kk, :], start=(s_ == 0 and kk == 0), stop=(s_ == 3 and kk == 3))
                        return ins
                    tk.op("pe", fn, reads=[("wr", b)] + xn_res(), writes=[x[2] for x in bks])
                for i, bl in enumerate(half):
                    dst = XS[0:bl["nr"], i, fg * 512:(fg + 1) * 512]
                    tk.op("act", lambda e, dst=dst, bk=bks[i][1], nr=bl["nr"]: e.activation(out=dst, in_=bk[0:nr, 0:512], func=AF.Gelu),
                          reads=[bks[i][2]], writes=[("XS", i)])
            for i, bl in enumerate(half):
                nr = bl["nr"]
                xv = XS[0:nr, i, :]
                st_ = stt[0:nr, i, :, :]
                for c4 in range(4):
                    tk.op("dve", lambda e, c4=c4, nr=nr, i=i, st_=st_: e.bn_stats(out=st_[:, c4, :], in_=XS[0:nr, i, c4 * 512:(c4 + 1) * 512]),
                          reads=[("XS", i)], writes=[("stt", i)])
                tk.op("dve", lambda e, nr=nr, i=i, st_=st_: e.bn_aggr(out=mv[0:nr, i, :], in_=st_.rearrange("p c s -> p (c s)")),
                      reads=[("stt", i)], writes=[("mv", i)])
                tk.op("act", lambda e, nr=nr, i=i: e.activation(out=lnr[0:nr, i, 0:1], in_=mv[0:nr, i, 1:2], func=AF.Sqrt, scale=1.0, bias=epsb[0:nr, :]),
                      reads=[("mv", i), "epsb"], writes=[("lnr", i)])
                tk.op("dve", lambda e, nr=nr, i=i: e.reciprocal(out=lnr[0:nr, i, 0:1], in_=lnr[0:nr, i, 0:1]), reads=[("lnr", i)], writes=[("lnr", i)])
                tk.op("dve", lambda e, nr=nr, i=i: e.tensor_scalar(out=lnr[0:nr, i, 1:2], in0=mv[0:nr, i, 0:1], scalar1=lnr[0:nr, i, 0:1], scalar2=-1.0,
                      op0=ALU.mult, op1=ALU.mult), reads=[("mv", i), ("lnr", i)], writes=[("lnr", i)])
                tk.op("act", lambda e, xv=xv, nr=nr, i=i: e.activation(out=xv, in_=xv, func=AF.Identity, scale=lnr[0:nr, i, 0:1], bias=lnr[0:nr, i, 1:2]),
                      reads=[("XS", i), ("lnr", i)], writes=[("XS", i)])
                ge = "dve" if i == 0 else "pool"
                tk.op(ge, lambda e, xv=xv, nr=nr: e.tensor_tensor(out=xv, in0=xv, in1=gb[0:nr, 0, :], op=ALU.mult),
                      reads=[("XS", i), "gb"], writes=[("XS", i)])
                if bl["sample"]:
                    tk.op(ge, lambda e, xv=xv, nr=nr: e.tensor_tensor(out=xv, in0=xv, in1=gb[0:nr, 1, :], op=ALU.add),
                          reads=[("XS", i), "gb"], writes=[("XS", i)])
                    tk.op("act", lambda e, xv=xv, nr=nr, i=i: e.activation(out=vn[0:nr, vb + i, :], in_=xv, func=AF.Identity),
                          reads=[("XS", i)], writes=[("vn", vb + i)])
                    io_dma(gv_s, xv, reads=[("XS", i)], writes=["gv_s"])
                else:
                    tk.op(ge, lambda e, xv=xv, nr=nr, i=i: e.tensor_tensor(out=vn[0:nr, vb + i, :], in0=xv, in1=gb[0:nr, 1, :], op=ALU.add),
                          reads=[("XS", i), "gb"], writes=[("vn", vb + i)])

        def mixing(half, vb):
            hc0 = half[0]["cs"]
            for fc in range(16):
                step()
                g = fc // 2
                bi, bk, bres = next_bank()

                def fnm(e, bk=bk, fc=fc, g=g, half=half, hc0=hc0):
                    for i, bl in enumerate(half):
                        nr = bl["nr"]
                        lc = bl["cs"] - hc0
                        ins = e.matmul(bk[:, lc:lc + nr], lhsT=vn[0:nr, vb + i, fc * P:(fc + 1) * P], rhs=wmT[0:nr, g, 0:nr],
                                       start=True, stop=True)
                    return ins
                tk.op("pe", fnm, reads=[("vn", vb + i) for i in range(len(half))] + ["wmT"], writes=[bres])
                cb = ct_ctr[0] % 6
                ct_ctr[0] += 1
                tmp = CT[:, cb // 3, cb % 3, :]
                hn = half[-1]["cs"] + half[-1]["nr"] - hc0
                if all(bl["nr"] == P for bl in half):
                    nb_ = len(half)
                    tk.op("dve", lambda e, bk=bk, tmp=tmp, g=g, nb_=nb_, hn=hn: e.tensor_tensor(
                          out=tmp[:, 0:hn].rearrange("p (a t) -> p a t", a=nb_), in0=bk[:, 0:hn].rearrange("p (a t) -> p a t", a=nb_),
                          in1=biasbc[:, g, :].unsqueeze(1).to_broadcast([P, nb_, P]), op=ALU.add), reads=[bres, "biasbc"], writes=[("ct", cb // 3, cb % 3)])
                else:
                    for i, bl in enumerate(half):
                        nr = bl["nr"]
                        lc = bl["cs"] - hc0
                        tk.op("dve", lambda e, bk=bk, tmp=tmp, lc=lc, nr=nr, g=g: e.tensor_tensor(out=tmp[:, lc:lc + nr], in0=bk[:, lc:lc + nr],
                              in1=biasbc[:, g, 0:nr], op=ALU.add), reads=[bres, "biasbc"], writes=[("ct", cb // 3, cb % 3)])
                tk.op("pool" if fc % 2 else "dve", lambda e, tmp=tmp, fc=fc, hn=hn, hc0=hc0: e.tensor_tensor(out=mixin[:, fc, hc0:hc0 + hn], in0=tmp[:, 0:hn],
                      in1=mixin[:, fc, hc0:hc0 + hn], op=ALU.mult), reads=[("ct", cb // 3, cb % 3), ("mix", fc)], writes=[("mix", fc)])

        assert len(halves) <= 2
        for hi, half in enumerate(halves):
            v_proj(half, 2 * hi)
            u_proj(usplit[hi])
        for hi, half in enumerate(halves):
            mixing(half, 2 * hi)

    def ffn(l, segs, N, feed=None):
        def up(q):
            hb = q % 2
            for jj in range(11):
                j = 11 * q + jj
                cb = ct_ctr[0] % 3
                ct_ctr[0] += 1
                tt = []
                pre = std_groups_il(S[("up", l)][j], N) if j == 0 else None
                info = []
                for ag in range(2):
                    uc = j + ag * NH
                    if pre is not None:
                        bk, bres = pre[ag]
                    else:
                        bk, bres = std_group(S[("up", l)][j][ag], 16, lambda k: xn[:, k, 0:N], xn_res(), N)
                    fi = 2 * cb + ag
                    t = CT[:, fi // 3, fi % 3, :]
                    tres = ("ct", fi // 3, fi % 3)
                    tk.op("act", lambda e, bk=bk, t=t, uc=uc: e.activation(out=t[:, 0:N], in_=bk[:, 0:N], func=AF.Identity,
                          scale=fcw[:, l, 2, uc:uc + 1], bias=fcb[:, l, uc:uc + 1]), reads=[bres, "fcw", "fcb"], writes=[tres])
                    info.append((uc, bk, bres, t, tres))
                    tt.append((t, tres))
                for sg in segs:
                    c0, T, st = sg["c0"], sg["T"], sg["stream"]
                    for stepi in range(6):
                        for (uc, bk, bres, t, tres) in info:
                            hh = histF[:, st, l, uc, :]
                            if stepi == 0:
                                tk.op("dve", lambda e, bk=bk, t=t, uc=uc, c0=c0, T=T: e.scalar_tensor_tensor(out=t[:, c0 + 1:c0 + T], in0=bk[:, c0:c0 + T - 1],
                                      scalar=fcw[:, l, 1, uc:uc + 1], in1=t[:, c0 + 1:c0 + T], op0=ALU.mult, op1=ALU.add),
                                      reads=[bres, tres, "fcw"], writes=[tres])
                            elif stepi == 1:
                                tk.op("dve", lambda e, bk=bk, t=t, uc=uc, c0=c0, T=T: e.scalar_tensor_tensor(out=t[:, c0 + 2:c0 + T], in0=bk[:, c0:c0 + T - 2],
                                      scalar=fcw[:, l, 0, uc:uc + 1], in1=t[:, c0 + 2:c0 + T], op0=ALU.mult, op1=ALU.add),
                                      reads=[bres, tres, "fcw"], writes=[tres])
                            elif stepi == 2:
                                tk.op("dve", lambda e, t=t, uc=uc, c0=c0, hh=hh: e.scalar_tensor_tensor(out=t[:, c0:c0 + 1], in0=hh[:, 1:2],
                                      scalar=fcw[:, l, 1, uc:uc + 1], in1=t[:, c0:c0 + 1], op0=ALU.mult, op1=ALU.add),
                                      reads=[("hF", st, l, uc), tres, "fcw", "histF"], writes=[tres])
                            elif stepi == 3:
                                tk.op("dve", lambda e, t=t, uc=uc, c0=c0, hh=hh: e.scalar_tensor_tensor(out=t[:, c0:c0 + 2], in0=hh[:, 0:2],
                                      scalar=fcw[:, l, 0, uc:uc + 1], in1=t[:, c0:c0 + 2], op0=ALU.mult, op1=ALU.add),
                                      reads=[("hF", st, l, uc), tres, "fcw", "histF"], writes=[tres])
                            elif stepi == 4:
                                tk.op("dve", lambda e, bk=bk, c0=c0, T=T, hh=hh: e.tensor_copy(out=hh, in_=bk[:, c0 + T - 2:c0 + T]),
                                      reads=[bres, "histF"], writes=[("hF", st, l, uc)])
                            elif sg.get("flag_after"):
                                tk.op("dve", lambda e, hh=hh: e.tensor_scalar(out=hh, in0=hh, scalar1=flag[:, 0:1], scalar2=None, op0=ALU.mult),
                                      reads=[("hF", st, l, uc), "flag"], writes=[("hF", st, l, uc)])
                (ta, ra), (tg, rg) = tt
                tk.op("act", lambda e, tg=tg: e.activation(out=tg[:, 0:N], in_=tg[:, 0:N], func=AF.Silu), reads=[rg], writes=[rg])
                tk.op("pool" if cur_pass[0] >= 3 else "dve", lambda e, ta=ta, tg=tg, hb=hb, jj=jj: e.tensor_tensor(out=hid[:, hb, jj, 0:N], in0=ta[:, 0:N], in1=tg[:, 0:N], op=ALU.mult),
                      reads=[ra, rg], writes=[("mix", hb * 11 + jj)])

        def down(q):
            hb = q % 2
            for oc in range(16):
                bk, bres = std_group(S[("dn", l)][q][oc], 11, lambda k: hid[:, hb, k, 0:N], [("mix", hb * 11 + k) for k in range(11)], N)
                tk.op("dve", lambda e, bk=bk, oc=oc: e.tensor_tensor(out=xT[:, oc, 0:N], in0=bk[:, 0:N], in1=xT[:, oc, 0:N], op=ALU.add),
                      reads=[bres, ("xT", oc)], writes=[("xT", oc)])
                if q == 3 and feed is not None:
                    feed.feed(oc)
        up(0); up(1); down(0); up(2); down(1); up(3); down(2); down(3)

    def hist_out(hist_ap, nch, dram_rows, res):
        for r in range(2):
            bi, bk, bres = next_bank()
            tk.op("pe", lambda e, bk=bk, r=r: e.transpose(bk[0:nch, 0:P], hist_ap[:, :, r], ident[:]),
                  reads=list(res) + ["ident", "histA", "histF"], writes=[bres])
            stg = XS[0:nch, r, 0:P]
            tk.op("act", lambda e, bk=bk, stg=stg: e.activation(out=stg, in_=bk[0:nch, 0:P], func=AF.Identity),
                  reads=[bres], writes=[("XS", r)])
            io_dma(dram_rows[r].rearrange("(c f) -> c f", f=P), stg, reads=[("XS", r)], writes=[("hout", id(dram_rows), r)])

    def hF_res(st):
        return [("hF", st, l, uc) for l in range(2) for uc in range(NUP)]

    for p in range(n_pass):
        cur_pass[0] = p
        segs, N = segs_of_pass(p)
        if p > 0:
            for sg in segs:
                load_tokens(sg["src"], sg["T"], xT, sg["c0"], "xT")
        nm = Norm(xT, "xT", g_mix[:, 0, :], N)
        for k in range(KD):
            nm.feed(k)
        for l in range(2):
            if p == n_pass - 1:
                prep_sample_kv(l)
            nm.finish()
            q_proj(S["qA"] if l == 0 else S["qB"], N)
            agen = attention_gen(l, segs)

            def step(agen=agen):
                next(agen, None)
            if l == 0:
                mixer_a(segs, N, step)
            else:
                mixer_b(segs, N, step)
            for _ in agen:
                pass
            nm = Norm(xT, "xT", g_ffn[:, l, :], N)
            out_proj(l, N, feed=nm)
            nm.finish()
            nm = Norm(xT, "xT", g_mix[:, 1, :], N) if l == 0 else Norm(xT, "xT", g_fin, N, inplace=True)
            ffn(l, segs, N, feed=nm)
        nm.finish()
        for sg in segs:
            if sg["dst"] is not None:
                store_tokens(xT, sg["c0"], sg["T"], sg["dst"], "xT")
    hist_out(histA[:, 1, :, :], KD, ca_s, [])
    for l in range(2):
        hist_out(histF[:, 1, l, :, :], NUP, fc_s[l], hF_res(1))
    hist_out(histA[:, 0, :, :], KD, ca_p, [])
    for l in range(2):
        hist_out(histF[:, 0, l, :, :], NUP, fc_p[l], hF_res(0))

    tk._need("sp", tk.final_tokens())
    tk.emit()
    return nc


_W_NAMES = ["norm_mix_g", "norm_mem_g", "w_mem_k", "w_mem_v", "w_in_a", "conv_a_w", "w_in_b", "gmlp_norm_g",
            "gmlp_norm_b", "gmlp_ws", "gmlp_bias", "w_out", "norm_ffn_g", "w_up", "ffn_conv_w", "ffn_conv_b",
            "w_down", "norm_final_g"]


def run(inputs, n_cores):
    x_prompt = np.asarray(inputs["x_prompt"], np.float32)
    x_sample = np.asarray(inputs["x_sample"], np.float32)
    B, SEQ, _ = x_prompt.shape
    DB = x_sample.shape[0]
    assert DB == n_cores and n_cores % B == 0
    cpb = n_cores // B
    per = SEQ // cpb
    n_tiles = per // TMAX
    assert n_tiles * TMAX == per
    nc = build_program(n_tiles)
    wts = {k: np.ascontiguousarray(np.asarray(inputs[k], np.float32)) for k in _W_NAMES}
    in_maps = []
    for c in range(n_cores):
        b, s = c // cpb, c % cpb
        xpc = np.zeros((WARM + per, D), np.float32)
        if s > 0:
            xpc[:] = x_prompt[b, s * per - WARM:(s + 1) * per]
        else:
            xpc[WARM:] = x_prompt[b, 0:per]
        m = dict(wts)
        m["xp"] = xpc
        m["xs"] = np.ascontiguousarray(x_sample[c])
        m["memp"] = np.ascontiguousarray(inputs["mem_prompt"][b], dtype=np.float32)
        m["flag"] = np.full((P, 1), 1.0 if s > 0 else 0.0, np.float32)
        m["cca"] = np.ascontiguousarray(inputs["cache_conv_a"][0, c], dtype=np.float32)
        m["cfc"] = np.ascontiguousarray(inputs["cache_ffn_conv"][:, c], dtype=np.float32)
        m["cmk"] = np.ascontiguousarray(np.asarray(inputs["cache_mem_k"])[:, c].reshape(2, NMEM, DXQ), dtype=np.float32)
        m["cmv"] = np.ascontiguousarray(np.asarray(inputs["cache_mem_v"])[:, c].reshape(2, NMEM, DXQ), dtype=np.float32)
        in_maps.append(m)
    res = run_bass_kernel_spmd(nc, in_maps, core_ids=list(range(n_cores)))
    R = res.results
    y_prompt = np.stack([np.concatenate([R[b * cpb + s]["y_p"] for s in range(cpb)], 0) for b in range(B)])
    y_sample = np.stack([R[c]["y_s"] for c in range(n_cores)])
    last = [b * cpb + cpb - 1 for b in range(B)]
    first = [b * cpb for b in range(B)]
    conv_a_prompt = np.stack([R[c]["ca_p"] for c in last])[None]
    ffn_conv_prompt = np.stack([R[c]["fc_p"] for c in last], 1)
    mem_k_prompt = np.stack([R[c]["mk_p"] for c in first], 1).reshape(2, B, NMEM, 4, 256)
    mem_v_prompt = np.stack([R[c]["mv_p"] for c in first], 1).reshape(2, B, NMEM, 4, 256)
    conv_a_sample = np.stack([R[c]["ca_s"] for c in range(n_cores)])[None]
    ffn_conv_sample = np.stack([R[c]["fc_s"] for c in range(n_cores)], 1)
    gmlp_v_sample = np.stack([R[c]["gv_s"] for c in range(n_cores)])[None]
    outs = (y_prompt, y_sample, conv_a_prompt, ffn_conv_prompt, mem_k_prompt, mem_v_prompt,
            conv_a_sample, ffn_conv_sample, gmlp_v_sample)
    return tuple(np.ascontiguousarray(o, dtype=np.float32) for o in outs)


def kernel(**inputs):
    return run(inputs, 8)
```
